# Optimizing a Trainium2 kernel written in Bass

```python
import jax, jax.numpy as jnp
from jax import lax
import numpy as np

D_MODEL = 2048
BATCH = 2
SEQ = 8192
DEPTH = 4
DEC_BATCH = 32
DEC_SEQ = 64
PAST_LEN = 1024

CHUNK = 64
HEAD_DIM = D_MODEL // 16
POOL_WINDOWS = (2, 4, 8, 16)
N_POOL_GROUPS = len(POOL_WINDOWS)
POOL_WIDTH = N_POOL_GROUPS * HEAD_DIM
N_CONV_HEADS = (3 * D_MODEL // 8) // HEAD_DIM
CONV_CH = N_CONV_HEADS * HEAD_DIM
CONV_K = 3
N_SGU_HEADS = (3 * D_MODEL // 8) // HEAD_DIM
SGU_WIDTH = N_SGU_HEADS * HEAD_DIM
SGU_LEN = 128
MIX_WIDTH = POOL_WIDTH + CONV_CH + SGU_WIDTH
IN_WIDTH = POOL_WIDTH + 3 * CONV_CH + 2 * SGU_WIDTH
D_FF = ((8 * D_MODEL // 3 + 255) // 256) * 256
POOL_STATE = max(POOL_WINDOWS) - 1
CONV_STATE = CONV_K - 1
N_NORMS = 6
EPS = 1e-6

kernel_name = "hybrid_pool_conv_sgu_streaming_step"


def rmsnorm(x, g):
    xf = x.astype(jnp.float32)
    y = xf * lax.rsqrt(jnp.mean(xf * xf, axis=-1, keepdims=True) + EPS)
    return (y * g.astype(jnp.float32)).astype(x.dtype)


def rms_plain(x):
    xf = x.astype(jnp.float32)
    return (xf * lax.rsqrt(jnp.mean(xf * xf, axis=-1, keepdims=True) + EPS)).astype(x.dtype)


def swiglu(x, wg, wu, wd):
    return (jax.nn.silu(x @ wg) * (x @ wu)) @ wd


def pool_mixer(p, hist, pos0, w_grp, scale):
    B, T, _ = p.shape
    ext = p if hist is None else jnp.concatenate([hist.astype(p.dtype), p], axis=1)
    H = ext.shape[1] - T
    cs = jnp.pad(jnp.cumsum(ext.astype(jnp.float32), axis=1), ((0, 0), (1, 0), (0, 0)))
    pf = p.astype(jnp.float32)
    t = np.arange(T)
    hi = t + H + 1
    outs = []
    for g, w in enumerate(POOL_WINDOWS):
        sl = slice(g * HEAD_DIM, (g + 1) * HEAD_DIM)
        lo = np.maximum(hi - w, 0)
        cnt = np.minimum(pos0 + t + 1, w).astype(np.float32)
        win_sum = cs[:, hi, sl] - cs[:, lo, sl]
        outs.append(win_sum / cnt[None, :, None] - pf[..., sl])
    d = jnp.stack(outs, axis=2)
    y = jnp.einsum('btgc,gcd->btgd', d, w_grp.astype(jnp.float32)).reshape(B, T, POOL_WIDTH)
    y = y * scale.astype(jnp.float32)
    return y.astype(p.dtype), ext[:, -POOL_STATE:]


def conv_mixer(h, bg, cg, hist, w_conv):
    T = h.shape[1]
    z = cg * h
    ext = jnp.concatenate([hist.astype(z.dtype), z], axis=1)
    y = ext[:, 0:T] * w_conv[0] + ext[:, 1:T + 1] * w_conv[1] + ext[:, 2:T + 2] * w_conv[2]
    return bg * y, ext[:, -CONV_STATE:]


def sgu_mixer(u, v, w_s, b_s):
    B, T, _ = u.shape
    n = -(-T // SGU_LEN)
    Tp = n * SGU_LEN
    vp = jnp.pad(v, ((0, 0), (0, Tp - T), (0, 0))).reshape(B, n, SGU_LEN, N_SGU_HEADS, HEAD_DIM)
    mask = jnp.tril(jnp.ones((SGU_LEN, SGU_LEN), dtype=bool))
    ws = jnp.where(mask[None], w_s, jnp.zeros_like(w_s))
    g = jnp.einsum('hst,bcthd->bcshd', ws, vp) + b_s.T[None, None, :, :, None]
    g = g.reshape(B, Tp, SGU_WIDTH)[:, :T]
    return u * g


def trunk_layer(x, pool_hist, conv_hist, pos0, w_in, w_out, pool_w, pool_scale, conv_w,
                sgu_w, sgu_b, f1g, f1u, f1d, f2g, f2u, f2d, ng, mix_g):
    x = x + 0.5 * rmsnorm(swiglu(rmsnorm(x, ng[0]), f1g, f1u, f1d), ng[1])
    proj = rmsnorm(x, ng[2]) @ w_in
    splits = np.cumsum([POOL_WIDTH, CONV_CH, CONV_CH, CONV_CH, SGU_WIDTH])
    p, h, bg, cg, u, v = jnp.split(proj, splits, axis=-1)
    ya, pool_new = pool_mixer(p, pool_hist, pos0, pool_w, pool_scale)
    yb, conv_new = conv_mixer(h, bg, cg, conv_hist, conv_w)
    yc = sgu_mixer(u, v, sgu_w, sgu_b)
    ymix = jnp.concatenate([rms_plain(ya), rms_plain(yb), rms_plain(yc)], axis=-1) * mix_g
    x = x + rmsnorm(ymix @ w_out, ng[3])
    x = x + 0.5 * rmsnorm(swiglu(rmsnorm(x, ng[4]), f2g, f2u, f2d), ng[5])
    return x, pool_new, conv_new, v


def setup_inputs(seed: int = 0) -> dict:
    key = jax.random.key(seed)
    ks = jax.random.split(key, 20)
    nrm = lambda k, shape, s: jax.random.normal(k, shape, jnp.float32) * s
    return {
        "x_prompt": nrm(ks[0], (BATCH, SEQ, D_MODEL), 1.0),
        "x_sample": nrm(ks[1], (DEC_BATCH, DEC_SEQ, D_MODEL), 1.0),
        "state_pool": nrm(ks[2], (DEPTH, DEC_BATCH, POOL_STATE, POOL_WIDTH), 1.0),
        "state_conv": nrm(ks[3], (DEPTH, DEC_BATCH, CONV_STATE, CONV_CH), 1.0),
        "w_in": nrm(ks[4], (DEPTH, D_MODEL, IN_WIDTH), D_MODEL ** -0.5),
        "w_out": nrm(ks[5], (DEPTH, MIX_WIDTH, D_MODEL), MIX_WIDTH ** -0.5),
        "pool_w": nrm(ks[6], (DEPTH, N_POOL_GROUPS, HEAD_DIM, HEAD_DIM), HEAD_DIM ** -0.5),
        "pool_scale": 1.0 + nrm(ks[7], (DEPTH, POOL_WIDTH), 0.1),
        "conv_w": nrm(ks[8], (DEPTH, CONV_K, CONV_CH), CONV_K ** -0.5),
        "sgu_w": nrm(ks[9], (DEPTH, N_SGU_HEADS, SGU_LEN, SGU_LEN), SGU_LEN ** -0.5),
        "sgu_b": 1.0 + nrm(ks[10], (DEPTH, N_SGU_HEADS, SGU_LEN), 0.1),
        "ffn1_gate": nrm(ks[11], (DEPTH, D_MODEL, D_FF), D_MODEL ** -0.5),
        "ffn1_up": nrm(ks[12], (DEPTH, D_MODEL, D_FF), D_MODEL ** -0.5),
        "ffn1_down": nrm(ks[13], (DEPTH, D_FF, D_MODEL), D_FF ** -0.5),
        "ffn2_gate": nrm(ks[14], (DEPTH, D_MODEL, D_FF), D_MODEL ** -0.5),
        "ffn2_up": nrm(ks[15], (DEPTH, D_MODEL, D_FF), D_MODEL ** -0.5),
        "ffn2_down": nrm(ks[16], (DEPTH, D_FF, D_MODEL), D_FF ** -0.5),
        "norm_gains": 1.0 + nrm(ks[17], (DEPTH, N_NORMS, D_MODEL), 0.1),
        "mix_gain": 1.0 + nrm(ks[18], (DEPTH, MIX_WIDTH), 0.1),
    }


def reference(x_prompt, x_sample, state_pool, state_conv, w_in, w_out, pool_w, pool_scale,
              conv_w, sgu_w, sgu_b, ffn1_gate, ffn1_up, ffn1_down, ffn2_gate, ffn2_up,
              ffn2_down, norm_gains, mix_gain):
    yp, ys = x_prompt, x_sample
    pool_p, conv_p, pool_s, conv_s, v_s = [], [], [], [], []
    conv_zero = jnp.zeros((BATCH, CONV_STATE, CONV_CH), x_prompt.dtype)
    for l in range(DEPTH):
        w = (w_in[l], w_out[l], pool_w[l], pool_scale[l], conv_w[l], sgu_w[l], sgu_b[l],
             ffn1_gate[l], ffn1_up[l], ffn1_down[l], ffn2_gate[l], ffn2_up[l], ffn2_down[l],
             norm_gains[l], mix_gain[l])
        yp, pp, cp, _ = trunk_layer(yp, None, conv_zero, 0, *w)
        ys, ps, cs, vs = trunk_layer(ys, state_pool[l], state_conv[l], PAST_LEN, *w)
        pool_p.append(pp); conv_p.append(cp)
        pool_s.append(ps); conv_s.append(cs); v_s.append(vs)
    new_pool_prompt = jnp.stack(pool_p, axis=0)
    new_conv_prompt = jnp.stack(conv_p, axis=0)
    new_pool_sample = jnp.stack(pool_s, axis=0)
    new_conv_sample = jnp.stack(conv_s, axis=0)
    new_sgu_v_sample = jnp.stack(v_s, axis=0)
    return (yp, ys, new_pool_prompt, new_conv_prompt, new_pool_sample, new_conv_sample, new_sgu_v_sample)
```

```python
import numpy as np
import concourse.bass as bass
import concourse.mybir as mybir
from concourse.bass_utils import run_bass_kernel_spmd

F32 = mybir.dt.float32
BF16 = mybir.dt.bfloat16
AF = mybir.ActivationFunctionType
ALU = mybir.AluOpType

EPS = 1e-6
POOLW = 512
CONVC = 768
SGUW = 768
MIXW = 2048
INW = 4352
MIXC = MIXW // 128
TB = 512
SEM_LIMIT = 30000


class Cfg:
    def __init__(self, D=2048, DFF=5632, L=4, blocks=None, halo=256, nsamp=4, tsamp=64):
        self.D, self.DFF, self.L = D, DFF, L
        self.KC = D // 128
        self.JC = DFF // 128
        self.JH = self.JC // 2
        self.JQ = self.JH // 2
        assert self.JQ * 4 == self.JC
        self.blocks = blocks if blocks is not None else [512, 512, 512, 512, 256]
        self.halo = halo
        self.nsamp, self.tsamp = nsamp, tsamp
        self.NTOK = TB * len(self.blocks)
        self.NEXT = sum(self.blocks)
        self.NS = nsamp * tsamp
        assert self.NTOK == self.NEXT + self.NS


class Region:
    __slots__ = ("w", "r")

    def __init__(self):
        self.w = None
        self.r = []


class EngCtx:
    def __init__(self, prog, name, is_pe=False):
        self.prog = prog
        self.name = name
        self.thunks = []
        self.sem = None
        self.count = 0
        self.waited = {}
        self.own = set()
        self.is_pe = is_pe
        self.nsem = 0

    def wait(self, tok):
        sem, val = tok
        if self.is_pe and sem in self.own:
            return
        if self.waited.get(sem, 0) >= val:
            return
        self.waited[sem] = val
        self.thunks.append(lambda e, sem=sem, val=val: e.wait_ge(sem, val))

    def emit(self, fn):
        if self.sem is None or self.count >= SEM_LIMIT:
            self.sem = self.prog.nc.alloc_semaphore(f"p_{self.name}_{self.nsem}")
            self.nsem += 1
            self.own.add(self.sem)
            self.count = 0
        self.count += 1
        sem, val = self.sem, self.count
        self.thunks.append(lambda e, fn=fn, sem=sem: fn(e).then_inc(sem, 1))
        return (sem, val)

    def emit_dma(self, fn, sem_state):
        sem_state[1] += 16
        sem = sem_state[0]
        self.thunks.append(lambda e, fn=fn, sem=sem: fn(e).then_inc(sem, 16))
        return (sem, sem_state[1])


class Prog:
    def __init__(self, cfg):
        self.cfg = cfg
        self.nc = bass.Bass("TRN2", target_bir_lowering=False)
        self.pe = EngCtx(self, "pe", is_pe=True)
        self.act = EngCtx(self, "act")
        self.dve = EngCtx(self, "dve")
        self.pool = EngCtx(self, "pool")
        self.sp = EngCtx(self, "sp")
        self.nbank = 0

    def op(self, eng, fn, reads=(), writes=()):
        deps = []
        for R in reads:
            if R.w is not None:
                deps.append(R.w)
        for W in writes:
            if W.w is not None:
                deps.append(W.w)
            deps.extend(W.r)
        for t in deps:
            eng.wait(t)
        tok = eng.emit(fn)
        for R in reads:
            R.r.append(tok)
        for W in writes:
            W.w = tok
            W.r = []
        return tok

    def dma(self, eng, fn, sem_state, reads=(), writes=()):
        deps = []
        for R in reads:
            if R.w is not None:
                deps.append(R.w)
        for W in writes:
            if W.w is not None:
                deps.append(W.w)
            deps.extend(W.r)
        for t in deps:
            eng.wait(t)
        tok = eng.emit_dma(fn, sem_state)
        for R in reads:
            R.r.append(tok)
        for W in writes:
            W.w = tok
            W.r = []
        return tok

    def new_dma_sem(self, name):
        return [self.nc.alloc_semaphore(name), 0]


def build_program(cfg):
    P = Prog(cfg)
    nc = P.nc
    D, DFF, L, KC, JC, JH, JQ = cfg.D, cfg.DFF, cfg.L, cfg.KC, cfg.JC, cfg.JH, cfg.JQ
    NB = len(cfg.blocks)
    NTOK = cfg.NTOK
    NSAMP, TS = cfg.nsamp, cfg.tsamp
    NS = cfg.NS

    def din(name, shape, dt=F32):
        return nc.dram_tensor(name, list(shape), dt, kind="ExternalInput").ap()

    def dout(name, shape, dt=F32):
        return nc.dram_tensor(name, list(shape), dt, kind="ExternalOutput").ap()

    xT_d = din("xT", [D, NTOK])
    w_ffn = {}
    for f in (1, 2):
        w_ffn[f] = (din(f"w{f}g", [L, D, DFF]), din(f"w{f}u", [L, D, DFF]), din(f"w{f}d", [L, DFF, D]))
    win_d = din("w_in", [L, D, INW])
    wout_d = din("w_out", [L, MIXW, D])
    gains_d = din("gains", [128, L * 6 * KC])
    poolw_d = din("poolw", [L, 4, 128, 128])
    pscale_d = din("pscale", [128, L * 4])
    convw_d = din("convw", [128, L * 3 * 6])
    mixg_d = din("mixg", [128, L * MIXC])
    sguw_d = din("sguwT", [128, L * 6 * 128])
    sguwbd_d = din("sguwbd", [128, L * 6 * 128])
    sgub_d = din("sgub", [128, L * 6 * 128])
    mask_d = din("mask", [128, 128])
    maskbd_d = din("maskbd", [128, 128])
    invc_d = din("invc", [128, 4 * 16])
    phist_d = din("phist", [128, L * NSAMP * 4 * 15])
    chist_d = din("chist", [128, L * NSAMP * 6 * 2])

    yT_d = dout("yT", [D, NTOK])
    ptp_d = dout("pool_tail_p", [128, L * 4 * 15])
    ctp_d = dout("conv_tail_p", [128, L * 6 * 2])
    pts_d = dout("pool_tail_s", [128, L * NSAMP * 4 * 15])
    cts_d = dout("conv_tail_s", [128, L * NSAMP * 6 * 2])
    vs_d = dout("v_s", [L, NS, SGUW])

    def tiles_ffn(f, l):
        g, u, d = w_ffn[f]
        out = []
        for hf in range(2):
            gu = []
            for jg in range(JH // 2):
                c0 = (hf * JH + jg * 2) * 128
                gu.append((g[l, :, c0:c0 + 256].rearrange("(k p) c -> p k c", p=128), KC, 256))
                gu.append((u[l, :, c0:c0 + 256].rearrange("(k p) c -> p k c", p=128), KC, 256))
            dn = []
            for mg in range(D // 256):
                for q in range(2):
                    r0 = (hf * JH + q * JQ) * 128
                    dn.append((d[l, r0:r0 + JQ * 128, mg * 256:(mg + 1) * 256].rearrange("(j p) c -> p j c", p=128), JQ, 256))
            out.append((gu, dn))
        return out

    def tiles_in(l):
        return [(win_d[l, :, ct * 256:(ct + 1) * 256].rearrange("(k p) c -> p k c", p=128), KC, 256) for ct in range(INW // 256)]

    def tiles_out(l):
        return [(wout_d[l, :, ct * 256:(ct + 1) * 256].rearrange("(k p) c -> p k c", p=128), MIXC, 256) for ct in range(D // 256)]

    SLOT_ELEMS = max(KC, MIXC, JQ) * 256
    n_tiles_layer = 2 * (2 * (JH + (D // 256) * 2)) + INW // 256 + D // 256
    n_tiles = n_tiles_layer * L
    scratch_l = [nc.dram_tensor(f"wscratch{l}", [n_tiles_layer, 128, SLOT_ELEMS], BF16, kind="Internal").ap() for l in range(L)]
    scratch_regions = [Region() for _ in range(n_tiles)]

    def sb(name, shape, dt):
        return nc.alloc_sbuf_tensor(name, list(shape), dt)

    xT = sb("xT_sb", [128, KC, TB], F32)
    xn = sb("xn_sb", [128, KC, TB], BF16)
    NA = max(JH, 20)
    aT = sb("aT_sb", [128, NA, TB], BF16)
    ymix = aT
    dT = aT[:, 16:20, :]
    NF = max(KC, 16)
    fT = sb("fT_sb", [128, NF, TB], F32)
    pT = fT[:, 0:4, :]
    cau = fT[:, 4:10, :]
    fflat = fT[:].rearrange("p a b -> p (a b)")
    pext1 = fflat[:, 10 * TB:10 * TB + 15 + TB]
    ZW = 2 + TB + 2 * NSAMP + 8
    zbv = [fflat[:, 12 * TB:12 * TB + ZW], fflat[:, 14 * TB:14 * TB + ZW]]
    ybuf = sb("ybuf_sb", [128, 6, TB], F32)
    sg = sb("sg_sb", [128, 3, TB + 16], F32)
    sq = sb("sq_sb", [128, 4, TB], BF16)
    rstd = sb("rstd_sb", [128, 2, TB], F32)
    v_bf = sb("vbf_sb", [128, 4, SGUW], BF16)
    v_f32 = sb("vf32_sb", [128, 2, SGUW], F32)
    slots = [sb(f"slot{i}", [128, SLOT_ELEMS], BF16) for i in range(4)]
    gains = sb("gains_sb", [128, L * 6 * KC], F32)
    poolw = sb("poolw_sb", [128, L * 4, 128], BF16)
    pscale = sb("pscale_sb", [128, L * 4], F32)
    convw = sb("convw_sb", [128, L * 3 * 6], F32)
    mixg = sb("mixg_sb", [128, L * MIXC], F32)
    wsT = sb("wsT_sb", [128, L * 6, 128], BF16)
    wsTbd = sb("wsTbd_sb", [128, L * 6, 128], BF16)
    sgub = sb("sgub_sb", [128, 6, 128], F32)
    maskt = sb("mask_sb", [128, 2, 128], F32)
    invc = sb("invc_sb", [128, 4, 16], F32)
    ones = sb("ones_sb", [128, 128], BF16)
    epsb = sb("epsb_sb", [128, 4], F32)
    phist = sb("phist_sb", [128, L * NSAMP, 4, 15], F32)
    chist = sb("chist_sb", [128, L * NSAMP, 6, 2], F32)
    ptail = sb("ptail_sb", [128, L, 4, 15], F32)
    ctail = sb("ctail_sb", [128, L, 6, 2], F32)
    pts = sb("pts_sb", [128, L * NSAMP, 4, 15], F32)
    cts = sb("cts_sb", [128, L * NSAMP, 6, 2], F32)

    psb = [nc.alloc_psum_tensor(f"ps{i}", [128, TB], F32) for i in range(8)]
    bank_r = [Region() for _ in range(8)]

    def next_bank():
        i = P.nbank % 7
        P.nbank += 1
        return psb[i], bank_r[i]

    ss_bank, ss_r = psb[7], bank_r[7]

    R_x = [Region() for _ in range(KC)]
    R_xn = [Region() for _ in range(KC)]
    R_a = [Region() for _ in range(NA)]
    R_ym = R_a
    R_dT = R_a[16:20]
    R_f = [Region() for _ in range(NF)]
    R_pg = R_f[0:4]
    R_cau = R_f[4:10]
    R_pext = R_f[10:12]
    R_zb = [R_f[12:14], R_f[14:16]]
    R_y = [Region() for _ in range(6)]
    R_sq = [Region() for _ in range(4)]
    R_sg = [Region() for _ in range(3)]
    R_rstd = [Region(), Region()]
    R_vbf = [Region() for _ in range(4)]
    R_vf = [Region(), Region()]
    R_slot = [Region() for _ in range(4)]
    R_const = Region()
    R_sgub = Region()
    R_ptail = [Region() for _ in range(L)]
    R_ctail = [Region() for _ in range(L)]
    R_pts = Region()
    R_cts = Region()
    R_tmpc = Region()

    ctr = {"sq": 0, "rstd": 0, "sg": 0, "zb": 0, "slot": 0}

    sem_x = P.new_dma_sem("sem_x")
    sem_y = P.new_dma_sem("sem_y")
    sem_small = P.new_dma_sem("sem_small")
    sem_const = P.new_dma_sem("sem_const")
    sem_slot = [P.new_dma_sem(f"sem_slot{i}") for i in range(4)]
    sem_wb = [P.new_dma_sem(f"sem_wb{i}") for i in range(4)]

    pe, act, dve, pool, sp = P.pe, P.act, P.dve, P.pool, P.sp

    def cload(dst_ap, src_ap, eng=sp):
        P.dma(eng, lambda e, d=dst_ap, s=src_ap: e.dma_start(out=d, in_=s), sem_const, writes=[R_const])

    cload(gains[:], gains_d[:, :])
    cload(pscale[:], pscale_d[:, :])
    cload(convw[:], convw_d[:, :])
    cload(mixg[:], mixg_d[:, :])
    cload(maskt[:, 0, :], mask_d[:, :])
    cload(maskt[:, 1, :], maskbd_d[:, :])
    cload(invc[:].rearrange("p a b -> p (a b)"), invc_d[:, :])
    cload(phist[:].rearrange("p a b c -> p (a b c)"), phist_d[:, :])
    cload(chist[:].rearrange("p a b c -> p (a b c)"), chist_d[:, :])
    cload(poolw[:], poolw_d.rearrange("l g c d -> c (l g) d"), eng=pool)
    assert L * 6 * 128 <= KC * TB or True
    tmpw = fT[:].rearrange("p a b -> p (a b)")
    n_w = L * 6 * 128
    if n_w <= KC * TB:
        for (src, dst, mi) in ((sguw_d, wsT, 0), (sguwbd_d, wsTbd, 1)):
            P.dma(sp, lambda e, s=src: e.dma_start(out=tmpw[:, 0:n_w], in_=s[:, :]), sem_const, writes=[R_tmpc])
            for lh in range(L * 6):
                P.op(dve, lambda e, lh=lh, dst=dst, mi=mi: e.tensor_tensor(
                    out=dst[:, lh, :], in0=tmpw[:, lh * 128:(lh + 1) * 128], in1=maskt[:, mi, :], op=ALU.mult),
                    reads=[R_tmpc, R_const], writes=[R_const])
            R_tmpc.r = [R_const.w]
    else:
        for (src, dst, mi) in ((sguw_d, wsT, 0), (sguwbd_d, wsTbd, 1)):
            for lh in range(L * 6):
                P.dma(sp, lambda e, s=src, lh=lh: e.dma_start(out=tmpw[:, 0:128], in_=s[:, lh * 128:(lh + 1) * 128]),
                      sem_const, writes=[R_tmpc])
                P.op(dve, lambda e, lh=lh, dst=dst, mi=mi: e.tensor_tensor(
                    out=dst[:, lh, :], in0=tmpw[:, 0:128], in1=maskt[:, mi, :], op=ALU.mult),
                    reads=[R_tmpc, R_const], writes=[R_tmpc])
        R_const.w = R_tmpc.w
    P.op(dve, lambda e: e.memset(ones[:], 1.0), writes=[R_const])
    for wi, wd in enumerate((D, POOLW, CONVC)):
        P.op(dve, lambda e, wi=wi, wd=wd: e.memset(epsb[:, wi:wi + 1], float(wd * EPS)), writes=[R_const])
    for l in range(L):
        P.op(dve, lambda e, l=l: e.memset(ptail[:, l], 0.0), writes=[R_ptail[l]])
        P.op(dve, lambda e, l=l: e.memset(ctail[:, l], 0.0), writes=[R_ctail[l]])
    sD = float(np.sqrt(D))
    for l in range(L):
        for s in range(6):
            sc = sD * (0.5 if s in (1, 5) else 1.0)
            o = (l * 6 + s) * KC
            P.op(dve, lambda e, o=o, sc=sc: e.tensor_scalar(
                out=gains[:, o:o + KC], in0=gains[:, o:o + KC], scalar1=sc, scalar2=None, op0=ALU.mult),
                reads=[R_const], writes=[R_const])
        o = l * MIXC
        P.op(dve, lambda e, o=o: e.tensor_scalar(
            out=mixg[:, o:o + 4], in0=mixg[:, o:o + 4], scalar1=float(np.sqrt(POOLW)), scalar2=None, op0=ALU.mult),
            reads=[R_const], writes=[R_const])
        P.op(dve, lambda e, o=o: e.tensor_scalar(
            out=mixg[:, o + 4:o + 16], in0=mixg[:, o + 4:o + 16], scalar1=float(np.sqrt(CONVC)), scalar2=None, op0=ALU.mult),
            reads=[R_const], writes=[R_const])

    def gcol(l, s, c):
        o = (l * 6 + s) * KC + c
        return gains[:, o:o + 1]

    tile_counter = {"i": 0}

    def fetch_tile(bi, tdesc, l):
        view, a, b = tdesc
        ti = tile_counter["i"]
        tile_counter["i"] += 1
        tg = ti
        si = ctr["slot"] % 4
        ctr["slot"] += 1
        slot = slots[si]
        sv = slot[:, 0:a * b].rearrange("p (a b) -> p a b", b=b)
        do_cast = (bi == 0) or (bi == 1 and l >= L // 2)
        do_wb = (bi == 0 and l < L // 2) or (bi == 1 and l >= L // 2)
        if do_cast:
            P.dma(pool, lambda e, sv=sv, view=view: e.dma_start(out=sv, in_=view), sem_slot[si], writes=[R_slot[si]])
            if do_wb:
                P.dma(sp, lambda e, tg=tg, slot=slot, n=a * b: e.dma_start(out=scratch_l[tg // n_tiles_layer][tg % n_tiles_layer, :, 0:n], in_=slot[:, 0:n]),
                      sem_wb[si], reads=[R_slot[si]], writes=[scratch_regions[tg]])
        else:
            P.dma(sp, lambda e, tg=tg, slot=slot, n=a * b: e.dma_start(out=slot[:, 0:n], in_=scratch_l[tg // n_tiles_layer][tg % n_tiles_layer, :, 0:n]),
                  sem_slot[si], reads=[scratch_regions[tg]], writes=[R_slot[si]])
        return sv, R_slot[si]

    pending = []

    def flush_ss(keep=0):
        while len(pending) > keep:
            pending.pop(0)()

    def pe_op(fn, reads=(), writes=()):
        tok = P.op(pe, fn, reads=reads, writes=writes)
        flush_ss()
        return tok

    def ss_add(src, c, reg, first, last, cs=0):
        flush_ss(keep=2)
        i = ctr["sq"] % 4
        ctr["sq"] += 1
        P.op(act, lambda e, i=i: e.activation(out=sq[:, i, cs:TB], in_=src[:, c, cs:TB], func=AF.Square),
             reads=[reg], writes=[R_sq[i]])
        pending.append(lambda i=i, first=first, last=last: P.op(
            pe, lambda e: e.matmul(ss_bank[:, cs:TB], ones[:, :], sq[:, i, cs:TB], start=first, stop=last),
            reads=[R_sq[i], R_const], writes=[ss_r]))

    def ss_finish(width):
        flush_ss()
        ri = ctr["rstd"] % 2
        ctr["rstd"] += 1
        wi = 0 if width == D else {POOLW: 1, CONVC: 2}[width]
        P.op(act, lambda e, ri=ri, wi=wi: e.activation(
            out=rstd[:, ri, :], in_=ss_bank[:, :], func=AF.Ln, bias=epsb[:, wi:wi + 1], scale=1.0),
            reads=[R_const], writes=[R_rstd[ri], ss_r])
        P.op(act, lambda e, ri=ri: e.activation(out=rstd[:, ri, :], in_=rstd[:, ri, :], func=AF.Exp, scale=-0.5),
             reads=[R_rstd[ri]], writes=[R_rstd[ri]])
        return rstd[:, ri, :], R_rstd[ri]

    def ss_x_all():
        for c in range(KC):
            ss_add(xT, c, R_x[c], c == 0, c == KC - 1)

    def prenorm(l, s, cs=0):
        rs, rr = ss_finish(D)
        for c in range(KC):
            P.op(dve, lambda e, c=c, rs=rs: e.scalar_tensor_tensor(
                out=xn[:, c, cs:TB], in0=xT[:, c, cs:TB], scalar=gcol(l, s, c), in1=rs[:, cs:TB], op0=ALU.mult, op1=ALU.mult),
                reads=[R_x[c], rr, R_const], writes=[R_xn[c]])

    def postnorm_add(l, s, want_next, cs=0):
        rs, rr = ss_finish(D)
        prev = None

        def fin(pc, pui):
            P.op(dve, lambda e, c=pc, ui=pui: e.tensor_tensor(out=xT[:, c, cs:TB], in0=xT[:, c, cs:TB], in1=sg[:, ui, cs:TB], op=ALU.add),
                 reads=[R_sg[pui]], writes=[R_x[pc]])
            if want_next:
                ss_add(xT, pc, R_x[pc], pc == 0, pc == KC - 1, cs)
        for c in range(KC):
            ui = ctr["sg"] % 3
            ctr["sg"] += 1
            P.op(dve, lambda e, c=c, rs=rs, ui=ui: e.scalar_tensor_tensor(
                out=sg[:, ui, cs:TB], in0=fT[:, c, cs:TB], scalar=gcol(l, s, c), in1=rs[:, cs:TB], op0=ALU.mult, op1=ALU.mult),
                reads=[R_f[c], rr, R_const], writes=[R_sg[ui]])
            if prev is not None:
                fin(*prev)
            prev = (c, ui)
        fin(*prev)

    def ffn(bi, l, f, s_pre, s_post, want_next, cs=0):
        prenorm(l, s_pre, cs)
        tl = tiles_ffn(f, l)
        for hf in range(2):
            gu, dn = tl[hf]
            for jg in range(JH // 2):
                gv, gr = fetch_tile(bi, gu[2 * jg], l)
                uv, ur = fetch_tile(bi, gu[2 * jg + 1], l)
                for ci in range(2):
                    jl = jg * 2 + ci
                    bg_, bgr = next_bank()
                    bu_, bur = next_bank()

                    def mmg(e, wv=gv, bank=bg_, ci=ci):
                        ins = None
                        for k in range(KC):
                            ins = e.matmul(bank[:, cs:TB], wv[:, k, ci * 128:(ci + 1) * 128], xn[:, k, cs:TB], start=(k == 0), stop=(k == KC - 1))
                        return ins
                    pe_op(mmg, reads=[gr] + R_xn[:KC], writes=[bgr])
                    pe_op(lambda e, wv=uv, bank=bu_, ci=ci: mmg(e, wv, bank, ci), reads=[ur] + R_xn[:KC], writes=[bur])
                    si = ctr["sg"] % 3
                    ctr["sg"] += 1
                    P.op(act, lambda e, si=si, bank=bg_: e.activation(out=sg[:, si, cs:TB], in_=bank[:, cs:TB], func=AF.Silu),
                         reads=[bgr], writes=[R_sg[si]])
                    P.op(dve, lambda e, si=si, bank=bu_, jl=jl: e.tensor_tensor(out=aT[:, jl, cs:TB], in0=sg[:, si, cs:TB], in1=bank[:, cs:TB], op=ALU.mult),
                         reads=[R_sg[si], bur], writes=[R_a[jl]])
            ti = 0
            for mg in range(D // 256):
                b0, b0r = next_bank()
                b1, b1r = next_bank()
                banks = ((b0, b0r), (b1, b1r))
                for q in range(2):
                    dv, dr = fetch_tile(bi, dn[ti], l)
                    ti += 1
                    pe_op(lambda e, dv=dv, q=q, banks_l=banks: _mmd(e, dv, q, banks_l, aT, JQ, cs),
                          reads=[dr] + R_a[q * JQ:(q + 1) * JQ], writes=[b0r, b1r])
                for mi in range(2):
                    m = mg * 2 + mi
                    bank, br = banks[mi]
                    if hf == 0:
                        P.op(act, lambda e, m=m, bank=bank: e.activation(out=fT[:, m, cs:TB], in_=bank[:, cs:TB], func=AF.Copy),
                             reads=[br], writes=[R_f[m]])
                    else:
                        P.op(dve, lambda e, m=m, bank=bank: e.tensor_tensor(out=fT[:, m, cs:TB], in0=bank[:, cs:TB], in1=fT[:, m, cs:TB], op=ALU.add),
                             reads=[br, R_f[m]], writes=[R_f[m]])
                        ss_add(fT, m, R_f[m], m == 0, m == KC - 1, cs)
        postnorm_add(l, s_post, want_next, cs)

    def group_rms(l, y_chunks, c0, width):
        n = len(y_chunks)
        rs, rr = ss_finish(width)
        for c in range(n):
            o = l * MIXC + c0 + c
            P.op(dve, lambda e, c=c, rs=rs, o=o: e.scalar_tensor_tensor(
                out=ymix[:, c0 + c, :], in0=ybuf[:, y_chunks[c], :], scalar=mixg[:, o:o + 1], in1=rs, op0=ALU.mult, op1=ALU.mult),
                reads=[R_y[y_chunks[c]], rr, R_const], writes=[R_ym[c0 + c]])

    def mixer(bi, l):
        np_ = cfg.blocks[bi]
        has_s = np_ < TB
        last_prompt = (sum(cfg.blocks[:bi + 1]) == cfg.NEXT)
        P.dma(sp, lambda e: e.dma_start(out=sgub[:].rearrange("p a b -> p (a b)"), in_=sgub_d[:, l * 768:(l + 1) * 768]),
              sem_const, writes=[R_sgub])
        prenorm(l, 2)
        tin = tiles_in(l)
        order = [0, 1, 8, 9, 10, 2, 3, 4, 5, 6, 7, 14, 15, 16, 11, 12, 13]

        def fm_group(wv, ci):
            bank, br = next_bank()

            def mm(e, wv=wv, bank=bank, ci=ci):
                ins = None
                for k in range(KC):
                    ins = e.matmul(bank[:, :], wv[:, k, ci * 128:(ci + 1) * 128], xn[:, k, :], start=(k == 0), stop=(k == KC - 1))
                return ins
            return bank, br, mm

        def cw(k, c):
            o = (l * 3 + k) * 6 + c
            return convw[:, o:o + 1]

        def sgu_pe_bias():
            for h in range(6):
                bank, br = next_bank()
                for tt in range(4):
                    is_s = has_s and tt * 128 >= np_
                    wmat = wsTbd if is_s else wsT
                    pe_op(lambda e, bank=bank, tt=tt, h=h, wmat=wmat: e.matmul(
                        bank[:, tt * 128:(tt + 1) * 128], v_bf[:, tt, h * 128:(h + 1) * 128], wmat[:, l * 6 + h, :], start=True, stop=True),
                        reads=[R_vbf[tt], R_const], writes=[br])
                for tt in range(4):
                    is_s = has_s and tt * 128 >= np_
                    if not is_s:
                        P.op(dve, lambda e, bank=bank, tt=tt, h=h: e.tensor_tensor(
                            out=ybuf[:, h, tt * 128:(tt + 1) * 128], in0=bank[:, tt * 128:(tt + 1) * 128], in1=sgub[:, h, :], op=ALU.add),
                            reads=[br, R_sgub], writes=[R_y[h]])
                    else:
                        for hh in range(128 // TS):
                            c0 = tt * 128 + hh * TS
                            P.op(dve, lambda e, bank=bank, c0=c0, h=h: e.tensor_tensor(
                                out=ybuf[:, h, c0:c0 + TS], in0=bank[:, c0:c0 + TS], in1=sgub[:, h, 0:TS], op=ALU.add),
                                reads=[br, R_sgub], writes=[R_y[h]])

        for ct in order:
            wv, wr = fetch_tile(bi, tin[ct], l)
            if ct <= 13:
                for ci in range(2):
                    bank, br, mm = fm_group(wv, ci)
                    pe_op(mm, reads=[wr] + R_xn[:KC], writes=[br])
                    if ct <= 1:
                        g = ct * 2 + ci
                        P.op(act, lambda e, g=g, bank=bank: e.activation(out=pT[:, g, :], in_=bank[:, :], func=AF.Copy),
                             reads=[br], writes=[R_pg[g]])
                    elif 8 <= ct <= 10:
                        c = (ct - 8) * 2 + ci
                        P.op(act, lambda e, c=c, bank=bank: e.activation(out=cau[:, c, :], in_=bank[:, :], func=AF.Copy),
                             reads=[br], writes=[R_cau[c]])
                    elif 2 <= ct <= 4:
                        c = (ct - 2) * 2 + ci
                        conv_chunk(l, c, bank, br, np_, has_s, last_prompt, cw)
                    elif 5 <= ct <= 7:
                        c = (ct - 5) * 2 + ci
                        P.op(dve, lambda e, c=c, bank=bank: e.tensor_tensor(out=ybuf[:, c, :], in0=bank[:, :], in1=cau[:, c, :], op=ALU.mult),
                             reads=[br, R_cau[c]], writes=[R_y[c]])
                        ss_add(ybuf, c, R_y[c], c == 0, c == 5)
                    else:
                        c = (ct - 11) * 2 + ci
                        P.op(act, lambda e, c=c, bank=bank: e.activation(out=cau[:, c, :], in_=bank[:, :], func=AF.Copy),
                             reads=[br], writes=[R_cau[c]])
                        P.op(dve, lambda e, h=c: e.tensor_tensor(out=ybuf[:, h, :], in0=ybuf[:, h, :], in1=cau[:, h, :], op=ALU.mult),
                             reads=[R_cau[c]], writes=[R_y[c]])
                        ss_add(ybuf, c, R_y[c], c == 0, c == 5)
                if ct == 1:
                    pool_dve(bi, l, np_, has_s, last_prompt)
                elif ct == 10:
                    pool_pe(l)
                elif ct == 2:
                    group_rms(l, list(range(4)), 0, POOLW)
            else:
                cv = (ct - 14) * 256
                for tt in range(4):
                    bank, br = next_bank()

                    def mmv(e, wv=wv, bank=bank, tt=tt):
                        ins = None
                        for k in range(KC):
                            ins = e.matmul(bank[:, 0:256], xn[:, k, tt * 128:(tt + 1) * 128], wv[:, k, :], start=(k == 0), stop=(k == KC - 1))
                        return ins
                    pe_op(mmv, reads=[wr] + R_xn[:KC], writes=[br])
                    P.op(act, lambda e, bank=bank, tt=tt, cv=cv: e.activation(out=v_bf[:, tt, cv:cv + 256], in_=bank[:, 0:256], func=AF.Copy),
                         writes=[R_vbf[tt], br])
                    if has_s and tt * 128 >= np_:
                        ts_ = (tt * 128 - np_) // 128
                        P.op(dve, lambda e, bank=bank, ts_=ts_, cv=cv: e.tensor_copy(out=v_f32[:, ts_, cv:cv + 256], in_=bank[:, 0:256]),
                             writes=[R_vf[ts_], br])
                if ct == 14:
                    group_rms(l, list(range(6)), 4, CONVC)
                if ct == 16:
                    if has_s:
                        for ts_ in range((TB - np_) // 128):
                            P.dma(pool, lambda e, ts_=ts_: e.dma_start(out=vs_d[l, ts_ * 128:(ts_ + 1) * 128, :], in_=v_f32[:, ts_, :]),
                                  sem_small, reads=[R_vf[ts_]])
                    sgu_pe_bias()
        group_rms(l, list(range(6)), 10, SGUW)
        tout = tiles_out(l)
        for ct in range(D // 256):
            wv, wr = fetch_tile(bi, tout[ct], l)
            for ci in range(2):
                bank, br = next_bank()

                def mmo(e, wv=wv, bank=bank, ci=ci):
                    ins = None
                    for k in range(MIXC):
                        ins = e.matmul(bank[:, :], wv[:, k, ci * 128:(ci + 1) * 128], ymix[:, k, :], start=(k == 0), stop=(k == MIXC - 1))
                    return ins
                pe_op(mmo, reads=[wr] + R_ym[:MIXC], writes=[br])
                m = ct * 2 + ci
                P.op(act, lambda e, m=m, bank=bank: e.activation(out=fT[:, m, :], in_=bank[:, :], func=AF.Copy),
                     reads=[br], writes=[R_f[m]])
                ss_add(fT, m, R_f[m], m == 0, m == KC - 1)
        postnorm_add(l, 3, True)

    def conv_chunk(l, c, bank, br, np_, has_s, last_prompt, cw):
        zi = ctr["zb"] % 2
        ctr["zb"] += 1
        z = zbv[zi]
        rzl = R_zb[zi]
        P.op(dve, lambda e, z=z, c=c: e.tensor_copy(out=z[:, 0:2], in_=ctail[:, l, c, :]), reads=[R_ctail[l]], writes=rzl)
        P.op(dve, lambda e, z=z, c=c, bank=bank: e.tensor_tensor(out=z[:, 2:2 + np_], in0=bank[:, 0:np_], in1=cau[:, c, 0:np_], op=ALU.mult),
             reads=[br, R_cau[c]], writes=rzl)
        if has_s:
            zs = z[:, 2 + np_:2 + np_ + NSAMP * (TS + 2)].rearrange("p (s t) -> p s t", t=TS + 2)
            P.op(dve, lambda e, zs=zs, c=c: e.tensor_copy(out=zs[:, :, 0:2], in_=chist[:, l * NSAMP:(l + 1) * NSAMP, c, :]),
                 reads=[R_const], writes=rzl)
            P.op(dve, lambda e, zs=zs, c=c, bank=bank: e.tensor_tensor(
                out=zs[:, :, 2:2 + TS], in0=bank[:, np_:TB].rearrange("p (s t) -> p s t", t=TS),
                in1=cau[:, c, np_:TB].rearrange("p (s t) -> p s t", t=TS), op=ALU.mult),
                reads=[br, R_cau[c]], writes=rzl)
        acc = cau[:, c, 0:np_]
        P.op(dve, lambda e, z=z, acc=acc, c=c: e.tensor_scalar(out=acc, in0=z[:, 0:np_], scalar1=cw(0, c), scalar2=None, op0=ALU.mult),
             reads=rzl + [R_const], writes=[R_cau[c]])
        for k in (1, 2):
            P.op(dve, lambda e, z=z, acc=acc, c=c, k=k: e.scalar_tensor_tensor(
                out=acc, in0=z[:, k:k + np_], scalar=cw(k, c), in1=acc, op0=ALU.mult, op1=ALU.add),
                reads=rzl + [R_const], writes=[R_cau[c]])
        if has_s:
            accs = cau[:, c, np_:TB].rearrange("p (s t) -> p s t", t=TS)
            P.op(dve, lambda e, zs=zs, accs=accs, c=c: e.tensor_scalar(out=accs, in0=zs[:, :, 0:TS], scalar1=cw(0, c), scalar2=None, op0=ALU.mult),
                 reads=rzl + [R_const], writes=[R_cau[c]])
            for k in (1, 2):
                P.op(dve, lambda e, zs=zs, accs=accs, c=c, k=k: e.scalar_tensor_tensor(
                    out=accs, in0=zs[:, :, k:k + TS], scalar=cw(k, c), in1=accs, op0=ALU.mult, op1=ALU.add),
                    reads=rzl + [R_const], writes=[R_cau[c]])
            P.op(dve, lambda e, zs=zs, c=c: e.tensor_copy(out=cts[:, l * NSAMP:(l + 1) * NSAMP, c, :], in_=zs[:, :, TS:TS + 2]),
                 reads=rzl, writes=[R_cts])
        P.op(dve, lambda e, z=z, c=c: e.tensor_copy(out=ctail[:, l, c, :], in_=z[:, np_:np_ + 2]), reads=rzl, writes=[R_ctail[l]])

    def pool_segment(l, n, col0, hist_ap, hist_reg, tail_ap, tail_reg, fix_col):
        for g in range(4):
            w = 2 << g
            P.op(dve, lambda e, g=g: e.tensor_copy(out=pext1[:, 0:15], in_=hist_ap[:, g, :]), reads=[hist_reg], writes=R_pext)
            P.op(dve, lambda e, g=g: e.tensor_copy(out=pext1[:, 15:15 + n], in_=pT[:, g, col0:col0 + n]), reads=[R_pg[g]], writes=R_pext)
            P.op(dve, lambda e, g=g: e.tensor_copy(out=tail_ap[:, g, :], in_=pext1[:, n:n + 15]), reads=R_pext, writes=[tail_reg])
            cur = pext1
            cur_r = R_pext
            k = 1
            ti = 0
            while k < w:
                nxt = sg[:, ti, :]
                nr = [R_sg[ti]]
                lo = 2 * k - 1
                P.op(dve, lambda e, cur=cur, nxt=nxt, lo=lo, k=k: e.tensor_tensor(
                    out=nxt[:, lo:15 + n], in0=cur[:, lo:15 + n], in1=cur[:, lo - k:15 + n - k], op=ALU.add),
                    reads=cur_r, writes=nr)
                cur, cur_r = nxt, nr
                ti ^= 1
                k *= 2
            P.op(dve, lambda e, cur=cur, g=g, w=w: e.scalar_tensor_tensor(
                out=dT[:, g, col0:col0 + n], in0=cur[:, 15:15 + n], scalar=1.0 / w, in1=pT[:, g, col0:col0 + n], op0=ALU.mult, op1=ALU.subtract),
                reads=cur_r + [R_pg[g]], writes=[R_dT[g]])
            if fix_col is not None:
                fc = fix_col
                oth = sg[:, ti, :]
                orr = [R_sg[ti]]
                P.op(dve, lambda e, cur=cur, g=g, oth=oth, fc=fc: e.tensor_tensor(
                    out=oth[:, 0:16], in0=cur[:, 15 + fc:15 + fc + 16], in1=invc[:, g, :], op=ALU.mult),
                    reads=cur_r + [R_const], writes=orr)
                P.op(dve, lambda e, g=g, oth=oth, fc=fc: e.tensor_tensor(
                    out=dT[:, g, col0 + fc:col0 + fc + 16], in0=oth[:, 0:16], in1=pT[:, g, col0 + fc:col0 + fc + 16], op=ALU.subtract),
                    reads=orr + [R_pg[g]], writes=[R_dT[g]])

    def pool_dve(bi, l, np_, has_s, last_prompt):
        ext0 = sum(cfg.blocks[:bi])
        fix = None
        if ext0 <= cfg.halo < ext0 + np_:
            fix = cfg.halo - ext0
        pool_segment(l, np_, 0, ptail[:, l], R_ptail[l], ptail[:, l], R_ptail[l], fix)
        if has_s:
            for s_ in range(NSAMP):
                pool_segment(l, TS, np_ + s_ * TS, phist[:, l * NSAMP + s_], R_const, pts[:, l * NSAMP + s_], R_pts, None)

    def pool_pe(l):
        bks = [next_bank() for _ in range(4)]
        for g in range(4):
            bank, br = bks[g]
            pe_op(lambda e, bank=bank, g=g: e.matmul(bank[:, :], poolw[:, l * 4 + g, :], dT[:, g, :], start=True, stop=True),
                  reads=[R_dT[g], R_const], writes=[br])
        for g in range(4):
            bank, br = bks[g]
            P.op(dve, lambda e, bank=bank, g=g: e.tensor_scalar(
                out=ybuf[:, g, :], in0=bank[:, :], scalar1=pscale[:, l * 4 + g:l * 4 + g + 1], scalar2=None, op0=ALU.mult),
                reads=[br, R_const], writes=[R_y[g]])
            ss_add(ybuf, g, R_y[g], g == 0, g == 3)

    for bi in range(NB):
        col0 = bi * TB
        tile_counter["i"] = 0
        P.dma(pool, lambda e, col0=col0: e.dma_start(out=xT[:], in_=xT_d[:, col0:col0 + TB].rearrange("(k p) c -> p k c", p=128)),
              sem_x, writes=R_x)
        import os
        dbg = int(os.environ.get("KDBG", "9"))
        for l in range(L):
            if l == 0:
                ss_x_all()
            cs1 = cs2 = 0
            if bi == 0 and cfg.halo == 256:
                r = L - 1 - l
                cs1 = {0: 240, 1: 128, 2: 112}.get(r, 0)
                cs2 = {0: 256, 1: 240, 2: 128, 3: 112}.get(r, 0)
            ffn(bi, l, 1, 0, 1, True, cs1)
            mixer(bi, l)
            ffn(bi, l, 2, 4, 5, l < L - 1, cs2)
        P.dma(pool, lambda e, col0=col0: e.dma_start(out=yT_d[:, col0:col0 + TB].rearrange("(k p) c -> p k c", p=128), in_=xT[:]),
              sem_y, reads=R_x)
    P.dma(pool, lambda e: e.dma_start(out=ptp_d[:, :], in_=ptail[:].rearrange("p a b c -> p (a b c)")), sem_small, reads=R_ptail)
    P.dma(pool, lambda e: e.dma_start(out=ctp_d[:, :], in_=ctail[:].rearrange("p a b c -> p (a b c)")), sem_small, reads=R_ctail)
    P.dma(pool, lambda e: e.dma_start(out=pts_d[:, :], in_=pts[:].rearrange("p a b c -> p (a b c)")), sem_small, reads=[R_pts])
    P.dma(pool, lambda e: e.dma_start(out=cts_d[:, :], in_=cts[:].rearrange("p a b c -> p (a b c)")), sem_small, reads=[R_cts])
    pool.wait((sem_y[0], sem_y[1]))
    pool.wait((sem_small[0], sem_small[1]))
    for i in range(4):
        if sem_wb[i][1] > 0:
            sp.wait((sem_wb[i][0], sem_wb[i][1]))

    with nc.Block() as block:
        @block.tensor
        def _(e):
            for t in pe.thunks:
                t(e)

        @block.scalar
        def _(e):
            for t in act.thunks:
                t(e)

        @block.vector
        def _(e):
            for t in dve.thunks:
                t(e)

        @block.gpsimd
        def _(e):
            for t in pool.thunks:
                t(e)

        @block.sync
        def _(e):
            for t in sp.thunks:
                t(e)
    return nc


def _mmd(e, dv, q, banks, aT, JQ, cs=0):
    ins = None
    for jj in range(JQ):
        for mi in range(2):
            ins = e.matmul(banks[mi][0][:, cs:TB], dv[:, jj, mi * 128:(mi + 1) * 128], aT[:, q * JQ + jj, cs:TB],
                           start=(q == 0 and jj == 0), stop=(q == 1 and jj == JQ - 1))
    return ins


def _pl(a, nchunk):
    sh = a.shape[:-1]
    b = a.reshape(*sh, nchunk, 128)
    b = np.moveaxis(b, -1, 0)
    return np.ascontiguousarray(b)


def make_core_inputs(cfg, xp_ext, xs, state_pool, state_conv, W, is_seq_start):
    L, KC = cfg.L, cfg.KC
    x = np.concatenate([xp_ext, xs.reshape(-1, cfg.D)], axis=0)
    m = dict(W)
    m["xT"] = np.ascontiguousarray(x.T)
    invc = np.zeros((128, 4, 16), np.float32)
    for g in range(4):
        w = 2 << g
        for t in range(16):
            invc[:, g, t] = 1.0 / (min(t + 1, w) if is_seq_start else w)
    m["invc"] = invc.reshape(128, 64)
    ph = state_pool.reshape(L, cfg.nsamp, 15, 4, 128).transpose(4, 0, 1, 3, 2)
    m["phist"] = np.ascontiguousarray(ph).reshape(128, -1)
    ch = state_conv.reshape(L, cfg.nsamp, 2, 6, 128).transpose(4, 0, 1, 3, 2)
    m["chist"] = np.ascontiguousarray(ch).reshape(128, -1)
    return m


def make_shared_weights(cfg, w_in, w_out, pool_w, pool_scale, conv_w, sgu_w, sgu_b, f1g, f1u, f1d, f2g, f2u, f2d,
                        norm_gains, mix_gain):
    L, KC = cfg.L, cfg.KC
    W = {}
    W["w1g"], W["w1u"], W["w1d"] = f1g, f1u, f1d
    W["w2g"], W["w2u"], W["w2d"] = f2g, f2u, f2d
    W["w_in"], W["w_out"] = w_in, w_out
    W["gains"] = _pl(norm_gains, KC).reshape(128, -1)
    W["poolw"] = pool_w
    W["pscale"] = _pl(pool_scale, 4).reshape(128, -1)
    W["convw"] = _pl(conv_w, 6).reshape(128, -1)
    W["mixg"] = _pl(mix_gain, MIXC).reshape(128, -1)
    wsT = np.ascontiguousarray(sgu_w.transpose(3, 0, 1, 2))
    W["sguwT"] = wsT.reshape(128, -1)
    ts = cfg.tsamp
    bd = np.zeros_like(wsT)
    for r in range(128 // ts):
        bd[r * ts:(r + 1) * ts, :, :, r * ts:(r + 1) * ts] = wsT[0:ts, :, :, 0:ts]
    W["sguwbd"] = bd.reshape(128, -1)
    W["sgub"] = np.ascontiguousarray(np.broadcast_to(sgu_b.reshape(1, -1), (128, L * 6 * 128))).astype(np.float32)
    t = np.arange(128)
    mask = (t[:, None] <= t[None, :]).astype(np.float32)
    W["mask"] = mask
    mbd = np.zeros((128, 128), np.float32)
    for r in range(128 // ts):
        mbd[r * ts:(r + 1) * ts, r * ts:(r + 1) * ts] = mask[0:ts, 0:ts]
    W["maskbd"] = mbd
    return {k: np.ascontiguousarray(v, dtype=np.float32) for k, v in W.items()}


_NC_CACHE = {}


def get_program(cfg, key):
    if key not in _NC_CACHE:
        _NC_CACHE[key] = build_program(cfg)
    return _NC_CACHE[key]


def kernel(x_prompt, x_sample, state_pool, state_conv, w_in, w_out, pool_w, pool_scale, conv_w, sgu_w, sgu_b,
           ffn1_gate, ffn1_up, ffn1_down, ffn2_gate, ffn2_up, ffn2_down, norm_gains, mix_gain):
    cfg = Cfg()
    f32 = lambda a: np.asarray(a, dtype=np.float32)
    x_prompt, x_sample, state_pool, state_conv = map(f32, (x_prompt, x_sample, state_pool, state_conv))
    W = make_shared_weights(cfg, f32(w_in), f32(w_out), f32(pool_w), f32(pool_scale), f32(conv_w), f32(sgu_w), f32(sgu_b),
                            f32(ffn1_gate), f32(ffn1_up), f32(ffn1_down), f32(ffn2_gate), f32(ffn2_up), f32(ffn2_down),
                            f32(norm_gains), f32(mix_gain))
    B, S, D = x_prompt.shape
    L = cfg.L
    NSEG = 4
    SEG = S // NSEG
    H = cfg.halo
    in_maps = []
    for c in range(8):
        b, s = c // NSEG, c % NSEG
        if s == 0:
            ext = np.concatenate([np.zeros((H, D), np.float32), x_prompt[b, 0:SEG]], axis=0)
        else:
            ext = x_prompt[b, s * SEG - H:(s + 1) * SEG]
        in_maps.append(make_core_inputs(cfg, ext, x_sample[4 * c:4 * c + 4], state_pool[:, 4 * c:4 * c + 4],
                                        state_conv[:, 4 * c:4 * c + 4], W, s == 0))
    nc = get_program(cfg, "full")
    res = run_bass_kernel_spmd(nc, in_maps, core_ids=list(range(8)))
    R = res.results
    y_prompt = np.zeros((B, S, D), np.float32)
    y_sample = np.zeros(x_sample.shape, np.float32)
    npp = np.zeros((L, B, 15, POOLW), np.float32)
    ncp = np.zeros((L, B, 2, CONVC), np.float32)
    nps = np.zeros((L, 32, 15, POOLW), np.float32)
    ncs = np.zeros((L, 32, 2, CONVC), np.float32)
    nvs = np.zeros((L, 32, 64, SGUW), np.float32)
    for c in range(8):
        b, s = c // NSEG, c % NSEG
        yT = R[c]["yT"]
        y = yT.T
        y_prompt[b, s * SEG:(s + 1) * SEG] = y[H:H + SEG]
        y_sample[4 * c:4 * c + 4] = y[H + SEG:].reshape(4, 64, D)
        if s == NSEG - 1:
            pt = R[c]["pool_tail_p"].reshape(128, L, 4, 15)
            npp[:, b] = pt.transpose(1, 3, 2, 0).reshape(L, 15, POOLW)
            ctl = R[c]["conv_tail_p"].reshape(128, L, 6, 2)
            ncp[:, b] = ctl.transpose(1, 3, 2, 0).reshape(L, 2, CONVC)
        ps_ = R[c]["pool_tail_s"].reshape(128, L, 4, 4, 15)
        nps[:, 4 * c:4 * c + 4] = ps_.transpose(1, 2, 4, 3, 0).reshape(L, 4, 15, POOLW)
        cs_ = R[c]["conv_tail_s"].reshape(128, L, 4, 6, 2)
        ncs[:, 4 * c:4 * c + 4] = cs_.transpose(1, 2, 4, 3, 0).reshape(L, 4, 2, CONVC)
        nvs[:, 4 * c:4 * c + 4] = R[c]["v_s"].reshape(L, 4, 64, SGUW)
    return (y_prompt, y_sample, npp, ncp, nps, ncs, nvs)
```

```python
import numpy as np
import concourse.bass as bass
import concourse.mybir as mybir
from concourse.bass_utils import run_bass_kernel_spmd

F32 = mybir.dt.float32
BF16 = mybir.dt.bfloat16
AF = mybir.ActivationFunctionType
ALU = mybir.AluOpType

EPS = 1e-6
POOLW = 512
CONVC = 768
SGUW = 768
MIXW = 2048
INW = 4352
MIXC = MIXW // 128
TB = 512
SEM_LIMIT = 30000


class Cfg:
    def __init__(self, D=2048, DFF=5632, L=4, blocks=None, halo=256, nsamp=4, tsamp=64):
        self.D, self.DFF, self.L = D, DFF, L
        self.KC = D // 128
        self.JC = DFF // 128
        self.JH = self.JC // 2
        self.JQ = self.JH // 2
        assert self.JQ * 4 == self.JC
        self.blocks = blocks if blocks is not None else [512, 512, 512, 512, 256]
        self.halo = halo
        self.nsamp, self.tsamp = nsamp, tsamp
        self.NTOK = TB * len(self.blocks)
        self.NEXT = sum(self.blocks)
        self.NS = nsamp * tsamp
        assert self.NTOK == self.NEXT + self.NS


class Region:
    __slots__ = ("w", "r")

    def __init__(self):
        self.w = None
        self.r = []


class EngCtx:
    def __init__(self, prog, name, is_pe=False):
        self.prog = prog
        self.name = name
        self.thunks = []
        self.sem = None
        self.count = 0
        self.waited = {}
        self.own = set()
        self.is_pe = is_pe
        self.nsem = 0

    def wait(self, tok):
        sem, val = tok
        if self.is_pe and sem in self.own:
            return
        if self.waited.get(sem, 0) >= val:
            return
        self.waited[sem] = val
        self.thunks.append(lambda e, sem=sem, val=val: e.wait_ge(sem, val))

    def emit(self, fn):
        if self.sem is None or self.count >= SEM_LIMIT:
            self.sem = self.prog.nc.alloc_semaphore(f"p_{self.name}_{self.nsem}")
            self.nsem += 1
            self.own.add(self.sem)
            self.count = 0
        self.count += 1
        sem, val = self.sem, self.count
        self.thunks.append(lambda e, fn=fn, sem=sem: fn(e).then_inc(sem, 1))
        return (sem, val)

    def emit_dma(self, fn, sem_state):
        sem_state[1] += 16
        sem = sem_state[0]
        self.thunks.append(lambda e, fn=fn, sem=sem: fn(e).then_inc(sem, 16))
        return (sem, sem_state[1])


class Prog:
    def __init__(self, cfg):
        self.cfg = cfg
        self.nc = bass.Bass("TRN2", target_bir_lowering=False)
        self.pe = EngCtx(self, "pe", is_pe=True)
        self.act = EngCtx(self, "act")
        self.dve = EngCtx(self, "dve")
        self.pool = EngCtx(self, "pool")
        self.sp = EngCtx(self, "sp")
        self.nbank = 0

    def op(self, eng, fn, reads=(), writes=()):
        deps = []
        for R in reads:
            if R.w is not None:
                deps.append(R.w)
        for W in writes:
            if W.w is not None:
                deps.append(W.w)
            deps.extend(W.r)
        for t in deps:
            eng.wait(t)
        tok = eng.emit(fn)
        for R in reads:
            R.r.append(tok)
        for W in writes:
            W.w = tok
            W.r = []
        return tok

    def dma(self, eng, fn, sem_state, reads=(), writes=()):
        deps = []
        for R in reads:
            if R.w is not None:
                deps.append(R.w)
        for W in writes:
            if W.w is not None:
                deps.append(W.w)
            deps.extend(W.r)
        for t in deps:
            eng.wait(t)
        tok = eng.emit_dma(fn, sem_state)
        for R in reads:
            R.r.append(tok)
        for W in writes:
            W.w = tok
            W.r = []
        return tok

    def new_dma_sem(self, name):
        return [self.nc.alloc_semaphore(name), 0]


def build_program(cfg):
    P = Prog(cfg)
    nc = P.nc
    D, DFF, L, KC, JC, JH, JQ = cfg.D, cfg.DFF, cfg.L, cfg.KC, cfg.JC, cfg.JH, cfg.JQ
    NB = len(cfg.blocks)
    NTOK = cfg.NTOK
    NSAMP, TS = cfg.nsamp, cfg.tsamp
    NS = cfg.NS

    def din(name, shape, dt=F32):
        return nc.dram_tensor(name, list(shape), dt, kind="ExternalInput").ap()

    def dout(name, shape, dt=F32):
        return nc.dram_tensor(name, list(shape), dt, kind="ExternalOutput").ap()

    xT_d = din("xT", [D, NTOK])
    w_ffn = {}
    for f in (1, 2):
        w_ffn[f] = (din(f"w{f}g", [L, D, DFF]), din(f"w{f}u", [L, D, DFF]), din(f"w{f}d", [L, DFF, D]))
    win_d = din("w_in", [L, D, INW])
    wout_d = din("w_out", [L, MIXW, D])
    gains_d = din("gains", [128, L * 6 * KC])
    poolw_d = din("poolw", [L, 4, 128, 128])
    pscale_d = din("pscale", [128, L * 4])
    convw_d = din("convw", [128, L * 3 * 6])
    mixg_d = din("mixg", [128, L * MIXC])
    sguw_d = din("sguwT", [128, L * 6 * 128])
    sguwbd_d = din("sguwbd", [128, L * 6 * 128])
    sgub_d = din("sgub", [128, L * 6 * 128])
    mask_d = din("mask", [128, 128])
    maskbd_d = din("maskbd", [128, 128])
    invc_d = din("invc", [128, 4 * 16])
    phist_d = din("phist", [128, L * NSAMP * 4 * 15])
    chist_d = din("chist", [128, L * NSAMP * 6 * 2])

    yT_d = dout("yT", [D, NTOK])
    ptp_d = dout("pool_tail_p", [128, L * 4 * 15])
    ctp_d = dout("conv_tail_p", [128, L * 6 * 2])
    pts_d = dout("pool_tail_s", [128, L * NSAMP * 4 * 15])
    cts_d = dout("conv_tail_s", [128, L * NSAMP * 6 * 2])
    vs_d = dout("v_s", [L, NS, SGUW])

    def tiles_ffn(f, l):
        g, u, d = w_ffn[f]
        out = []
        for hf in range(2):
            gu = []
            for jg in range(JH // 2):
                c0 = (hf * JH + jg * 2) * 128
                gu.append((g[l, :, c0:c0 + 256].rearrange("(k p) c -> p k c", p=128), KC, 256))
                gu.append((u[l, :, c0:c0 + 256].rearrange("(k p) c -> p k c", p=128), KC, 256))
            dn = []
            for mg in range(D // 256):
                for q in range(2):
                    r0 = (hf * JH + q * JQ) * 128
                    dn.append((d[l, r0:r0 + JQ * 128, mg * 256:(mg + 1) * 256].rearrange("(j p) c -> p j c", p=128), JQ, 256))
            out.append((gu, dn))
        return out

    def tiles_in(l):
        return [(win_d[l, :, ct * 256:(ct + 1) * 256].rearrange("(k p) c -> p k c", p=128), KC, 256) for ct in range(INW // 256)]

    def tiles_out(l):
        return [(wout_d[l, :, ct * 256:(ct + 1) * 256].rearrange("(k p) c -> p k c", p=128), MIXC, 256) for ct in range(D // 256)]

    SLOT_ELEMS = max(KC, MIXC, JQ) * 256
    n_tiles_layer = 2 * (2 * (JH + (D // 256) * 2)) + INW // 256 + D // 256
    n_tiles = n_tiles_layer * L
    scratch_l = [nc.dram_tensor(f"wscratch{l}", [n_tiles_layer, 128, SLOT_ELEMS], BF16, kind="Internal").ap() for l in range(L)]
    scratch_regions = [Region() for _ in range(n_tiles)]

    def sb(name, shape, dt):
        return nc.alloc_sbuf_tensor(name, list(shape), dt)

    xT = sb("xT_sb", [128, KC, TB], F32)
    xn = sb("xn_sb", [128, KC, TB], BF16)
    NA = max(JH, 20)
    aT = sb("aT_sb", [128, NA, TB], BF16)
    ymix = aT
    dT = aT[:, 16:20, :]
    NF = max(KC, 16)
    fT = sb("fT_sb", [128, NF, TB], F32)
    pT = fT[:, 0:4, :]
    cau = fT[:, 4:10, :]
    fflat = fT[:].rearrange("p a b -> p (a b)")
    pext1 = fflat[:, 10 * TB:10 * TB + 15 + TB]
    ZW = 2 + TB + 2 * NSAMP + 8
    zbv = [fflat[:, 12 * TB:12 * TB + ZW], fflat[:, 14 * TB:14 * TB + ZW]]
    ybuf = sb("ybuf_sb", [128, 6, TB], F32)
    sg = sb("sg_sb", [128, 3, TB + 16], F32)
    sq = sb("sq_sb", [128, 4, TB], BF16)
    rstd = sb("rstd_sb", [128, 2, TB], F32)
    v_bf = sb("vbf_sb", [128, 4, SGUW], BF16)
    v_f32 = sb("vf32_sb", [128, 2, SGUW], F32)
    slots = [sb(f"slot{i}", [128, SLOT_ELEMS], BF16) for i in range(4)]
    gains = sb("gains_sb", [128, L * 6 * KC], F32)
    poolw = sb("poolw_sb", [128, L * 4, 128], BF16)
    pscale = sb("pscale_sb", [128, L * 4], F32)
    convw = sb("convw_sb", [128, L * 3 * 6], F32)
    mixg = sb("mixg_sb", [128, L * MIXC], F32)
    wsT = sb("wsT_sb", [128, L * 6, 128], BF16)
    wsTbd = sb("wsTbd_sb", [128, L * 6, 128], BF16)
    sgub = sb("sgub_sb", [128, 6, 128], F32)
    maskt = sb("mask_sb", [128, 2, 128], F32)
    invc = sb("invc_sb", [128, 4, 16], F32)
    ones = sb("ones_sb", [128, 128], BF16)
    epsb = sb("epsb_sb", [128, 4], F32)
    phist = sb("phist_sb", [128, L * NSAMP, 4, 15], F32)
    chist = sb("chist_sb", [128, L * NSAMP, 6, 2], F32)
    ptail = sb("ptail_sb", [128, L, 4, 15], F32)
    ctail = sb("ctail_sb", [128, L, 6, 2], F32)
    pts = sb("pts_sb", [128, L * NSAMP, 4, 15], F32)
    cts = sb("cts_sb", [128, L * NSAMP, 6, 2], F32)

    psb = [nc.alloc_psum_tensor(f"ps{i}", [128, TB], F32) for i in range(8)]
    bank_r = [Region() for _ in range(8)]

    def next_bank():
        i = P.nbank % 7
        P.nbank += 1
        return psb[i], bank_r[i]

    ss_bank, ss_r = psb[7], bank_r[7]

    R_x = [Region() for _ in range(KC)]
    R_xn = [Region() for _ in range(KC)]
    R_a = [Region() for _ in range(NA)]
    R_ym = R_a
    R_dT = R_a[16:20]
    R_f = [Region() for _ in range(NF)]
    R_pg = R_f[0:4]
    R_cau = R_f[4:10]
    R_pext = R_f[10:12]
    R_zb = [R_f[12:14], R_f[14:16]]
    R_y = [Region() for _ in range(6)]
    R_sq = [Region() for _ in range(4)]
    R_sg = [Region() for _ in range(3)]
    R_rstd = [Region(), Region()]
    R_vbf = [Region() for _ in range(4)]
    R_vf = [Region(), Region()]
    R_slot = [Region() for _ in range(4)]
    R_const = Region()
    R_sgub = Region()
    R_ptail = [Region() for _ in range(L)]
    R_ctail = [Region() for _ in range(L)]
    R_pts = Region()
    R_cts = Region()
    R_tmpc = Region()

    ctr = {"sq": 0, "rstd": 0, "sg": 0, "zb": 0, "slot": 0}

    sem_x = P.new_dma_sem("sem_x")
    sem_y = P.new_dma_sem("sem_y")
    sem_small = P.new_dma_sem("sem_small")
    sem_const = P.new_dma_sem("sem_const")
    sem_slot = [P.new_dma_sem(f"sem_slot{i}") for i in range(4)]
    sem_wb = [P.new_dma_sem(f"sem_wb{i}") for i in range(4)]

    pe, act, dve, pool, sp = P.pe, P.act, P.dve, P.pool, P.sp

    def cload(dst_ap, src_ap, eng=sp):
        P.dma(eng, lambda e, d=dst_ap, s=src_ap: e.dma_start(out=d, in_=s), sem_const, writes=[R_const])

    cload(gains[:], gains_d[:, :])
    cload(pscale[:], pscale_d[:, :])
    cload(convw[:], convw_d[:, :])
    cload(mixg[:], mixg_d[:, :])
    cload(maskt[:, 0, :], mask_d[:, :])
    cload(maskt[:, 1, :], maskbd_d[:, :])
    cload(invc[:].rearrange("p a b -> p (a b)"), invc_d[:, :])
    cload(phist[:].rearrange("p a b c -> p (a b c)"), phist_d[:, :])
    cload(chist[:].rearrange("p a b c -> p (a b c)"), chist_d[:, :])
    cload(poolw[:], poolw_d.rearrange("l g c d -> c (l g) d"), eng=pool)
    assert L * 6 * 128 <= KC * TB or True
    tmpw = fT[:].rearrange("p a b -> p (a b)")
    n_w = L * 6 * 128
    if n_w <= KC * TB:
        for (src, dst, mi) in ((sguw_d, wsT, 0), (sguwbd_d, wsTbd, 1)):
            P.dma(sp, lambda e, s=src: e.dma_start(out=tmpw[:, 0:n_w], in_=s[:, :]), sem_const, writes=[R_tmpc])
            for lh in range(L * 6):
                P.op(dve, lambda e, lh=lh, dst=dst, mi=mi: e.tensor_tensor(
                    out=dst[:, lh, :], in0=tmpw[:, lh * 128:(lh + 1) * 128], in1=maskt[:, mi, :], op=ALU.mult),
                    reads=[R_tmpc, R_const], writes=[R_const])
            R_tmpc.r = [R_const.w]
    else:
        for (src, dst, mi) in ((sguw_d, wsT, 0), (sguwbd_d, wsTbd, 1)):
            for lh in range(L * 6):
                P.dma(sp, lambda e, s=src, lh=lh: e.dma_start(out=tmpw[:, 0:128], in_=s[:, lh * 128:(lh + 1) * 128]),
                      sem_const, writes=[R_tmpc])
                P.op(dve, lambda e, lh=lh, dst=dst, mi=mi: e.tensor_tensor(
                    out=dst[:, lh, :], in0=tmpw[:, 0:128], in1=maskt[:, mi, :], op=ALU.mult),
                    reads=[R_tmpc, R_const], writes=[R_tmpc])
        R_const.w = R_tmpc.w
    P.op(dve, lambda e: e.memset(ones[:], 1.0), writes=[R_const])
    for wi, wd in enumerate((D, POOLW, CONVC)):
        P.op(dve, lambda e, wi=wi, wd=wd: e.memset(epsb[:, wi:wi + 1], float(wd * EPS)), writes=[R_const])
    for l in range(L):
        P.op(dve, lambda e, l=l: e.memset(ptail[:, l], 0.0), writes=[R_ptail[l]])
        P.op(dve, lambda e, l=l: e.memset(ctail[:, l], 0.0), writes=[R_ctail[l]])
    sD = float(np.sqrt(D))
    for l in range(L):
        for s in range(6):
            sc = sD * (0.5 if s in (1, 5) else 1.0)
            o = (l * 6 + s) * KC
            P.op(dve, lambda e, o=o, sc=sc: e.tensor_scalar(
                out=gains[:, o:o + KC], in0=gains[:, o:o + KC], scalar1=sc, scalar2=None, op0=ALU.mult),
                reads=[R_const], writes=[R_const])
        o = l * MIXC
        P.op(dve, lambda e, o=o: e.tensor_scalar(
            out=mixg[:, o:o + 4], in0=mixg[:, o:o + 4], scalar1=float(np.sqrt(POOLW)), scalar2=None, op0=ALU.mult),
            reads=[R_const], writes=[R_const])
        P.op(dve, lambda e, o=o: e.tensor_scalar(
            out=mixg[:, o + 4:o + 16], in0=mixg[:, o + 4:o + 16], scalar1=float(np.sqrt(CONVC)), scalar2=None, op0=ALU.mult),
            reads=[R_const], writes=[R_const])

    def gcol(l, s, c):
        o = (l * 6 + s) * KC + c
        return gains[:, o:o + 1]

    tile_counter = {"i": 0}

    def fetch_tile(bi, tdesc, l):
        view, a, b = tdesc
        ti = tile_counter["i"]
        tile_counter["i"] += 1
        tg = ti
        si = ctr["slot"] % 4
        ctr["slot"] += 1
        slot = slots[si]
        sv = slot[:, 0:a * b].rearrange("p (a b) -> p a b", b=b)
        do_cast = (bi == 0) or (bi == 1 and l >= L // 2)
        do_wb = (bi == 0 and l < L // 2) or (bi == 1 and l >= L // 2)
        if do_cast:
            P.dma(pool, lambda e, sv=sv, view=view: e.dma_start(out=sv, in_=view), sem_slot[si], writes=[R_slot[si]])
            if do_wb:
                P.dma(sp, lambda e, tg=tg, slot=slot, n=a * b: e.dma_start(out=scratch_l[tg // n_tiles_layer][tg % n_tiles_layer, :, 0:n], in_=slot[:, 0:n]),
                      sem_wb[si], reads=[R_slot[si]], writes=[scratch_regions[tg]])
        else:
            P.dma(sp, lambda e, tg=tg, slot=slot, n=a * b: e.dma_start(out=slot[:, 0:n], in_=scratch_l[tg // n_tiles_layer][tg % n_tiles_layer, :, 0:n]),
                  sem_slot[si], reads=[scratch_regions[tg]], writes=[R_slot[si]])
        return sv, R_slot[si]

    pending = []
    sched = {"dense": True}

    def flush_ss(keep=0):
        while len(pending) > keep:
            pending.pop(0)()

    def pe_op(fn, reads=(), writes=()):
        tok = P.op(pe, fn, reads=reads, writes=writes)
        flush_ss()
        return tok

    def ss_add(src, c, reg, first, last, cs=0):
        flush_ss(keep=2)
        i = ctr["sq"] % 4
        ctr["sq"] += 1
        P.op(act, lambda e, i=i: e.activation(out=sq[:, i, cs:TB], in_=src[:, c, cs:TB], func=AF.Square),
             reads=[reg], writes=[R_sq[i]])
        pending.append(lambda i=i, first=first, last=last: P.op(
            pe, lambda e: e.matmul(ss_bank[:, cs:TB], ones[:, :], sq[:, i, cs:TB], start=first, stop=last),
            reads=[R_sq[i], R_const], writes=[ss_r]))
        if not sched["dense"]:
            flush_ss()

    def ss_finish(width):
        flush_ss()
        ri = ctr["rstd"] % 2
        ctr["rstd"] += 1
        wi = 0 if width == D else {POOLW: 1, CONVC: 2}[width]
        P.op(act, lambda e, ri=ri, wi=wi: e.activation(
            out=rstd[:, ri, :], in_=ss_bank[:, :], func=AF.Ln, bias=epsb[:, wi:wi + 1], scale=1.0),
            reads=[R_const], writes=[R_rstd[ri], ss_r])
        P.op(act, lambda e, ri=ri: e.activation(out=rstd[:, ri, :], in_=rstd[:, ri, :], func=AF.Exp, scale=-0.5),
             reads=[R_rstd[ri]], writes=[R_rstd[ri]])
        return rstd[:, ri, :], R_rstd[ri]

    def ss_x_all():
        for c in range(KC):
            ss_add(xT, c, R_x[c], c == 0, c == KC - 1)

    def prenorm(l, s, cs=0):
        rs, rr = ss_finish(D)
        for c in range(KC):
            P.op(dve, lambda e, c=c, rs=rs: e.scalar_tensor_tensor(
                out=xn[:, c, cs:TB], in0=xT[:, c, cs:TB], scalar=gcol(l, s, c), in1=rs[:, cs:TB], op0=ALU.mult, op1=ALU.mult),
                reads=[R_x[c], rr, R_const], writes=[R_xn[c]])

    def postnorm_add(l, s, want_next, cs=0):
        rs, rr = ss_finish(D)
        prev = None

        def fin(pc, pui):
            P.op(dve, lambda e, c=pc, ui=pui: e.tensor_tensor(out=xT[:, c, cs:TB], in0=xT[:, c, cs:TB], in1=sg[:, ui, cs:TB], op=ALU.add),
                 reads=[R_sg[pui]], writes=[R_x[pc]])
            if want_next:
                ss_add(xT, pc, R_x[pc], pc == 0, pc == KC - 1, cs)
        for c in range(KC):
            ui = ctr["sg"] % 3
            ctr["sg"] += 1
            P.op(dve, lambda e, c=c, rs=rs, ui=ui: e.scalar_tensor_tensor(
                out=sg[:, ui, cs:TB], in0=fT[:, c, cs:TB], scalar=gcol(l, s, c), in1=rs[:, cs:TB], op0=ALU.mult, op1=ALU.mult),
                reads=[R_f[c], rr, R_const], writes=[R_sg[ui]])
            if prev is not None:
                fin(*prev)
            prev = (c, ui)
        fin(*prev)

    def ffn(bi, l, f, s_pre, s_post, want_next, cs=0):
        prenorm(l, s_pre, cs)
        tl = tiles_ffn(f, l)
        for hf in range(2):
            gu, dn = tl[hf]
            for jg in range(JH // 2):
                gv, gr = fetch_tile(bi, gu[2 * jg], l)
                uv, ur = fetch_tile(bi, gu[2 * jg + 1], l)
                for ci in range(2):
                    jl = jg * 2 + ci
                    bg_, bgr = next_bank()
                    bu_, bur = next_bank()

                    def mmg(e, wv=gv, bank=bg_, ci=ci):
                        ins = None
                        for k in range(KC):
                            ins = e.matmul(bank[:, cs:TB], wv[:, k, ci * 128:(ci + 1) * 128], xn[:, k, cs:TB], start=(k == 0), stop=(k == KC - 1))
                        return ins
                    pe_op(mmg, reads=[gr] + R_xn[:KC], writes=[bgr])
                    pe_op(lambda e, wv=uv, bank=bu_, ci=ci: mmg(e, wv, bank, ci), reads=[ur] + R_xn[:KC], writes=[bur])
                    si = ctr["sg"] % 3
                    ctr["sg"] += 1
                    P.op(act, lambda e, si=si, bank=bg_: e.activation(out=sg[:, si, cs:TB], in_=bank[:, cs:TB], func=AF.Silu),
                         reads=[bgr], writes=[R_sg[si]])
                    P.op(dve, lambda e, si=si, bank=bu_, jl=jl: e.tensor_tensor(out=aT[:, jl, cs:TB], in0=sg[:, si, cs:TB], in1=bank[:, cs:TB], op=ALU.mult),
                         reads=[R_sg[si], bur], writes=[R_a[jl]])
            ti = 0
            for mg in range(D // 256):
                b0, b0r = next_bank()
                b1, b1r = next_bank()
                banks = ((b0, b0r), (b1, b1r))
                for q in range(2):
                    dv, dr = fetch_tile(bi, dn[ti], l)
                    ti += 1
                    pe_op(lambda e, dv=dv, q=q, banks_l=banks: _mmd(e, dv, q, banks_l, aT, JQ, cs),
                          reads=[dr] + R_a[q * JQ:(q + 1) * JQ], writes=[b0r, b1r])
                for mi in range(2):
                    m = mg * 2 + mi
                    bank, br = banks[mi]
                    if hf == 0:
                        P.op(act, lambda e, m=m, bank=bank: e.activation(out=fT[:, m, cs:TB], in_=bank[:, cs:TB], func=AF.Copy),
                             reads=[br], writes=[R_f[m]])
                    else:
                        P.op(dve, lambda e, m=m, bank=bank: e.tensor_tensor(out=fT[:, m, cs:TB], in0=bank[:, cs:TB], in1=fT[:, m, cs:TB], op=ALU.add),
                             reads=[br, R_f[m]], writes=[R_f[m]])
                        ss_add(fT, m, R_f[m], m == 0, m == KC - 1, cs)
        postnorm_add(l, s_post, want_next, cs)

    def group_rms(l, y_chunks, c0, width):
        n = len(y_chunks)
        rs, rr = ss_finish(width)
        for c in range(n):
            o = l * MIXC + c0 + c
            P.op(dve, lambda e, c=c, rs=rs, o=o: e.scalar_tensor_tensor(
                out=ymix[:, c0 + c, :], in0=ybuf[:, y_chunks[c], :], scalar=mixg[:, o:o + 1], in1=rs, op0=ALU.mult, op1=ALU.mult),
                reads=[R_y[y_chunks[c]], rr, R_const], writes=[R_ym[c0 + c]])

    def mixer(bi, l):
        np_ = cfg.blocks[bi]
        has_s = np_ < TB
        last_prompt = (sum(cfg.blocks[:bi + 1]) == cfg.NEXT)
        P.dma(sp, lambda e: e.dma_start(out=sgub[:].rearrange("p a b -> p (a b)"), in_=sgub_d[:, l * 768:(l + 1) * 768]),
              sem_const, writes=[R_sgub])
        prenorm(l, 2)
        tin = tiles_in(l)
        order = [0, 1, 8, 9, 10, 2, 3, 4, 5, 6, 7, 14, 15, 16, 11, 12, 13]

        def fm_group(wv, ci):
            bank, br = next_bank()

            def mm(e, wv=wv, bank=bank, ci=ci):
                ins = None
                for k in range(KC):
                    ins = e.matmul(bank[:, :], wv[:, k, ci * 128:(ci + 1) * 128], xn[:, k, :], start=(k == 0), stop=(k == KC - 1))
                return ins
            return bank, br, mm

        def cw(k, c):
            o = (l * 3 + k) * 6 + c
            return convw[:, o:o + 1]

        def sgu_pe_bias():
            for h in range(6):
                bank, br = next_bank()
                for tt in range(4):
                    is_s = has_s and tt * 128 >= np_
                    wmat = wsTbd if is_s else wsT
                    pe_op(lambda e, bank=bank, tt=tt, h=h, wmat=wmat: e.matmul(
                        bank[:, tt * 128:(tt + 1) * 128], v_bf[:, tt, h * 128:(h + 1) * 128], wmat[:, l * 6 + h, :], start=True, stop=True),
                        reads=[R_vbf[tt], R_const], writes=[br])
                for tt in range(4):
                    is_s = has_s and tt * 128 >= np_
                    if not is_s:
                        P.op(dve, lambda e, bank=bank, tt=tt, h=h: e.tensor_tensor(
                            out=ybuf[:, h, tt * 128:(tt + 1) * 128], in0=bank[:, tt * 128:(tt + 1) * 128], in1=sgub[:, h, :], op=ALU.add),
                            reads=[br, R_sgub], writes=[R_y[h]])
                    else:
                        for hh in range(128 // TS):
                            c0 = tt * 128 + hh * TS
                            P.op(dve, lambda e, bank=bank, c0=c0, h=h: e.tensor_tensor(
                                out=ybuf[:, h, c0:c0 + TS], in0=bank[:, c0:c0 + TS], in1=sgub[:, h, 0:TS], op=ALU.add),
                                reads=[br, R_sgub], writes=[R_y[h]])

        for ct in order:
            wv, wr = fetch_tile(bi, tin[ct], l)
            if ct <= 13:
                for ci in range(2):
                    bank, br, mm = fm_group(wv, ci)
                    pe_op(mm, reads=[wr] + R_xn[:KC], writes=[br])
                    if ct <= 1:
                        g = ct * 2 + ci
                        P.op(act, lambda e, g=g, bank=bank: e.activation(out=pT[:, g, :], in_=bank[:, :], func=AF.Copy),
                             reads=[br], writes=[R_pg[g]])
                    elif 8 <= ct <= 10:
                        c = (ct - 8) * 2 + ci
                        P.op(act, lambda e, c=c, bank=bank: e.activation(out=cau[:, c, :], in_=bank[:, :], func=AF.Copy),
                             reads=[br], writes=[R_cau[c]])
                    elif 2 <= ct <= 4:
                        c = (ct - 2) * 2 + ci
                        conv_chunk(l, c, bank, br, np_, has_s, last_prompt, cw)
                    elif 5 <= ct <= 7:
                        c = (ct - 5) * 2 + ci
                        P.op(dve, lambda e, c=c, bank=bank: e.tensor_tensor(out=ybuf[:, c, :], in0=bank[:, :], in1=cau[:, c, :], op=ALU.mult),
                             reads=[br, R_cau[c]], writes=[R_y[c]])
                        ss_add(ybuf, c, R_y[c], c == 0, c == 5)
                    else:
                        c = (ct - 11) * 2 + ci
                        P.op(act, lambda e, c=c, bank=bank: e.activation(out=cau[:, c, :], in_=bank[:, :], func=AF.Copy),
                             reads=[br], writes=[R_cau[c]])
                        P.op(dve, lambda e, h=c: e.tensor_tensor(out=ybuf[:, h, :], in0=ybuf[:, h, :], in1=cau[:, h, :], op=ALU.mult),
                             reads=[R_cau[c]], writes=[R_y[c]])
                        ss_add(ybuf, c, R_y[c], c == 0, c == 5)
                if ct == 1:
                    pool_dve(bi, l, np_, has_s, last_prompt)
                    if not sched["dense"]:
                        pool_pe(l)
                elif ct == 10 and sched["dense"]:
                    pool_pe(l)
                elif ct == 2:
                    group_rms(l, list(range(4)), 0, POOLW)
            else:
                cv = (ct - 14) * 256
                for tt in range(4):
                    bank, br = next_bank()

                    def mmv(e, wv=wv, bank=bank, tt=tt):
                        ins = None
                        for k in range(KC):
                            ins = e.matmul(bank[:, 0:256], xn[:, k, tt * 128:(tt + 1) * 128], wv[:, k, :], start=(k == 0), stop=(k == KC - 1))
                        return ins
                    pe_op(mmv, reads=[wr] + R_xn[:KC], writes=[br])
                    P.op(act, lambda e, bank=bank, tt=tt, cv=cv: e.activation(out=v_bf[:, tt, cv:cv + 256], in_=bank[:, 0:256], func=AF.Copy),
                         writes=[R_vbf[tt], br])
                    if has_s and tt * 128 >= np_:
                        ts_ = (tt * 128 - np_) // 128
                        P.op(dve, lambda e, bank=bank, ts_=ts_, cv=cv: e.tensor_copy(out=v_f32[:, ts_, cv:cv + 256], in_=bank[:, 0:256]),
                             writes=[R_vf[ts_], br])
                if ct == 14:
                    group_rms(l, list(range(6)), 4, CONVC)
                if ct == 16:
                    if has_s:
                        for ts_ in range((TB - np_) // 128):
                            P.dma(pool, lambda e, ts_=ts_: e.dma_start(out=vs_d[l, ts_ * 128:(ts_ + 1) * 128, :], in_=v_f32[:, ts_, :]),
                                  sem_small, reads=[R_vf[ts_]])
                    sgu_pe_bias()
        group_rms(l, list(range(6)), 10, SGUW)
        tout = tiles_out(l)
        for ct in range(D // 256):
            wv, wr = fetch_tile(bi, tout[ct], l)
            for ci in range(2):
                bank, br = next_bank()

                def mmo(e, wv=wv, bank=bank, ci=ci):
                    ins = None
                    for k in range(MIXC):
                        ins = e.matmul(bank[:, :], wv[:, k, ci * 128:(ci + 1) * 128], ymix[:, k, :], start=(k == 0), stop=(k == MIXC - 1))
                    return ins
                pe_op(mmo, reads=[wr] + R_ym[:MIXC], writes=[br])
                m = ct * 2 + ci
                P.op(act, lambda e, m=m, bank=bank: e.activation(out=fT[:, m, :], in_=bank[:, :], func=AF.Copy),
                     reads=[br], writes=[R_f[m]])
                ss_add(fT, m, R_f[m], m == 0, m == KC - 1)
        postnorm_add(l, 3, True)

    def conv_chunk(l, c, bank, br, np_, has_s, last_prompt, cw):
        zi = ctr["zb"] % 2
        ctr["zb"] += 1
        z = zbv[zi]
        rzl = R_zb[zi]
        P.op(dve, lambda e, z=z, c=c: e.tensor_copy(out=z[:, 0:2], in_=ctail[:, l, c, :]), reads=[R_ctail[l]], writes=rzl)
        P.op(dve, lambda e, z=z, c=c, bank=bank: e.tensor_tensor(out=z[:, 2:2 + np_], in0=bank[:, 0:np_], in1=cau[:, c, 0:np_], op=ALU.mult),
             reads=[br, R_cau[c]], writes=rzl)
        if has_s:
            zs = z[:, 2 + np_:2 + np_ + NSAMP * (TS + 2)].rearrange("p (s t) -> p s t", t=TS + 2)
            P.op(dve, lambda e, zs=zs, c=c: e.tensor_copy(out=zs[:, :, 0:2], in_=chist[:, l * NSAMP:(l + 1) * NSAMP, c, :]),
                 reads=[R_const], writes=rzl)
            P.op(dve, lambda e, zs=zs, c=c, bank=bank: e.tensor_tensor(
                out=zs[:, :, 2:2 + TS], in0=bank[:, np_:TB].rearrange("p (s t) -> p s t", t=TS),
                in1=cau[:, c, np_:TB].rearrange("p (s t) -> p s t", t=TS), op=ALU.mult),
                reads=[br, R_cau[c]], writes=rzl)
        acc = cau[:, c, 0:np_]
        P.op(dve, lambda e, z=z, acc=acc, c=c: e.tensor_scalar(out=acc, in0=z[:, 0:np_], scalar1=cw(0, c), scalar2=None, op0=ALU.mult),
             reads=rzl + [R_const], writes=[R_cau[c]])
        for k in (1, 2):
            P.op(dve, lambda e, z=z, acc=acc, c=c, k=k: e.scalar_tensor_tensor(
                out=acc, in0=z[:, k:k + np_], scalar=cw(k, c), in1=acc, op0=ALU.mult, op1=ALU.add),
                reads=rzl + [R_const], writes=[R_cau[c]])
        if has_s:
            accs = cau[:, c, np_:TB].rearrange("p (s t) -> p s t", t=TS)
            P.op(dve, lambda e, zs=zs, accs=accs, c=c: e.tensor_scalar(out=accs, in0=zs[:, :, 0:TS], scalar1=cw(0, c), scalar2=None, op0=ALU.mult),
                 reads=rzl + [R_const], writes=[R_cau[c]])
            for k in (1, 2):
                P.op(dve, lambda e, zs=zs, accs=accs, c=c, k=k: e.scalar_tensor_tensor(
                    out=accs, in0=zs[:, :, k:k + TS], scalar=cw(k, c), in1=accs, op0=ALU.mult, op1=ALU.add),
                    reads=rzl + [R_const], writes=[R_cau[c]])
            P.op(dve, lambda e, zs=zs, c=c: e.tensor_copy(out=cts[:, l * NSAMP:(l + 1) * NSAMP, c, :], in_=zs[:, :, TS:TS + 2]),
                 reads=rzl, writes=[R_cts])
        P.op(dve, lambda e, z=z, c=c: e.tensor_copy(out=ctail[:, l, c, :], in_=z[:, np_:np_ + 2]), reads=rzl, writes=[R_ctail[l]])

    def pool_segment(l, n, col0, hist_ap, hist_reg, tail_ap, tail_reg, fix_col):
        for g in range(4):
            w = 2 << g
            P.op(dve, lambda e, g=g: e.tensor_copy(out=pext1[:, 0:15], in_=hist_ap[:, g, :]), reads=[hist_reg], writes=R_pext)
            P.op(dve, lambda e, g=g: e.tensor_copy(out=pext1[:, 15:15 + n], in_=pT[:, g, col0:col0 + n]), reads=[R_pg[g]], writes=R_pext)
            P.op(dve, lambda e, g=g: e.tensor_copy(out=tail_ap[:, g, :], in_=pext1[:, n:n + 15]), reads=R_pext, writes=[tail_reg])
            cur = pext1
            cur_r = R_pext
            k = 1
            ti = 0
            while k < w:
                nxt = sg[:, ti, :]
                nr = [R_sg[ti]]
                lo = 2 * k - 1
                P.op(dve, lambda e, cur=cur, nxt=nxt, lo=lo, k=k: e.tensor_tensor(
                    out=nxt[:, lo:15 + n], in0=cur[:, lo:15 + n], in1=cur[:, lo - k:15 + n - k], op=ALU.add),
                    reads=cur_r, writes=nr)
                cur, cur_r = nxt, nr
                ti ^= 1
                k *= 2
            P.op(dve, lambda e, cur=cur, g=g, w=w: e.scalar_tensor_tensor(
                out=dT[:, g, col0:col0 + n], in0=cur[:, 15:15 + n], scalar=1.0 / w, in1=pT[:, g, col0:col0 + n], op0=ALU.mult, op1=ALU.subtract),
                reads=cur_r + [R_pg[g]], writes=[R_dT[g]])
            if fix_col is not None:
                fc = fix_col
                oth = sg[:, ti, :]
                orr = [R_sg[ti]]
                P.op(dve, lambda e, cur=cur, g=g, oth=oth, fc=fc: e.tensor_tensor(
                    out=oth[:, 0:16], in0=cur[:, 15 + fc:15 + fc + 16], in1=invc[:, g, :], op=ALU.mult),
                    reads=cur_r + [R_const], writes=orr)
                P.op(dve, lambda e, g=g, oth=oth, fc=fc: e.tensor_tensor(
                    out=dT[:, g, col0 + fc:col0 + fc + 16], in0=oth[:, 0:16], in1=pT[:, g, col0 + fc:col0 + fc + 16], op=ALU.subtract),
                    reads=orr + [R_pg[g]], writes=[R_dT[g]])

    def pool_dve(bi, l, np_, has_s, last_prompt):
        ext0 = sum(cfg.blocks[:bi])
        fix = None
        if ext0 <= cfg.halo < ext0 + np_:
            fix = cfg.halo - ext0
        pool_segment(l, np_, 0, ptail[:, l], R_ptail[l], ptail[:, l], R_ptail[l], fix)
        if has_s:
            for s_ in range(NSAMP):
                pool_segment(l, TS, np_ + s_ * TS, phist[:, l * NSAMP + s_], R_const, pts[:, l * NSAMP + s_], R_pts, None)

    def pool_pe(l):
        bks = [next_bank() for _ in range(4)]
        for g in range(4):
            bank, br = bks[g]
            pe_op(lambda e, bank=bank, g=g: e.matmul(bank[:, :], poolw[:, l * 4 + g, :], dT[:, g, :], start=True, stop=True),
                  reads=[R_dT[g], R_const], writes=[br])
        for g in range(4):
            bank, br = bks[g]
            P.op(dve, lambda e, bank=bank, g=g: e.tensor_scalar(
                out=ybuf[:, g, :], in0=bank[:, :], scalar1=pscale[:, l * 4 + g:l * 4 + g + 1], scalar2=None, op0=ALU.mult),
                reads=[br, R_const], writes=[R_y[g]])
            ss_add(ybuf, g, R_y[g], g == 0, g == 3)

    for bi in range(NB):
        col0 = bi * TB
        tile_counter["i"] = 0
        sched["dense"] = bi >= 2
        P.dma(pool, lambda e, col0=col0: e.dma_start(out=xT[:], in_=xT_d[:, col0:col0 + TB].rearrange("(k p) c -> p k c", p=128)),
              sem_x, writes=R_x)
        import os
        dbg = int(os.environ.get("KDBG", "9"))
        for l in range(L):
            if l == 0:
                ss_x_all()
            cs1 = cs2 = 0
            if bi == 0 and cfg.halo == 256:
                r = L - 1 - l
                cs1 = {0: 240, 1: 128, 2: 112}.get(r, 0)
                cs2 = {0: 256, 1: 240, 2: 128, 3: 112}.get(r, 0)
            ffn(bi, l, 1, 0, 1, True, cs1)
            mixer(bi, l)
            ffn(bi, l, 2, 4, 5, l < L - 1, cs2)
        P.dma(pool, lambda e, col0=col0: e.dma_start(out=yT_d[:, col0:col0 + TB].rearrange("(k p) c -> p k c", p=128), in_=xT[:]),
              sem_y, reads=R_x)
    P.dma(pool, lambda e: e.dma_start(out=ptp_d[:, :], in_=ptail[:].rearrange("p a b c -> p (a b c)")), sem_small, reads=R_ptail)
    P.dma(pool, lambda e: e.dma_start(out=ctp_d[:, :], in_=ctail[:].rearrange("p a b c -> p (a b c)")), sem_small, reads=R_ctail)
    P.dma(pool, lambda e: e.dma_start(out=pts_d[:, :], in_=pts[:].rearrange("p a b c -> p (a b c)")), sem_small, reads=[R_pts])
    P.dma(pool, lambda e: e.dma_start(out=cts_d[:, :], in_=cts[:].rearrange("p a b c -> p (a b c)")), sem_small, reads=[R_cts])
    pool.wait((sem_y[0], sem_y[1]))
    pool.wait((sem_small[0], sem_small[1]))
    for i in range(4):
        if sem_wb[i][1] > 0:
            sp.wait((sem_wb[i][0], sem_wb[i][1]))

    with nc.Block() as block:
        @block.tensor
        def _(e):
            for t in pe.thunks:
                t(e)

        @block.scalar
        def _(e):
            for t in act.thunks:
                t(e)

        @block.vector
        def _(e):
            for t in dve.thunks:
                t(e)

        @block.gpsimd
        def _(e):
            for t in pool.thunks:
                t(e)

        @block.sync
        def _(e):
            for t in sp.thunks:
                t(e)
    return nc


def _mmd(e, dv, q, banks, aT, JQ, cs=0):
    ins = None
    for jj in range(JQ):
        for mi in range(2):
            ins = e.matmul(banks[mi][0][:, cs:TB], dv[:, jj, mi * 128:(mi + 1) * 128], aT[:, q * JQ + jj, cs:TB],
                           start=(q == 0 and jj == 0), stop=(q == 1 and jj == JQ - 1))
    return ins


def _pl(a, nchunk):
    sh = a.shape[:-1]
    b = a.reshape(*sh, nchunk, 128)
    b = np.moveaxis(b, -1, 0)
    return np.ascontiguousarray(b)


def make_core_inputs(cfg, xp_ext, xs, state_pool, state_conv, W, is_seq_start):
    L, KC = cfg.L, cfg.KC
    x = np.concatenate([xp_ext, xs.reshape(-1, cfg.D)], axis=0)
    m = dict(W)
    m["xT"] = np.ascontiguousarray(x.T)
    invc = np.zeros((128, 4, 16), np.float32)
    for g in range(4):
        w = 2 << g
        for t in range(16):
            invc[:, g, t] = 1.0 / (min(t + 1, w) if is_seq_start else w)
    m["invc"] = invc.reshape(128, 64)
    ph = state_pool.reshape(L, cfg.nsamp, 15, 4, 128).transpose(4, 0, 1, 3, 2)
    m["phist"] = np.ascontiguousarray(ph).reshape(128, -1)
    ch = state_conv.reshape(L, cfg.nsamp, 2, 6, 128).transpose(4, 0, 1, 3, 2)
    m["chist"] = np.ascontiguousarray(ch).reshape(128, -1)
    return m


def make_shared_weights(cfg, w_in, w_out, pool_w, pool_scale, conv_w, sgu_w, sgu_b, f1g, f1u, f1d, f2g, f2u, f2d,
                        norm_gains, mix_gain):
    L, KC = cfg.L, cfg.KC
    W = {}
    W["w1g"], W["w1u"], W["w1d"] = f1g, f1u, f1d
    W["w2g"], W["w2u"], W["w2d"] = f2g, f2u, f2d
    W["w_in"], W["w_out"] = w_in, w_out
    W["gains"] = _pl(norm_gains, KC).reshape(128, -1)
    W["poolw"] = pool_w
    W["pscale"] = _pl(pool_scale, 4).reshape(128, -1)
    W["convw"] = _pl(conv_w, 6).reshape(128, -1)
    W["mixg"] = _pl(mix_gain, MIXC).reshape(128, -1)
    wsT = np.ascontiguousarray(sgu_w.transpose(3, 0, 1, 2))
    W["sguwT"] = wsT.reshape(128, -1)
    ts = cfg.tsamp
    bd = np.zeros_like(wsT)
    for r in range(128 // ts):
        bd[r * ts:(r + 1) * ts, :, :, r * ts:(r + 1) * ts] = wsT[0:ts, :, :, 0:ts]
    W["sguwbd"] = bd.reshape(128, -1)
    W["sgub"] = np.ascontiguousarray(np.broadcast_to(sgu_b.reshape(1, -1), (128, L * 6 * 128))).astype(np.float32)
    t = np.arange(128)
    mask = (t[:, None] <= t[None, :]).astype(np.float32)
    W["mask"] = mask
    mbd = np.zeros((128, 128), np.float32)
    for r in range(128 // ts):
        mbd[r * ts:(r + 1) * ts, r * ts:(r + 1) * ts] = mask[0:ts, 0:ts]
    W["maskbd"] = mbd
    return {k: np.ascontiguousarray(v, dtype=np.float32) for k, v in W.items()}


_NC_CACHE = {}


def get_program(cfg, key):
    if key not in _NC_CACHE:
        _NC_CACHE[key] = build_program(cfg)
    return _NC_CACHE[key]


def kernel(x_prompt, x_sample, state_pool, state_conv, w_in, w_out, pool_w, pool_scale, conv_w, sgu_w, sgu_b,
           ffn1_gate, ffn1_up, ffn1_down, ffn2_gate, ffn2_up, ffn2_down, norm_gains, mix_gain):
    cfg = Cfg()
    f32 = lambda a: np.asarray(a, dtype=np.float32)
    x_prompt, x_sample, state_pool, state_conv = map(f32, (x_prompt, x_sample, state_pool, state_conv))
    W = make_shared_weights(cfg, f32(w_in), f32(w_out), f32(pool_w), f32(pool_scale), f32(conv_w), f32(sgu_w), f32(sgu_b),
                            f32(ffn1_gate), f32(ffn1_up), f32(ffn1_down), f32(ffn2_gate), f32(ffn2_up), f32(ffn2_down),
                            f32(norm_gains), f32(mix_gain))
    B, S, D = x_prompt.shape
    L = cfg.L
    NSEG = 4
    SEG = S // NSEG
    H = cfg.halo
    in_maps = []
    for c in range(8):
        b, s = c // NSEG, c % NSEG
        if s == 0:
            ext = np.concatenate([np.zeros((H, D), np.float32), x_prompt[b, 0:SEG]], axis=0)
        else:
            ext = x_prompt[b, s * SEG - H:(s + 1) * SEG]
        in_maps.append(make_core_inputs(cfg, ext, x_sample[4 * c:4 * c + 4], state_pool[:, 4 * c:4 * c + 4],
                                        state_conv[:, 4 * c:4 * c + 4], W, s == 0))
    nc = get_program(cfg, "full")
    res = run_bass_kernel_spmd(nc, in_maps, core_ids=list(range(8)))
    R = res.results
    y_prompt = np.zeros((B, S, D), np.float32)
    y_sample = np.zeros(x_sample.shape, np.float32)
    npp = np.zeros((L, B, 15, POOLW), np.float32)
    ncp = np.zeros((L, B, 2, CONVC), np.float32)
    nps = np.zeros((L, 32, 15, POOLW), np.float32)
    ncs = np.zeros((L, 32, 2, CONVC), np.float32)
    nvs = np.zeros((L, 32, 64, SGUW), np.float32)
    for c in range(8):
        b, s = c // NSEG, c % NSEG
        yT = R[c]["yT"]
        y = yT.T
        y_prompt[b, s * SEG:(s + 1) * SEG] = y[H:H + SEG]
        y_sample[4 * c:4 * c + 4] = y[H + SEG:].reshape(4, 64, D)
        if s == NSEG - 1:
            pt = R[c]["pool_tail_p"].reshape(128, L, 4, 15)
            npp[:, b] = pt.transpose(1, 3, 2, 0).reshape(L, 15, POOLW)
            ctl = R[c]["conv_tail_p"].reshape(128, L, 6, 2)
            ncp[:, b] = ctl.transpose(1, 3, 2, 0).reshape(L, 2, CONVC)
        ps_ = R[c]["pool_tail_s"].reshape(128, L, 4, 4, 15)
        nps[:, 4 * c:4 * c + 4] = ps_.transpose(1, 2, 4, 3, 0).reshape(L, 4, 15, POOLW)
        cs_ = R[c]["conv_tail_s"].reshape(128, L, 4, 6, 2)
        ncs[:, 4 * c:4 * c + 4] = cs_.transpose(1, 2, 4, 3, 0).reshape(L, 4, 2, CONVC)
        nvs[:, 4 * c:4 * c + 4] = R[c]["v_s"].reshape(L, 4, 64, SGUW)
    return (y_prompt, y_sample, npp, ncp, nps, ncs, nvs)
```

```python
import numpy as np
import concourse.bass as bass
import concourse.mybir as mybir
from concourse.bass_utils import run_bass_kernel_spmd

F32 = mybir.dt.float32
BF16 = mybir.dt.bfloat16
AF = mybir.ActivationFunctionType
ALU = mybir.AluOpType

EPS = 1e-6
POOLW = 512
CONVC = 768
SGUW = 768
MIXW = 2048
INW = 4352
MIXC = MIXW // 128
TB = 512
SEM_LIMIT = 30000


class Cfg:
    def __init__(self, D=2048, DFF=5632, L=4, blocks=None, halo=256, nsamp=4, tsamp=64):
        self.D, self.DFF, self.L = D, DFF, L
        self.KC = D // 128
        self.JC = DFF // 128
        self.JH = self.JC // 2
        self.JQ = self.JH // 2
        assert self.JQ * 4 == self.JC
        self.blocks = blocks if blocks is not None else [512, 512, 512, 512, 256]
        self.halo = halo
        self.nsamp, self.tsamp = nsamp, tsamp
        self.NTOK = TB * len(self.blocks)
        self.NEXT = sum(self.blocks)
        self.NS = nsamp * tsamp
        assert self.NTOK == self.NEXT + self.NS


class Region:
    __slots__ = ("w", "r")

    def __init__(self):
        self.w = None
        self.r = []


class EngCtx:
    def __init__(self, prog, name, is_pe=False):
        self.prog = prog
        self.name = name
        self.thunks = []
        self.sem = None
        self.count = 0
        self.waited = {}
        self.own = set()
        self.is_pe = is_pe
        self.nsem = 0

    def wait(self, tok):
        sem, val = tok
        if self.is_pe and sem in self.own:
            return
        if self.waited.get(sem, 0) >= val:
            return
        self.waited[sem] = val
        self.thunks.append(lambda e, sem=sem, val=val: e.wait_ge(sem, val))

    def emit(self, fn):
        if self.sem is None or self.count >= SEM_LIMIT:
            self.sem = self.prog.nc.alloc_semaphore(f"p_{self.name}_{self.nsem}")
            self.nsem += 1
            self.own.add(self.sem)
            self.count = 0
        self.count += 1
        sem, val = self.sem, self.count
        self.thunks.append(lambda e, fn=fn, sem=sem: fn(e).then_inc(sem, 1))
        return (sem, val)

    def emit_dma(self, fn, sem_state):
        sem_state[1] += 16
        sem = sem_state[0]
        self.thunks.append(lambda e, fn=fn, sem=sem: fn(e).then_inc(sem, 16))
        return (sem, sem_state[1])


class Prog:
    def __init__(self, cfg):
        self.cfg = cfg
        self.nc = bass.Bass("TRN2", target_bir_lowering=False)
        self.pe = EngCtx(self, "pe", is_pe=True)
        self.act = EngCtx(self, "act")
        self.dve = EngCtx(self, "dve")
        self.pool = EngCtx(self, "pool")
        self.sp = EngCtx(self, "sp")
        self.nbank = 0

    def op(self, eng, fn, reads=(), writes=()):
        deps = []
        for R in reads:
            if R.w is not None:
                deps.append(R.w)
        for W in writes:
            if W.w is not None:
                deps.append(W.w)
            deps.extend(W.r)
        for t in deps:
            eng.wait(t)
        tok = eng.emit(fn)
        for R in reads:
            R.r.append(tok)
        for W in writes:
            W.w = tok
            W.r = []
        return tok

    def dma(self, eng, fn, sem_state, reads=(), writes=()):
        deps = []
        for R in reads:
            if R.w is not None:
                deps.append(R.w)
        for W in writes:
            if W.w is not None:
                deps.append(W.w)
            deps.extend(W.r)
        for t in deps:
            eng.wait(t)
        tok = eng.emit_dma(fn, sem_state)
        for R in reads:
            R.r.append(tok)
        for W in writes:
            W.w = tok
            W.r = []
        return tok

    def new_dma_sem(self, name):
        return [self.nc.alloc_semaphore(name), 0]


def build_program(cfg):
    P = Prog(cfg)
    nc = P.nc
    D, DFF, L, KC, JC, JH, JQ = cfg.D, cfg.DFF, cfg.L, cfg.KC, cfg.JC, cfg.JH, cfg.JQ
    NB = len(cfg.blocks)
    NTOK = cfg.NTOK
    NSAMP, TS = cfg.nsamp, cfg.tsamp
    NS = cfg.NS

    def din(name, shape, dt=F32):
        return nc.dram_tensor(name, list(shape), dt, kind="ExternalInput").ap()

    def dout(name, shape, dt=F32):
        return nc.dram_tensor(name, list(shape), dt, kind="ExternalOutput").ap()

    xT_d = din("xT", [D, NTOK])
    w_ffn = {}
    for f in (1, 2):
        w_ffn[f] = (din(f"w{f}g", [L, DFF // 256, 128, KC * 256]), din(f"w{f}u", [L, DFF // 256, 128, KC * 256]),
                    din(f"w{f}d", [L, 2, D // 256, 2, 128, JQ * 256]))
    win_d = din("w_in", [L, INW // 256, 128, KC * 256])
    wout_d = din("w_out", [L, D // 256, 128, MIXC * 256])
    gains_d = din("gains", [128, L * 6 * KC])
    poolw_d = din("poolw", [L, 4, 128, 128])
    pscale_d = din("pscale", [128, L * 4])
    convw_d = din("convw", [128, L * 3 * 6])
    mixg_d = din("mixg", [128, L * MIXC])
    sguw_d = din("sguwT", [128, L * 6 * 128])
    sguwbd_d = din("sguwbd", [128, L * 6 * 128])
    sgub_d = din("sgub", [128, L * 6 * 128])
    mask_d = din("mask", [128, 128])
    maskbd_d = din("maskbd", [128, 128])
    invc_d = din("invc", [128, 4 * 16])
    phist_d = din("phist", [128, L * NSAMP * 4 * 15])
    chist_d = din("chist", [128, L * NSAMP * 6 * 2])

    yT_d = dout("yT", [D, NTOK])
    ptp_d = dout("pool_tail_p", [128, L * 4 * 15])
    ctp_d = dout("conv_tail_p", [128, L * 6 * 2])
    pts_d = dout("pool_tail_s", [128, L * NSAMP * 4 * 15])
    cts_d = dout("conv_tail_s", [128, L * NSAMP * 6 * 2])
    vs_d = dout("v_s", [L, NS, SGUW])

    def tiles_ffn(f, l):
        g, u, d = w_ffn[f]
        out = []
        for hf in range(2):
            gu = []
            for jg in range(JH // 2):
                jt = hf * (JH // 2) + jg
                gu.append((g[l, jt].rearrange("p (k c) -> p k c", c=256), KC, 256))
                gu.append((u[l, jt].rearrange("p (k c) -> p k c", c=256), KC, 256))
            dn = []
            for mg in range(D // 256):
                for q in range(2):
                    dn.append((d[l, hf, mg, q].rearrange("p (j c) -> p j c", c=256), JQ, 256))
            out.append((gu, dn))
        return out

    def tiles_in(l):
        return [(win_d[l, ct].rearrange("p (k c) -> p k c", c=256), KC, 256) for ct in range(INW // 256)]

    def tiles_out(l):
        return [(wout_d[l, ct].rearrange("p (k c) -> p k c", c=256), MIXC, 256) for ct in range(D // 256)]

    SLOT_ELEMS = max(KC, MIXC, JQ) * 256
    n_tiles_layer = 2 * (2 * (JH + (D // 256) * 2)) + INW // 256 + D // 256
    n_tiles = n_tiles_layer * L
    scratch_l = [nc.dram_tensor(f"wscratch{l}", [n_tiles_layer, 128, SLOT_ELEMS], BF16, kind="Internal").ap() for l in range(L)]
    scratch_regions = [Region() for _ in range(n_tiles)]

    def sb(name, shape, dt):
        return nc.alloc_sbuf_tensor(name, list(shape), dt)

    xT = sb("xT_sb", [128, KC, TB], F32)
    xn = sb("xn_sb", [128, KC, TB], BF16)
    NA = max(JH, 20)
    aT = sb("aT_sb", [128, NA, TB], BF16)
    ymix = aT
    dT = aT[:, 16:20, :]
    NF = max(KC, 16)
    fT = sb("fT_sb", [128, NF, TB], F32)
    pT = fT[:, 0:4, :]
    cau = fT[:, 4:10, :]
    fflat = fT[:].rearrange("p a b -> p (a b)")
    pext1 = fflat[:, 10 * TB:10 * TB + 15 + TB]
    ZW = 2 + TB + 2 * NSAMP + 8
    zbv = [fflat[:, 12 * TB:12 * TB + ZW], fflat[:, 14 * TB:14 * TB + ZW]]
    ybuf = sb("ybuf_sb", [128, 6, TB], F32)
    sg = sb("sg_sb", [128, 3, TB + 16], F32)
    sq = sb("sq_sb", [128, 4, TB], BF16)
    rstd = sb("rstd_sb", [128, 2, TB], F32)
    v_bf = sb("vbf_sb", [128, 4, SGUW], BF16)
    v_f32 = sb("vf32_sb", [128, 2, SGUW], F32)
    slots = [sb(f"slot{i}", [128, SLOT_ELEMS], BF16) for i in range(4)]
    gains = sb("gains_sb", [128, L * 6 * KC], F32)
    poolw = sb("poolw_sb", [128, L * 4, 128], BF16)
    pscale = sb("pscale_sb", [128, L * 4], F32)
    convw = sb("convw_sb", [128, L * 3 * 6], F32)
    mixg = sb("mixg_sb", [128, L * MIXC], F32)
    wsT = sb("wsT_sb", [128, L * 6, 128], BF16)
    wsTbd = sb("wsTbd_sb", [128, L * 6, 128], BF16)
    sgub = sb("sgub_sb", [128, 6, 128], F32)
    maskt = sb("mask_sb", [128, 2, 128], F32)
    invc = sb("invc_sb", [128, 4, 16], F32)
    ones = sb("ones_sb", [128, 128], BF16)
    epsb = sb("epsb_sb", [128, 4], F32)
    phist = sb("phist_sb", [128, L * NSAMP, 4, 15], F32)
    chist = sb("chist_sb", [128, L * NSAMP, 6, 2], F32)
    ptail = sb("ptail_sb", [128, L, 4, 15], F32)
    ctail = sb("ctail_sb", [128, L, 6, 2], F32)
    pts = sb("pts_sb", [128, L * NSAMP, 4, 15], F32)
    cts = sb("cts_sb", [128, L * NSAMP, 6, 2], F32)

    psb = [nc.alloc_psum_tensor(f"ps{i}", [128, TB], F32) for i in range(8)]
    bank_r = [Region() for _ in range(8)]

    def next_bank():
        i = P.nbank % 7
        P.nbank += 1
        return psb[i], bank_r[i]

    ss_bank, ss_r = psb[7], bank_r[7]

    R_x = [Region() for _ in range(KC)]
    R_xn = [Region() for _ in range(KC)]
    R_a = [Region() for _ in range(NA)]
    R_ym = R_a
    R_dT = R_a[16:20]
    R_f = [Region() for _ in range(NF)]
    R_pg = R_f[0:4]
    R_cau = R_f[4:10]
    R_pext = R_f[10:12]
    R_zb = [R_f[12:14], R_f[14:16]]
    R_y = [Region() for _ in range(6)]
    R_sq = [Region() for _ in range(4)]
    R_sg = [Region() for _ in range(3)]
    R_rstd = [Region(), Region()]
    R_vbf = [Region() for _ in range(4)]
    R_vf = [Region(), Region()]
    R_slot = [Region() for _ in range(4)]
    R_const = Region()
    R_sgub = Region()
    R_ptail = [Region() for _ in range(L)]
    R_ctail = [Region() for _ in range(L)]
    R_pts = Region()
    R_cts = Region()
    R_tmpc = Region()

    ctr = {"sq": 0, "rstd": 0, "sg": 0, "zb": 0, "slot": 0}

    sem_x = P.new_dma_sem("sem_x")
    sem_y = P.new_dma_sem("sem_y")
    sem_small = P.new_dma_sem("sem_small")
    sem_const = P.new_dma_sem("sem_const")
    sem_slot = [P.new_dma_sem(f"sem_slot{i}") for i in range(4)]
    sem_wb = [P.new_dma_sem(f"sem_wb{i}") for i in range(4)]

    pe, act, dve, pool, sp = P.pe, P.act, P.dve, P.pool, P.sp

    def cload(dst_ap, src_ap, eng=sp):
        P.dma(eng, lambda e, d=dst_ap, s=src_ap: e.dma_start(out=d, in_=s), sem_const, writes=[R_const])

    cload(gains[:], gains_d[:, :])
    cload(pscale[:], pscale_d[:, :])
    cload(convw[:], convw_d[:, :])
    cload(mixg[:], mixg_d[:, :])
    cload(maskt[:, 0, :], mask_d[:, :])
    cload(maskt[:, 1, :], maskbd_d[:, :])
    cload(invc[:].rearrange("p a b -> p (a b)"), invc_d[:, :])
    cload(phist[:].rearrange("p a b c -> p (a b c)"), phist_d[:, :])
    cload(chist[:].rearrange("p a b c -> p (a b c)"), chist_d[:, :])
    cload(poolw[:], poolw_d.rearrange("l g c d -> c (l g) d"), eng=pool)
    assert L * 6 * 128 <= KC * TB or True
    tmpw = fT[:].rearrange("p a b -> p (a b)")
    n_w = L * 6 * 128
    if n_w <= KC * TB:
        for (src, dst, mi) in ((sguw_d, wsT, 0), (sguwbd_d, wsTbd, 1)):
            P.dma(sp, lambda e, s=src: e.dma_start(out=tmpw[:, 0:n_w], in_=s[:, :]), sem_const, writes=[R_tmpc])
            for lh in range(L * 6):
                P.op(dve, lambda e, lh=lh, dst=dst, mi=mi: e.tensor_tensor(
                    out=dst[:, lh, :], in0=tmpw[:, lh * 128:(lh + 1) * 128], in1=maskt[:, mi, :], op=ALU.mult),
                    reads=[R_tmpc, R_const], writes=[R_const])
            R_tmpc.r = [R_const.w]
    else:
        for (src, dst, mi) in ((sguw_d, wsT, 0), (sguwbd_d, wsTbd, 1)):
            for lh in range(L * 6):
                P.dma(sp, lambda e, s=src, lh=lh: e.dma_start(out=tmpw[:, 0:128], in_=s[:, lh * 128:(lh + 1) * 128]),
                      sem_const, writes=[R_tmpc])
                P.op(dve, lambda e, lh=lh, dst=dst, mi=mi: e.tensor_tensor(
                    out=dst[:, lh, :], in0=tmpw[:, 0:128], in1=maskt[:, mi, :], op=ALU.mult),
                    reads=[R_tmpc, R_const], writes=[R_tmpc])
        R_const.w = R_tmpc.w
    P.op(dve, lambda e: e.memset(ones[:], 1.0), writes=[R_const])
    for wi, wd in enumerate((D, POOLW, CONVC)):
        P.op(dve, lambda e, wi=wi, wd=wd: e.memset(epsb[:, wi:wi + 1], float(wd * EPS)), writes=[R_const])
    for l in range(L):
        P.op(dve, lambda e, l=l: e.memset(ptail[:, l], 0.0), writes=[R_ptail[l]])
        P.op(dve, lambda e, l=l: e.memset(ctail[:, l], 0.0), writes=[R_ctail[l]])
    sD = float(np.sqrt(D))
    for l in range(L):
        for s in range(6):
            sc = sD * (0.5 if s in (1, 5) else 1.0)
            o = (l * 6 + s) * KC
            P.op(dve, lambda e, o=o, sc=sc: e.tensor_scalar(
                out=gains[:, o:o + KC], in0=gains[:, o:o + KC], scalar1=sc, scalar2=None, op0=ALU.mult),
                reads=[R_const], writes=[R_const])
        o = l * MIXC
        P.op(dve, lambda e, o=o: e.tensor_scalar(
            out=mixg[:, o:o + 4], in0=mixg[:, o:o + 4], scalar1=float(np.sqrt(POOLW)), scalar2=None, op0=ALU.mult),
            reads=[R_const], writes=[R_const])
        P.op(dve, lambda e, o=o: e.tensor_scalar(
            out=mixg[:, o + 4:o + 16], in0=mixg[:, o + 4:o + 16], scalar1=float(np.sqrt(CONVC)), scalar2=None, op0=ALU.mult),
            reads=[R_const], writes=[R_const])

    def gcol(l, s, c):
        o = (l * 6 + s) * KC + c
        return gains[:, o:o + 1]

    tile_counter = {"i": 0}

    def fetch_tile(bi, tdesc, l):
        view, a, b = tdesc
        ti = tile_counter["i"]
        tile_counter["i"] += 1
        tg = ti
        si = ctr["slot"] % 4
        ctr["slot"] += 1
        slot = slots[si]
        sv = slot[:, 0:a * b].rearrange("p (a b) -> p a b", b=b)
        do_cast = (bi == 0) or (bi == 1 and l >= L // 2)
        do_wb = (bi == 0 and l < L // 2) or (bi == 1 and l >= L // 2)
        if do_cast:
            P.dma(pool, lambda e, sv=sv, view=view: e.dma_start(out=sv, in_=view), sem_slot[si], writes=[R_slot[si]])
            if do_wb:
                P.dma(sp, lambda e, tg=tg, slot=slot, n=a * b: e.dma_start(out=scratch_l[tg // n_tiles_layer][tg % n_tiles_layer, :, 0:n], in_=slot[:, 0:n]),
                      sem_wb[si], reads=[R_slot[si]], writes=[scratch_regions[tg]])
        else:
            P.dma(sp, lambda e, tg=tg, slot=slot, n=a * b: e.dma_start(out=slot[:, 0:n], in_=scratch_l[tg // n_tiles_layer][tg % n_tiles_layer, :, 0:n]),
                  sem_slot[si], reads=[scratch_regions[tg]], writes=[R_slot[si]])
        return sv, R_slot[si]

    pending = []
    sched = {"dense": True}

    def flush_ss(keep=0):
        while len(pending) > keep:
            pending.pop(0)()

    def pe_op(fn, reads=(), writes=()):
        tok = P.op(pe, fn, reads=reads, writes=writes)
        flush_ss()
        return tok

    def ss_add(src, c, reg, first, last, cs=0):
        flush_ss(keep=2)
        i = ctr["sq"] % 4
        ctr["sq"] += 1
        P.op(act, lambda e, i=i: e.activation(out=sq[:, i, cs:TB], in_=src[:, c, cs:TB], func=AF.Square),
             reads=[reg], writes=[R_sq[i]])
        pending.append(lambda i=i, first=first, last=last: P.op(
            pe, lambda e: e.matmul(ss_bank[:, cs:TB], ones[:, :], sq[:, i, cs:TB], start=first, stop=last),
            reads=[R_sq[i], R_const], writes=[ss_r]))
        if not sched["dense"]:
            flush_ss()

    def ss_finish(width):
        flush_ss()
        ri = ctr["rstd"] % 2
        ctr["rstd"] += 1
        wi = 0 if width == D else {POOLW: 1, CONVC: 2}[width]
        P.op(act, lambda e, ri=ri, wi=wi: e.activation(
            out=rstd[:, ri, :], in_=ss_bank[:, :], func=AF.Ln, bias=epsb[:, wi:wi + 1], scale=1.0),
            reads=[R_const], writes=[R_rstd[ri], ss_r])
        P.op(act, lambda e, ri=ri: e.activation(out=rstd[:, ri, :], in_=rstd[:, ri, :], func=AF.Exp, scale=-0.5),
             reads=[R_rstd[ri]], writes=[R_rstd[ri]])
        return rstd[:, ri, :], R_rstd[ri]

    def ss_x_all():
        for c in range(KC):
            ss_add(xT, c, R_x[c], c == 0, c == KC - 1)

    def prenorm(l, s, cs=0):
        rs, rr = ss_finish(D)
        for c in range(KC):
            P.op(dve, lambda e, c=c, rs=rs: e.scalar_tensor_tensor(
                out=xn[:, c, cs:TB], in0=xT[:, c, cs:TB], scalar=gcol(l, s, c), in1=rs[:, cs:TB], op0=ALU.mult, op1=ALU.mult),
                reads=[R_x[c], rr, R_const], writes=[R_xn[c]])

    def postnorm_add(l, s, want_next, cs=0):
        rs, rr = ss_finish(D)
        prev = None

        def fin(pc, pui):
            P.op(dve, lambda e, c=pc, ui=pui: e.tensor_tensor(out=xT[:, c, cs:TB], in0=xT[:, c, cs:TB], in1=sg[:, ui, cs:TB], op=ALU.add),
                 reads=[R_sg[pui]], writes=[R_x[pc]])
            if want_next:
                ss_add(xT, pc, R_x[pc], pc == 0, pc == KC - 1, cs)
        for c in range(KC):
            ui = ctr["sg"] % 3
            ctr["sg"] += 1
            P.op(dve, lambda e, c=c, rs=rs, ui=ui: e.scalar_tensor_tensor(
                out=sg[:, ui, cs:TB], in0=fT[:, c, cs:TB], scalar=gcol(l, s, c), in1=rs[:, cs:TB], op0=ALU.mult, op1=ALU.mult),
                reads=[R_f[c], rr, R_const], writes=[R_sg[ui]])
            if prev is not None:
                fin(*prev)
            prev = (c, ui)
        fin(*prev)

    def ffn(bi, l, f, s_pre, s_post, want_next, cs=0):
        prenorm(l, s_pre, cs)
        tl = tiles_ffn(f, l)
        for hf in range(2):
            gu, dn = tl[hf]
            for jg in range(JH // 2):
                gv, gr = fetch_tile(bi, gu[2 * jg], l)
                uv, ur = fetch_tile(bi, gu[2 * jg + 1], l)
                for ci in range(2):
                    jl = jg * 2 + ci
                    bg_, bgr = next_bank()
                    bu_, bur = next_bank()

                    def mmg(e, wv=gv, bank=bg_, ci=ci):
                        ins = None
                        for k in range(KC):
                            ins = e.matmul(bank[:, cs:TB], wv[:, k, ci * 128:(ci + 1) * 128], xn[:, k, cs:TB], start=(k == 0), stop=(k == KC - 1))
                        return ins
                    pe_op(mmg, reads=[gr] + R_xn[:KC], writes=[bgr])
                    pe_op(lambda e, wv=uv, bank=bu_, ci=ci: mmg(e, wv, bank, ci), reads=[ur] + R_xn[:KC], writes=[bur])
                    si = ctr["sg"] % 3
                    ctr["sg"] += 1
                    P.op(act, lambda e, si=si, bank=bg_: e.activation(out=sg[:, si, cs:TB], in_=bank[:, cs:TB], func=AF.Silu),
                         reads=[bgr], writes=[R_sg[si]])
                    P.op(dve, lambda e, si=si, bank=bu_, jl=jl: e.tensor_tensor(out=aT[:, jl, cs:TB], in0=sg[:, si, cs:TB], in1=bank[:, cs:TB], op=ALU.mult),
                         reads=[R_sg[si], bur], writes=[R_a[jl]])
            ti = 0
            for mg in range(D // 256):
                b0, b0r = next_bank()
                b1, b1r = next_bank()
                banks = ((b0, b0r), (b1, b1r))
                for q in range(2):
                    dv, dr = fetch_tile(bi, dn[ti], l)
                    ti += 1
                    pe_op(lambda e, dv=dv, q=q, banks_l=banks: _mmd(e, dv, q, banks_l, aT, JQ, cs),
                          reads=[dr] + R_a[q * JQ:(q + 1) * JQ], writes=[b0r, b1r])
                for mi in range(2):
                    m = mg * 2 + mi
                    bank, br = banks[mi]
                    if hf == 0:
                        P.op(act, lambda e, m=m, bank=bank: e.activation(out=fT[:, m, cs:TB], in_=bank[:, cs:TB], func=AF.Copy),
                             reads=[br], writes=[R_f[m]])
                    else:
                        P.op(dve, lambda e, m=m, bank=bank: e.tensor_tensor(out=fT[:, m, cs:TB], in0=bank[:, cs:TB], in1=fT[:, m, cs:TB], op=ALU.add),
                             reads=[br, R_f[m]], writes=[R_f[m]])
                        ss_add(fT, m, R_f[m], m == 0, m == KC - 1, cs)
        postnorm_add(l, s_post, want_next, cs)

    def group_rms(l, y_chunks, c0, width):
        n = len(y_chunks)
        rs, rr = ss_finish(width)
        for c in range(n):
            o = l * MIXC + c0 + c
            P.op(dve, lambda e, c=c, rs=rs, o=o: e.scalar_tensor_tensor(
                out=ymix[:, c0 + c, :], in0=ybuf[:, y_chunks[c], :], scalar=mixg[:, o:o + 1], in1=rs, op0=ALU.mult, op1=ALU.mult),
                reads=[R_y[y_chunks[c]], rr, R_const], writes=[R_ym[c0 + c]])

    def mixer(bi, l):
        np_ = cfg.blocks[bi]
        has_s = np_ < TB
        last_prompt = (sum(cfg.blocks[:bi + 1]) == cfg.NEXT)
        P.dma(sp, lambda e: e.dma_start(out=sgub[:].rearrange("p a b -> p (a b)"), in_=sgub_d[:, l * 768:(l + 1) * 768]),
              sem_const, writes=[R_sgub])
        prenorm(l, 2)
        tin = tiles_in(l)
        order = [0, 1, 8, 9, 10, 2, 3, 4, 5, 6, 7, 14, 15, 16, 11, 12, 13]

        def fm_group(wv, ci):
            bank, br = next_bank()

            def mm(e, wv=wv, bank=bank, ci=ci):
                ins = None
                for k in range(KC):
                    ins = e.matmul(bank[:, :], wv[:, k, ci * 128:(ci + 1) * 128], xn[:, k, :], start=(k == 0), stop=(k == KC - 1))
                return ins
            return bank, br, mm

        def cw(k, c):
            o = (l * 3 + k) * 6 + c
            return convw[:, o:o + 1]

        def sgu_pe_bias():
            for h in range(6):
                bank, br = next_bank()
                for tt in range(4):
                    is_s = has_s and tt * 128 >= np_
                    wmat = wsTbd if is_s else wsT
                    pe_op(lambda e, bank=bank, tt=tt, h=h, wmat=wmat: e.matmul(
                        bank[:, tt * 128:(tt + 1) * 128], v_bf[:, tt, h * 128:(h + 1) * 128], wmat[:, l * 6 + h, :], start=True, stop=True),
                        reads=[R_vbf[tt], R_const], writes=[br])
                for tt in range(4):
                    is_s = has_s and tt * 128 >= np_
                    if not is_s:
                        P.op(dve, lambda e, bank=bank, tt=tt, h=h: e.tensor_tensor(
                            out=ybuf[:, h, tt * 128:(tt + 1) * 128], in0=bank[:, tt * 128:(tt + 1) * 128], in1=sgub[:, h, :], op=ALU.add),
                            reads=[br, R_sgub], writes=[R_y[h]])
                    else:
                        for hh in range(128 // TS):
                            c0 = tt * 128 + hh * TS
                            P.op(dve, lambda e, bank=bank, c0=c0, h=h: e.tensor_tensor(
                                out=ybuf[:, h, c0:c0 + TS], in0=bank[:, c0:c0 + TS], in1=sgub[:, h, 0:TS], op=ALU.add),
                                reads=[br, R_sgub], writes=[R_y[h]])

        for ct in order:
            wv, wr = fetch_tile(bi, tin[ct], l)
            if ct <= 13:
                for ci in range(2):
                    bank, br, mm = fm_group(wv, ci)
                    pe_op(mm, reads=[wr] + R_xn[:KC], writes=[br])
                    if ct <= 1:
                        g = ct * 2 + ci
                        P.op(act, lambda e, g=g, bank=bank: e.activation(out=pT[:, g, :], in_=bank[:, :], func=AF.Copy),
                             reads=[br], writes=[R_pg[g]])
                    elif 8 <= ct <= 10:
                        c = (ct - 8) * 2 + ci
                        P.op(act, lambda e, c=c, bank=bank: e.activation(out=cau[:, c, :], in_=bank[:, :], func=AF.Copy),
                             reads=[br], writes=[R_cau[c]])
                    elif 2 <= ct <= 4:
                        c = (ct - 2) * 2 + ci
                        conv_chunk(l, c, bank, br, np_, has_s, last_prompt, cw)
                    elif 5 <= ct <= 7:
                        c = (ct - 5) * 2 + ci
                        P.op(dve, lambda e, c=c, bank=bank: e.tensor_tensor(out=ybuf[:, c, :], in0=bank[:, :], in1=cau[:, c, :], op=ALU.mult),
                             reads=[br, R_cau[c]], writes=[R_y[c]])
                        ss_add(ybuf, c, R_y[c], c == 0, c == 5)
                    else:
                        c = (ct - 11) * 2 + ci
                        P.op(act, lambda e, c=c, bank=bank: e.activation(out=cau[:, c, :], in_=bank[:, :], func=AF.Copy),
                             reads=[br], writes=[R_cau[c]])
                        P.op(dve, lambda e, h=c: e.tensor_tensor(out=ybuf[:, h, :], in0=ybuf[:, h, :], in1=cau[:, h, :], op=ALU.mult),
                             reads=[R_cau[c]], writes=[R_y[c]])
                        ss_add(ybuf, c, R_y[c], c == 0, c == 5)
                if ct == 1:
                    pool_dve(bi, l, np_, has_s, last_prompt)
                    if not sched["dense"]:
                        pool_pe(l)
                elif ct == 10 and sched["dense"]:
                    pool_pe(l)
                elif ct == 2:
                    group_rms(l, list(range(4)), 0, POOLW)
            else:
                cv = (ct - 14) * 256
                for tt in range(4):
                    bank, br = next_bank()

                    def mmv(e, wv=wv, bank=bank, tt=tt):
                        ins = None
                        for k in range(KC):
                            ins = e.matmul(bank[:, 0:256], xn[:, k, tt * 128:(tt + 1) * 128], wv[:, k, :], start=(k == 0), stop=(k == KC - 1))
                        return ins
                    pe_op(mmv, reads=[wr] + R_xn[:KC], writes=[br])
                    P.op(act, lambda e, bank=bank, tt=tt, cv=cv: e.activation(out=v_bf[:, tt, cv:cv + 256], in_=bank[:, 0:256], func=AF.Copy),
                         writes=[R_vbf[tt], br])
                    if has_s and tt * 128 >= np_:
                        ts_ = (tt * 128 - np_) // 128
                        P.op(dve, lambda e, bank=bank, ts_=ts_, cv=cv: e.tensor_copy(out=v_f32[:, ts_, cv:cv + 256], in_=bank[:, 0:256]),
                             writes=[R_vf[ts_], br])
                if ct == 14:
                    group_rms(l, list(range(6)), 4, CONVC)
                if ct == 16:
                    if has_s:
                        for ts_ in range((TB - np_) // 128):
                            P.dma(pool, lambda e, ts_=ts_: e.dma_start(out=vs_d[l, ts_ * 128:(ts_ + 1) * 128, :], in_=v_f32[:, ts_, :]),
                                  sem_small, reads=[R_vf[ts_]])
                    sgu_pe_bias()
        group_rms(l, list(range(6)), 10, SGUW)
        tout = tiles_out(l)
        for ct in range(D // 256):
            wv, wr = fetch_tile(bi, tout[ct], l)
            for ci in range(2):
                bank, br = next_bank()

                def mmo(e, wv=wv, bank=bank, ci=ci):
                    ins = None
                    for k in range(MIXC):
                        ins = e.matmul(bank[:, :], wv[:, k, ci * 128:(ci + 1) * 128], ymix[:, k, :], start=(k == 0), stop=(k == MIXC - 1))
                    return ins
                pe_op(mmo, reads=[wr] + R_ym[:MIXC], writes=[br])
                m = ct * 2 + ci
                P.op(act, lambda e, m=m, bank=bank: e.activation(out=fT[:, m, :], in_=bank[:, :], func=AF.Copy),
                     reads=[br], writes=[R_f[m]])
                ss_add(fT, m, R_f[m], m == 0, m == KC - 1)
        postnorm_add(l, 3, True)

    def conv_chunk(l, c, bank, br, np_, has_s, last_prompt, cw):
        zi = ctr["zb"] % 2
        ctr["zb"] += 1
        z = zbv[zi]
        rzl = R_zb[zi]
        P.op(dve, lambda e, z=z, c=c: e.tensor_copy(out=z[:, 0:2], in_=ctail[:, l, c, :]), reads=[R_ctail[l]], writes=rzl)
        P.op(dve, lambda e, z=z, c=c, bank=bank: e.tensor_tensor(out=z[:, 2:2 + np_], in0=bank[:, 0:np_], in1=cau[:, c, 0:np_], op=ALU.mult),
             reads=[br, R_cau[c]], writes=rzl)
        if has_s:
            zs = z[:, 2 + np_:2 + np_ + NSAMP * (TS + 2)].rearrange("p (s t) -> p s t", t=TS + 2)
            P.op(dve, lambda e, zs=zs, c=c: e.tensor_copy(out=zs[:, :, 0:2], in_=chist[:, l * NSAMP:(l + 1) * NSAMP, c, :]),
                 reads=[R_const], writes=rzl)
            P.op(dve, lambda e, zs=zs, c=c, bank=bank: e.tensor_tensor(
                out=zs[:, :, 2:2 + TS], in0=bank[:, np_:TB].rearrange("p (s t) -> p s t", t=TS),
                in1=cau[:, c, np_:TB].rearrange("p (s t) -> p s t", t=TS), op=ALU.mult),
                reads=[br, R_cau[c]], writes=rzl)
        acc = cau[:, c, 0:np_]
        P.op(dve, lambda e, z=z, acc=acc, c=c: e.tensor_scalar(out=acc, in0=z[:, 0:np_], scalar1=cw(0, c), scalar2=None, op0=ALU.mult),
             reads=rzl + [R_const], writes=[R_cau[c]])
        for k in (1, 2):
            P.op(dve, lambda e, z=z, acc=acc, c=c, k=k: e.scalar_tensor_tensor(
                out=acc, in0=z[:, k:k + np_], scalar=cw(k, c), in1=acc, op0=ALU.mult, op1=ALU.add),
                reads=rzl + [R_const], writes=[R_cau[c]])
        if has_s:
            accs = cau[:, c, np_:TB].rearrange("p (s t) -> p s t", t=TS)
            P.op(dve, lambda e, zs=zs, accs=accs, c=c: e.tensor_scalar(out=accs, in0=zs[:, :, 0:TS], scalar1=cw(0, c), scalar2=None, op0=ALU.mult),
                 reads=rzl + [R_const], writes=[R_cau[c]])
            for k in (1, 2):
                P.op(dve, lambda e, zs=zs, accs=accs, c=c, k=k: e.scalar_tensor_tensor(
                    out=accs, in0=zs[:, :, k:k + TS], scalar=cw(k, c), in1=accs, op0=ALU.mult, op1=ALU.add),
                    reads=rzl + [R_const], writes=[R_cau[c]])
            P.op(dve, lambda e, zs=zs, c=c: e.tensor_copy(out=cts[:, l * NSAMP:(l + 1) * NSAMP, c, :], in_=zs[:, :, TS:TS + 2]),
                 reads=rzl, writes=[R_cts])
        P.op(dve, lambda e, z=z, c=c: e.tensor_copy(out=ctail[:, l, c, :], in_=z[:, np_:np_ + 2]), reads=rzl, writes=[R_ctail[l]])

    def pool_segment(l, n, col0, hist_ap, hist_reg, tail_ap, tail_reg, fix_col):
        for g in range(4):
            w = 2 << g
            P.op(dve, lambda e, g=g: e.tensor_copy(out=pext1[:, 0:15], in_=hist_ap[:, g, :]), reads=[hist_reg], writes=R_pext)
            P.op(dve, lambda e, g=g: e.tensor_copy(out=pext1[:, 15:15 + n], in_=pT[:, g, col0:col0 + n]), reads=[R_pg[g]], writes=R_pext)
            P.op(dve, lambda e, g=g: e.tensor_copy(out=tail_ap[:, g, :], in_=pext1[:, n:n + 15]), reads=R_pext, writes=[tail_reg])
            cur = pext1
            cur_r = R_pext
            k = 1
            ti = 0
            while k < w:
                nxt = sg[:, ti, :]
                nr = [R_sg[ti]]
                lo = 2 * k - 1
                P.op(dve, lambda e, cur=cur, nxt=nxt, lo=lo, k=k: e.tensor_tensor(
                    out=nxt[:, lo:15 + n], in0=cur[:, lo:15 + n], in1=cur[:, lo - k:15 + n - k], op=ALU.add),
                    reads=cur_r, writes=nr)
                cur, cur_r = nxt, nr
                ti ^= 1
                k *= 2
            P.op(dve, lambda e, cur=cur, g=g, w=w: e.scalar_tensor_tensor(
                out=dT[:, g, col0:col0 + n], in0=cur[:, 15:15 + n], scalar=1.0 / w, in1=pT[:, g, col0:col0 + n], op0=ALU.mult, op1=ALU.subtract),
                reads=cur_r + [R_pg[g]], writes=[R_dT[g]])
            if fix_col is not None:
                fc = fix_col
                oth = sg[:, ti, :]
                orr = [R_sg[ti]]
                P.op(dve, lambda e, cur=cur, g=g, oth=oth, fc=fc: e.tensor_tensor(
                    out=oth[:, 0:16], in0=cur[:, 15 + fc:15 + fc + 16], in1=invc[:, g, :], op=ALU.mult),
                    reads=cur_r + [R_const], writes=orr)
                P.op(dve, lambda e, g=g, oth=oth, fc=fc: e.tensor_tensor(
                    out=dT[:, g, col0 + fc:col0 + fc + 16], in0=oth[:, 0:16], in1=pT[:, g, col0 + fc:col0 + fc + 16], op=ALU.subtract),
                    reads=orr + [R_pg[g]], writes=[R_dT[g]])

    def pool_dve(bi, l, np_, has_s, last_prompt):
        ext0 = sum(cfg.blocks[:bi])
        fix = None
        if ext0 <= cfg.halo < ext0 + np_:
            fix = cfg.halo - ext0
        pool_segment(l, np_, 0, ptail[:, l], R_ptail[l], ptail[:, l], R_ptail[l], fix)
        if has_s:
            for s_ in range(NSAMP):
                pool_segment(l, TS, np_ + s_ * TS, phist[:, l * NSAMP + s_], R_const, pts[:, l * NSAMP + s_], R_pts, None)

    def pool_pe(l):
        bks = [next_bank() for _ in range(4)]
        for g in range(4):
            bank, br = bks[g]
            pe_op(lambda e, bank=bank, g=g: e.matmul(bank[:, :], poolw[:, l * 4 + g, :], dT[:, g, :], start=True, stop=True),
                  reads=[R_dT[g], R_const], writes=[br])
        for g in range(4):
            bank, br = bks[g]
            P.op(dve, lambda e, bank=bank, g=g: e.tensor_scalar(
                out=ybuf[:, g, :], in0=bank[:, :], scalar1=pscale[:, l * 4 + g:l * 4 + g + 1], scalar2=None, op0=ALU.mult),
                reads=[br, R_const], writes=[R_y[g]])
            ss_add(ybuf, g, R_y[g], g == 0, g == 3)

    for bi in range(NB):
        col0 = bi * TB
        tile_counter["i"] = 0
        sched["dense"] = bi >= 2
        P.dma(pool, lambda e, col0=col0: e.dma_start(out=xT[:], in_=xT_d[:, col0:col0 + TB].rearrange("(k p) c -> p k c", p=128)),
              sem_x, writes=R_x)
        import os
        dbg = int(os.environ.get("KDBG", "9"))
        for l in range(L):
            if l == 0:
                ss_x_all()
            cs1 = cs2 = 0
            if bi == 0 and cfg.halo == 256:
                r = L - 1 - l
                cs1 = {0: 240, 1: 128, 2: 112}.get(r, 0)
                cs2 = {0: 256, 1: 240, 2: 128, 3: 112}.get(r, 0)
            ffn(bi, l, 1, 0, 1, True, cs1)
            mixer(bi, l)
            ffn(bi, l, 2, 4, 5, l < L - 1, cs2)
        P.dma(pool, lambda e, col0=col0: e.dma_start(out=yT_d[:, col0:col0 + TB].rearrange("(k p) c -> p k c", p=128), in_=xT[:]),
              sem_y, reads=R_x)
    P.dma(pool, lambda e: e.dma_start(out=ptp_d[:, :], in_=ptail[:].rearrange("p a b c -> p (a b c)")), sem_small, reads=R_ptail)
    P.dma(pool, lambda e: e.dma_start(out=ctp_d[:, :], in_=ctail[:].rearrange("p a b c -> p (a b c)")), sem_small, reads=R_ctail)
    P.dma(pool, lambda e: e.dma_start(out=pts_d[:, :], in_=pts[:].rearrange("p a b c -> p (a b c)")), sem_small, reads=[R_pts])
    P.dma(pool, lambda e: e.dma_start(out=cts_d[:, :], in_=cts[:].rearrange("p a b c -> p (a b c)")), sem_small, reads=[R_cts])
    pool.wait((sem_y[0], sem_y[1]))
    pool.wait((sem_small[0], sem_small[1]))
    for i in range(4):
        if sem_wb[i][1] > 0:
            sp.wait((sem_wb[i][0], sem_wb[i][1]))

    with nc.Block() as block:
        @block.tensor
        def _(e):
            for t in pe.thunks:
                t(e)

        @block.scalar
        def _(e):
            for t in act.thunks:
                t(e)

        @block.vector
        def _(e):
            for t in dve.thunks:
                t(e)

        @block.gpsimd
        def _(e):
            for t in pool.thunks:
                t(e)

        @block.sync
        def _(e):
            for t in sp.thunks:
                t(e)
    return nc


def _mmd(e, dv, q, banks, aT, JQ, cs=0):
    ins = None
    for jj in range(JQ):
        for mi in range(2):
            ins = e.matmul(banks[mi][0][:, cs:TB], dv[:, jj, mi * 128:(mi + 1) * 128], aT[:, q * JQ + jj, cs:TB],
                           start=(q == 0 and jj == 0), stop=(q == 1 and jj == JQ - 1))
    return ins


def _pl(a, nchunk):
    sh = a.shape[:-1]
    b = a.reshape(*sh, nchunk, 128)
    b = np.moveaxis(b, -1, 0)
    return np.ascontiguousarray(b)


def make_core_inputs(cfg, xp_ext, xs, state_pool, state_conv, W, is_seq_start):
    L, KC = cfg.L, cfg.KC
    x = np.concatenate([xp_ext, xs.reshape(-1, cfg.D)], axis=0)
    m = dict(W)
    m["xT"] = np.ascontiguousarray(x.T)
    invc = np.zeros((128, 4, 16), np.float32)
    for g in range(4):
        w = 2 << g
        for t in range(16):
            invc[:, g, t] = 1.0 / (min(t + 1, w) if is_seq_start else w)
    m["invc"] = invc.reshape(128, 64)
    ph = state_pool.reshape(L, cfg.nsamp, 15, 4, 128).transpose(4, 0, 1, 3, 2)
    m["phist"] = np.ascontiguousarray(ph).reshape(128, -1)
    ch = state_conv.reshape(L, cfg.nsamp, 2, 6, 128).transpose(4, 0, 1, 3, 2)
    m["chist"] = np.ascontiguousarray(ch).reshape(128, -1)
    return m


def make_shared_weights(cfg, w_in, w_out, pool_w, pool_scale, conv_w, sgu_w, sgu_b, f1g, f1u, f1d, f2g, f2u, f2d,
                        norm_gains, mix_gain):
    L, KC = cfg.L, cfg.KC
    W = {}
    D, DFF, JQ = cfg.D, cfg.DFF, cfg.JQ

    def tile_cols(a, nk):
        C = a.shape[2]
        return np.ascontiguousarray(a.reshape(L, nk, 128, C // 256, 256).transpose(0, 3, 2, 1, 4)).reshape(L, C // 256, 128, nk * 256)

    def tile_down(a):
        b = a.reshape(L, 2, 2, JQ, 128, D // 256, 256)
        return np.ascontiguousarray(b.transpose(0, 1, 5, 2, 4, 3, 6)).reshape(L, 2, D // 256, 2, 128, JQ * 256)
    W["w1g"], W["w1u"], W["w1d"] = tile_cols(f1g, KC), tile_cols(f1u, KC), tile_down(f1d)
    W["w2g"], W["w2u"], W["w2d"] = tile_cols(f2g, KC), tile_cols(f2u, KC), tile_down(f2d)
    W["w_in"], W["w_out"] = tile_cols(w_in, KC), tile_cols(w_out, MIXC)
    W["gains"] = _pl(norm_gains, KC).reshape(128, -1)
    W["poolw"] = pool_w
    W["pscale"] = _pl(pool_scale, 4).reshape(128, -1)
    W["convw"] = _pl(conv_w, 6).reshape(128, -1)
    W["mixg"] = _pl(mix_gain, MIXC).reshape(128, -1)
    wsT = np.ascontiguousarray(sgu_w.transpose(3, 0, 1, 2))
    W["sguwT"] = wsT.reshape(128, -1)
    ts = cfg.tsamp
    bd = np.zeros_like(wsT)
    for r in range(128 // ts):
        bd[r * ts:(r + 1) * ts, :, :, r * ts:(r + 1) * ts] = wsT[0:ts, :, :, 0:ts]
    W["sguwbd"] = bd.reshape(128, -1)
    W["sgub"] = np.ascontiguousarray(np.broadcast_to(sgu_b.reshape(1, -1), (128, L * 6 * 128))).astype(np.float32)
    t = np.arange(128)
    mask = (t[:, None] <= t[None, :]).astype(np.float32)
    W["mask"] = mask
    mbd = np.zeros((128, 128), np.float32)
    for r in range(128 // ts):
        mbd[r * ts:(r + 1) * ts, r * ts:(r + 1) * ts] = mask[0:ts, 0:ts]
    W["maskbd"] = mbd
    return {k: np.ascontiguousarray(v, dtype=np.float32) for k, v in W.items()}


_NC_CACHE = {}


def get_program(cfg, key):
    if key not in _NC_CACHE:
        _NC_CACHE[key] = build_program(cfg)
    return _NC_CACHE[key]


def kernel(x_prompt, x_sample, state_pool, state_conv, w_in, w_out, pool_w, pool_scale, conv_w, sgu_w, sgu_b,
           ffn1_gate, ffn1_up, ffn1_down, ffn2_gate, ffn2_up, ffn2_down, norm_gains, mix_gain):
    cfg = Cfg()
    f32 = lambda a: np.asarray(a, dtype=np.float32)
    x_prompt, x_sample, state_pool, state_conv = map(f32, (x_prompt, x_sample, state_pool, state_conv))
    W = make_shared_weights(cfg, f32(w_in), f32(w_out), f32(pool_w), f32(pool_scale), f32(conv_w), f32(sgu_w), f32(sgu_b),
                            f32(ffn1_gate), f32(ffn1_up), f32(ffn1_down), f32(ffn2_gate), f32(ffn2_up), f32(ffn2_down),
                            f32(norm_gains), f32(mix_gain))
    B, S, D = x_prompt.shape
    L = cfg.L
    NSEG = 4
    SEG = S // NSEG
    H = cfg.halo
    in_maps = []
    for c in range(8):
        b, s = c // NSEG, c % NSEG
        if s == 0:
            ext = np.concatenate([np.zeros((H, D), np.float32), x_prompt[b, 0:SEG]], axis=0)
        else:
            ext = x_prompt[b, s * SEG - H:(s + 1) * SEG]
        in_maps.append(make_core_inputs(cfg, ext, x_sample[4 * c:4 * c + 4], state_pool[:, 4 * c:4 * c + 4],
                                        state_conv[:, 4 * c:4 * c + 4], W, s == 0))
    nc = get_program(cfg, "full")
    res = run_bass_kernel_spmd(nc, in_maps, core_ids=list(range(8)))
    R = res.results
    y_prompt = np.zeros((B, S, D), np.float32)
    y_sample = np.zeros(x_sample.shape, np.float32)
    npp = np.zeros((L, B, 15, POOLW), np.float32)
    ncp = np.zeros((L, B, 2, CONVC), np.float32)
    nps = np.zeros((L, 32, 15, POOLW), np.float32)
    ncs = np.zeros((L, 32, 2, CONVC), np.float32)
    nvs = np.zeros((L, 32, 64, SGUW), np.float32)
    for c in range(8):
        b, s = c // NSEG, c % NSEG
        yT = R[c]["yT"]
        y = yT.T
        y_prompt[b, s * SEG:(s + 1) * SEG] = y[H:H + SEG]
        y_sample[4 * c:4 * c + 4] = y[H + SEG:].reshape(4, 64, D)
        if s == NSEG - 1:
            pt = R[c]["pool_tail_p"].reshape(128, L, 4, 15)
            npp[:, b] = pt.transpose(1, 3, 2, 0).reshape(L, 15, POOLW)
            ctl = R[c]["conv_tail_p"].reshape(128, L, 6, 2)
            ncp[:, b] = ctl.transpose(1, 3, 2, 0).reshape(L, 2, CONVC)
        ps_ = R[c]["pool_tail_s"].reshape(128, L, 4, 4, 15)
        nps[:, 4 * c:4 * c + 4] = ps_.transpose(1, 2, 4, 3, 0).reshape(L, 4, 15, POOLW)
        cs_ = R[c]["conv_tail_s"].reshape(128, L, 4, 6, 2)
        ncs[:, 4 * c:4 * c + 4] = cs_.transpose(1, 2, 4, 3, 0).reshape(L, 4, 2, CONVC)
        nvs[:, 4 * c:4 * c + 4] = R[c]["v_s"].reshape(L, 4, 64, SGUW)
    return (y_prompt, y_sample, npp, ncp, nps, ncs, nvs)
```

```python
import numpy as np
import concourse.bass as bass
import concourse.mybir as mybir
from concourse.bass_utils import run_bass_kernel_spmd

F32 = mybir.dt.float32
BF16 = mybir.dt.bfloat16
AF = mybir.ActivationFunctionType
ALU = mybir.AluOpType

EPS = 1e-6
POOLW = 512
CONVC = 768
SGUW = 768
MIXW = 2048
INW = 4352
MIXC = MIXW // 128
TB = 512
SEM_LIMIT = 30000


class Cfg:
    def __init__(self, D=2048, DFF=5632, L=4, blocks=None, halo=256, nsamp=4, tsamp=64):
        self.D, self.DFF, self.L = D, DFF, L
        self.KC = D // 128
        self.JC = DFF // 128
        self.JH = self.JC // 2
        self.JQ = self.JH // 2
        assert self.JQ * 4 == self.JC
        self.blocks = blocks if blocks is not None else [512, 512, 512, 512, 256]
        self.halo = halo
        self.nsamp, self.tsamp = nsamp, tsamp
        self.NTOK = TB * len(self.blocks)
        self.NEXT = sum(self.blocks)
        self.NS = nsamp * tsamp
        assert self.NTOK == self.NEXT + self.NS


class Region:
    __slots__ = ("w", "r")

    def __init__(self):
        self.w = None
        self.r = []


class EngCtx:
    def __init__(self, prog, name, is_pe=False):
        self.prog = prog
        self.name = name
        self.thunks = []
        self.sem = None
        self.count = 0
        self.waited = {}
        self.own = set()
        self.is_pe = is_pe
        self.nsem = 0

    def wait(self, tok):
        sem, val = tok
        if self.is_pe and sem in self.own:
            return
        if self.waited.get(sem, 0) >= val:
            return
        self.waited[sem] = val
        self.thunks.append(lambda e, sem=sem, val=val: e.wait_ge(sem, val))

    def emit(self, fn):
        if self.sem is None or self.count >= SEM_LIMIT:
            self.sem = self.prog.nc.alloc_semaphore(f"p_{self.name}_{self.nsem}")
            self.nsem += 1
            self.own.add(self.sem)
            self.count = 0
        self.count += 1
        sem, val = self.sem, self.count
        self.thunks.append(lambda e, fn=fn, sem=sem: fn(e).then_inc(sem, 1))
        return (sem, val)

    def emit_dma(self, fn, sem_state):
        sem_state[1] += 16
        sem = sem_state[0]
        self.thunks.append(lambda e, fn=fn, sem=sem: fn(e).then_inc(sem, 16))
        return (sem, sem_state[1])


class Prog:
    def __init__(self, cfg):
        self.cfg = cfg
        self.nc = bass.Bass("TRN2", target_bir_lowering=False)
        self.pe = EngCtx(self, "pe", is_pe=True)
        self.act = EngCtx(self, "act")
        self.dve = EngCtx(self, "dve")
        self.pool = EngCtx(self, "pool")
        self.sp = EngCtx(self, "sp")
        self.nbank = 0

    def op(self, eng, fn, reads=(), writes=()):
        deps = []
        for R in reads:
            if R.w is not None:
                deps.append(R.w)
        for W in writes:
            if W.w is not None:
                deps.append(W.w)
            deps.extend(W.r)
        for t in deps:
            eng.wait(t)
        tok = eng.emit(fn)
        for R in reads:
            R.r.append(tok)
        for W in writes:
            W.w = tok
            W.r = []
        return tok

    def dma(self, eng, fn, sem_state, reads=(), writes=()):
        deps = []
        for R in reads:
            if R.w is not None:
                deps.append(R.w)
        for W in writes:
            if W.w is not None:
                deps.append(W.w)
            deps.extend(W.r)
        for t in deps:
            eng.wait(t)
        tok = eng.emit_dma(fn, sem_state)
        for R in reads:
            R.r.append(tok)
        for W in writes:
            W.w = tok
            W.r = []
        return tok

    def new_dma_sem(self, name):
        return [self.nc.alloc_semaphore(name), 0]


def build_program(cfg):
    P = Prog(cfg)
    nc = P.nc
    D, DFF, L, KC, JC, JH, JQ = cfg.D, cfg.DFF, cfg.L, cfg.KC, cfg.JC, cfg.JH, cfg.JQ
    NB = len(cfg.blocks)
    NTOK = cfg.NTOK
    NSAMP, TS = cfg.nsamp, cfg.tsamp
    NS = cfg.NS

    def din(name, shape, dt=F32):
        return nc.dram_tensor(name, list(shape), dt, kind="ExternalInput").ap()

    def dout(name, shape, dt=F32):
        return nc.dram_tensor(name, list(shape), dt, kind="ExternalOutput").ap()

    xT_d = din("xT", [D, NTOK])
    w_ffn = {}
    for f in (1, 2):
        w_ffn[f] = (din(f"w{f}g", [L, DFF // 256, 128, KC * 256]), din(f"w{f}u", [L, DFF // 256, 128, KC * 256]),
                    din(f"w{f}d", [L, 2, D // 256, 2, 128, JQ * 256]))
    win_d = din("w_in", [L, INW // 256, 128, KC * 256])
    wout_d = din("w_out", [L, D // 256, 128, MIXC * 256])
    gains_d = din("gains", [128, L * 6 * KC])
    poolw_d = din("poolw", [L, 4, 128, 128])
    pscale_d = din("pscale", [128, L * 4])
    convw_d = din("convw", [128, L * 3 * 6])
    mixg_d = din("mixg", [128, L * MIXC])
    sguw_d = din("sguwT", [128, L * 6 * 128])
    sguwbd_d = din("sguwbd", [128, L * 6 * 128])
    sgub_d = din("sgub", [128, L * 6 * 128])
    mask_d = din("mask", [128, 128])
    maskbd_d = din("maskbd", [128, 128])
    invc_d = din("invc", [128, 4 * 16])
    phist_d = din("phist", [128, L * NSAMP * 4 * 15])
    chist_d = din("chist", [128, L * NSAMP * 6 * 2])

    yT_d = dout("yT", [D, NTOK])
    ptp_d = dout("pool_tail_p", [128, L * 4 * 15])
    ctp_d = dout("conv_tail_p", [128, L * 6 * 2])
    pts_d = dout("pool_tail_s", [128, L * NSAMP * 4 * 15])
    cts_d = dout("conv_tail_s", [128, L * NSAMP * 6 * 2])
    vs_d = dout("v_s", [L, NS, SGUW])

    def tiles_ffn(f, l):
        g, u, d = w_ffn[f]
        out = []
        for hf in range(2):
            gu = []
            for jg in range(JH // 2):
                jt = hf * (JH // 2) + jg
                gu.append((g[l, jt].rearrange("p (k c) -> p k c", c=256), KC, 256))
                gu.append((u[l, jt].rearrange("p (k c) -> p k c", c=256), KC, 256))
            dn = []
            for mg in range(D // 256):
                for q in range(2):
                    dn.append((d[l, hf, mg, q].rearrange("p (j c) -> p j c", c=256), JQ, 256))
            out.append((gu, dn))
        return out

    def tiles_in(l):
        return [(win_d[l, ct].rearrange("p (k c) -> p k c", c=256), KC, 256) for ct in range(INW // 256)]

    def tiles_out(l):
        return [(wout_d[l, ct].rearrange("p (k c) -> p k c", c=256), MIXC, 256) for ct in range(D // 256)]

    SLOT_ELEMS = max(KC, MIXC, JQ) * 256
    n_tiles_layer = 2 * (2 * (JH + (D // 256) * 2)) + INW // 256 + D // 256
    n_tiles = n_tiles_layer * L
    scratch_l = [nc.dram_tensor(f"wscratch{l}", [n_tiles_layer, 128, SLOT_ELEMS], BF16, kind="Internal").ap() for l in range(L)]
    scratch_regions = [Region() for _ in range(n_tiles)]

    def sb(name, shape, dt):
        return nc.alloc_sbuf_tensor(name, list(shape), dt)

    xT = sb("xT_sb", [128, KC, TB], F32)
    xn = sb("xn_sb", [128, KC, TB], BF16)
    NA = max(JH, 20)
    aT = sb("aT_sb", [128, NA, TB], BF16)
    ymix = aT
    dT = aT[:, 16:20, :]
    NF = max(KC, 16)
    fT = sb("fT_sb", [128, NF, TB], F32)
    pT = fT[:, 0:4, :]
    cau = fT[:, 4:10, :]
    fflat = fT[:].rearrange("p a b -> p (a b)")
    pext1 = fflat[:, 10 * TB:10 * TB + 15 + TB]
    ZW = 2 + TB + 2 * NSAMP + 8
    zbv = [fflat[:, 12 * TB:12 * TB + ZW], fflat[:, 14 * TB:14 * TB + ZW]]
    ybuf = sb("ybuf_sb", [128, 6, TB], F32)
    sg = sb("sg_sb", [128, 3, TB + 16], F32)
    sq = sb("sq_sb", [128, 4, TB], BF16)
    rstd = sb("rstd_sb", [128, 2, TB], F32)
    v_bf = sb("vbf_sb", [128, 4, SGUW], BF16)
    v_f32 = sb("vf32_sb", [128, 2, SGUW], F32)
    slots = [sb(f"slot{i}", [128, SLOT_ELEMS], BF16) for i in range(4)]
    gains = sb("gains_sb", [128, L * 6 * KC], F32)
    poolw = sb("poolw_sb", [128, L * 4, 128], BF16)
    pscale = sb("pscale_sb", [128, L * 4], F32)
    convw = sb("convw_sb", [128, L * 3 * 6], F32)
    mixg = sb("mixg_sb", [128, L * MIXC], F32)
    wsT = sb("wsT_sb", [128, L * 6, 128], BF16)
    wsTbd = sb("wsTbd_sb", [128, L * 6, 128], BF16)
    sgub = sb("sgub_sb", [128, 6, 128], F32)
    maskt = sb("mask_sb", [128, 2, 128], F32)
    invc = sb("invc_sb", [128, 4, 16], F32)
    ones = sb("ones_sb", [128, 128], BF16)
    epsb = sb("epsb_sb", [128, 4], F32)
    phist = sb("phist_sb", [128, L * NSAMP, 4, 15], F32)
    chist = sb("chist_sb", [128, L * NSAMP, 6, 2], F32)
    ptail = sb("ptail_sb", [128, L, 4, 15], F32)
    ctail = sb("ctail_sb", [128, L, 6, 2], F32)
    pts = sb("pts_sb", [128, L * NSAMP, 4, 15], F32)
    cts = sb("cts_sb", [128, L * NSAMP, 6, 2], F32)

    psb = [nc.alloc_psum_tensor(f"ps{i}", [128, TB], F32) for i in range(8)]
    bank_r = [Region() for _ in range(8)]

    def next_bank():
        i = P.nbank % 7
        P.nbank += 1
        return psb[i], bank_r[i]

    ss_bank, ss_r = psb[7], bank_r[7]

    R_x = [Region() for _ in range(KC)]
    R_xn = [Region() for _ in range(KC)]
    R_a = [Region() for _ in range(NA)]
    R_ym = R_a
    R_dT = R_a[16:20]
    R_f = [Region() for _ in range(NF)]
    R_pg = R_f[0:4]
    R_cau = R_f[4:10]
    R_pext = R_f[10:12]
    R_zb = [R_f[12:14], R_f[14:16]]
    R_y = [Region() for _ in range(6)]
    R_sq = [Region() for _ in range(4)]
    R_sg = [Region() for _ in range(3)]
    R_rstd = [Region(), Region()]
    R_vbf = [Region() for _ in range(4)]
    R_vf = [Region(), Region()]
    R_slot = [Region() for _ in range(4)]
    R_const = Region()
    R_sgub = Region()
    R_ptail = [Region() for _ in range(L)]
    R_ctail = [Region() for _ in range(L)]
    R_pts = Region()
    R_cts = Region()
    R_tmpc = Region()

    ctr = {"sq": 0, "rstd": 0, "sg": 0, "zb": 0, "slot": 0}

    sem_x = P.new_dma_sem("sem_x")
    sem_y = P.new_dma_sem("sem_y")
    sem_small = P.new_dma_sem("sem_small")
    sem_const = P.new_dma_sem("sem_const")
    sem_slot = [P.new_dma_sem(f"sem_slot{i}") for i in range(4)]
    sem_wb = [P.new_dma_sem(f"sem_wb{i}") for i in range(4)]

    pe, act, dve, pool, sp = P.pe, P.act, P.dve, P.pool, P.sp

    def cload(dst_ap, src_ap, eng=sp):
        P.dma(eng, lambda e, d=dst_ap, s=src_ap: e.dma_start(out=d, in_=s), sem_const, writes=[R_const])

    cload(gains[:], gains_d[:, :])
    cload(pscale[:], pscale_d[:, :])
    cload(convw[:], convw_d[:, :])
    cload(mixg[:], mixg_d[:, :])
    cload(maskt[:, 0, :], mask_d[:, :])
    cload(maskt[:, 1, :], maskbd_d[:, :])
    cload(invc[:].rearrange("p a b -> p (a b)"), invc_d[:, :])
    cload(phist[:].rearrange("p a b c -> p (a b c)"), phist_d[:, :])
    cload(chist[:].rearrange("p a b c -> p (a b c)"), chist_d[:, :])
    cload(poolw[:], poolw_d.rearrange("l g c d -> c (l g) d"), eng=pool)
    assert L * 6 * 128 <= KC * TB or True
    tmpw = fT[:].rearrange("p a b -> p (a b)")
    n_w = L * 6 * 128
    if n_w <= KC * TB:
        for (src, dst, mi) in ((sguw_d, wsT, 0), (sguwbd_d, wsTbd, 1)):
            P.dma(sp, lambda e, s=src: e.dma_start(out=tmpw[:, 0:n_w], in_=s[:, :]), sem_const, writes=[R_tmpc])
            for lh in range(L * 6):
                P.op(dve, lambda e, lh=lh, dst=dst, mi=mi: e.tensor_tensor(
                    out=dst[:, lh, :], in0=tmpw[:, lh * 128:(lh + 1) * 128], in1=maskt[:, mi, :], op=ALU.mult),
                    reads=[R_tmpc, R_const], writes=[R_const])
            R_tmpc.r = [R_const.w]
    else:
        for (src, dst, mi) in ((sguw_d, wsT, 0), (sguwbd_d, wsTbd, 1)):
            for lh in range(L * 6):
                P.dma(sp, lambda e, s=src, lh=lh: e.dma_start(out=tmpw[:, 0:128], in_=s[:, lh * 128:(lh + 1) * 128]),
                      sem_const, writes=[R_tmpc])
                P.op(dve, lambda e, lh=lh, dst=dst, mi=mi: e.tensor_tensor(
                    out=dst[:, lh, :], in0=tmpw[:, 0:128], in1=maskt[:, mi, :], op=ALU.mult),
                    reads=[R_tmpc, R_const], writes=[R_tmpc])
        R_const.w = R_tmpc.w
    P.op(dve, lambda e: e.memset(ones[:], 1.0), writes=[R_const])
    for wi, wd in enumerate((D, POOLW, CONVC)):
        P.op(dve, lambda e, wi=wi, wd=wd: e.memset(epsb[:, wi:wi + 1], float(wd * EPS)), writes=[R_const])
    for l in range(L):
        P.op(dve, lambda e, l=l: e.memset(ptail[:, l], 0.0), writes=[R_ptail[l]])
        P.op(dve, lambda e, l=l: e.memset(ctail[:, l], 0.0), writes=[R_ctail[l]])
    sD = float(np.sqrt(D))
    for l in range(L):
        for s in range(6):
            sc = sD * (0.5 if s in (1, 5) else 1.0)
            o = (l * 6 + s) * KC
            P.op(dve, lambda e, o=o, sc=sc: e.tensor_scalar(
                out=gains[:, o:o + KC], in0=gains[:, o:o + KC], scalar1=sc, scalar2=None, op0=ALU.mult),
                reads=[R_const], writes=[R_const])
        o = l * MIXC
        P.op(dve, lambda e, o=o: e.tensor_scalar(
            out=mixg[:, o:o + 4], in0=mixg[:, o:o + 4], scalar1=float(np.sqrt(POOLW)), scalar2=None, op0=ALU.mult),
            reads=[R_const], writes=[R_const])
        P.op(dve, lambda e, o=o: e.tensor_scalar(
            out=mixg[:, o + 4:o + 16], in0=mixg[:, o + 4:o + 16], scalar1=float(np.sqrt(CONVC)), scalar2=None, op0=ALU.mult),
            reads=[R_const], writes=[R_const])

    def gcol(l, s, c):
        o = (l * 6 + s) * KC + c
        return gains[:, o:o + 1]

    tile_counter = {"i": 0}

    def fetch_tile(bi, tdesc, l):
        view, a, b = tdesc
        ti = tile_counter["i"]
        tile_counter["i"] += 1
        tg = ti
        si = ctr["slot"] % 4
        ctr["slot"] += 1
        slot = slots[si]
        sv = slot[:, 0:a * b].rearrange("p (a b) -> p a b", b=b)
        do_cast = (bi == 0) or (bi == 1 and l >= L // 2)
        do_wb = (bi == 0 and l < L // 2) or (bi == 1 and l >= L // 2)
        if do_cast:
            P.dma(pool, lambda e, sv=sv, view=view: e.dma_start(out=sv, in_=view), sem_slot[si], writes=[R_slot[si]])
            if do_wb:
                P.dma(sp, lambda e, tg=tg, slot=slot, n=a * b: e.dma_start(out=scratch_l[tg // n_tiles_layer][tg % n_tiles_layer, :, 0:n], in_=slot[:, 0:n]),
                      sem_wb[si], reads=[R_slot[si]], writes=[scratch_regions[tg]])
        else:
            P.dma(sp, lambda e, tg=tg, slot=slot, n=a * b: e.dma_start(out=slot[:, 0:n], in_=scratch_l[tg // n_tiles_layer][tg % n_tiles_layer, :, 0:n]),
                  sem_slot[si], reads=[scratch_regions[tg]], writes=[R_slot[si]])
        return sv, R_slot[si]

    pending = []
    sched = {"dense": True}

    def flush_ss(keep=0):
        while len(pending) > keep:
            pending.pop(0)()

    def pe_op(fn, reads=(), writes=()):
        tok = P.op(pe, fn, reads=reads, writes=writes)
        flush_ss()
        return tok

    def ss_add(src, c, reg, first, last, cs=0):
        flush_ss(keep=2)
        i = ctr["sq"] % 4
        ctr["sq"] += 1
        P.op(act, lambda e, i=i: e.activation(out=sq[:, i, cs:TB], in_=src[:, c, cs:TB], func=AF.Square),
             reads=[reg], writes=[R_sq[i]])
        pending.append(lambda i=i, first=first, last=last: P.op(
            pe, lambda e: e.matmul(ss_bank[:, cs:TB], ones[:, :], sq[:, i, cs:TB], start=first, stop=last),
            reads=[R_sq[i], R_const], writes=[ss_r]))
        if not sched["dense"]:
            flush_ss()

    def ss_finish(width):
        flush_ss()
        ri = ctr["rstd"] % 2
        ctr["rstd"] += 1
        wi = 0 if width == D else {POOLW: 1, CONVC: 2}[width]
        P.op(act, lambda e, ri=ri, wi=wi: e.activation(
            out=rstd[:, ri, :], in_=ss_bank[:, :], func=AF.Ln, bias=epsb[:, wi:wi + 1], scale=1.0),
            reads=[R_const], writes=[R_rstd[ri], ss_r])
        P.op(act, lambda e, ri=ri: e.activation(out=rstd[:, ri, :], in_=rstd[:, ri, :], func=AF.Exp, scale=-0.5),
             reads=[R_rstd[ri]], writes=[R_rstd[ri]])
        return rstd[:, ri, :], R_rstd[ri]

    def ss_x_all():
        for c in range(KC):
            ss_add(xT, c, R_x[c], c == 0, c == KC - 1)

    def prenorm(l, s, cs=0):
        rs, rr = ss_finish(D)
        for c in range(KC):
            P.op(dve, lambda e, c=c, rs=rs: e.scalar_tensor_tensor(
                out=xn[:, c, cs:TB], in0=xT[:, c, cs:TB], scalar=gcol(l, s, c), in1=rs[:, cs:TB], op0=ALU.mult, op1=ALU.mult),
                reads=[R_x[c], rr, R_const], writes=[R_xn[c]])

    def postnorm_add(l, s, want_next, cs=0):
        rs, rr = ss_finish(D)
        prev = None

        def fin(pc, pui):
            P.op(dve, lambda e, c=pc, ui=pui: e.tensor_tensor(out=xT[:, c, cs:TB], in0=xT[:, c, cs:TB], in1=sg[:, ui, cs:TB], op=ALU.add),
                 reads=[R_sg[pui]], writes=[R_x[pc]])
            if want_next:
                ss_add(xT, pc, R_x[pc], pc == 0, pc == KC - 1, cs)
        for c in range(KC):
            ui = ctr["sg"] % 3
            ctr["sg"] += 1
            P.op(dve, lambda e, c=c, rs=rs, ui=ui: e.scalar_tensor_tensor(
                out=sg[:, ui, cs:TB], in0=fT[:, c, cs:TB], scalar=gcol(l, s, c), in1=rs[:, cs:TB], op0=ALU.mult, op1=ALU.mult),
                reads=[R_f[c], rr, R_const], writes=[R_sg[ui]])
            if prev is not None:
                fin(*prev)
            prev = (c, ui)
        fin(*prev)

    def ffn(bi, l, f, s_pre, s_post, want_next, cs=0):
        prenorm(l, s_pre, cs)
        tl = tiles_ffn(f, l)
        for hf in range(2):
            gu, dn = tl[hf]
            for jg in range(JH // 2):
                gv, gr = fetch_tile(bi, gu[2 * jg], l)
                uv, ur = fetch_tile(bi, gu[2 * jg + 1], l)
                for ci in range(2):
                    jl = jg * 2 + ci
                    bg_, bgr = next_bank()
                    bu_, bur = next_bank()

                    def mmg(e, wv=gv, bank=bg_, ci=ci):
                        ins = None
                        for k in range(KC):
                            ins = e.matmul(bank[:, cs:TB], wv[:, k, ci * 128:(ci + 1) * 128], xn[:, k, cs:TB], start=(k == 0), stop=(k == KC - 1))
                        return ins
                    if hf == 0 and jg == 0 and ci == 0:
                        for k in range(KC):
                            for (wv_, bank_, br_, wr_) in ((gv, bg_, bgr, gr), (uv, bu_, bur, ur)):
                                pe_op(lambda e, wv_=wv_, bank_=bank_, k=k: e.matmul(
                                    bank_[:, cs:TB], wv_[:, k, 0:128], xn[:, k, cs:TB], start=(k == 0), stop=(k == KC - 1)),
                                    reads=[wr_, R_xn[k]], writes=[br_])
                    else:
                        pe_op(mmg, reads=[gr] + R_xn[:KC], writes=[bgr])
                        pe_op(lambda e, wv=uv, bank=bu_, ci=ci: mmg(e, wv, bank, ci), reads=[ur] + R_xn[:KC], writes=[bur])
                    si = ctr["sg"] % 3
                    ctr["sg"] += 1
                    P.op(act, lambda e, si=si, bank=bg_: e.activation(out=sg[:, si, cs:TB], in_=bank[:, cs:TB], func=AF.Silu),
                         reads=[bgr], writes=[R_sg[si]])
                    P.op(dve, lambda e, si=si, bank=bu_, jl=jl: e.tensor_tensor(out=aT[:, jl, cs:TB], in0=sg[:, si, cs:TB], in1=bank[:, cs:TB], op=ALU.mult),
                         reads=[R_sg[si], bur], writes=[R_a[jl]])
            ti = 0
            for mg in range(D // 256):
                b0, b0r = next_bank()
                b1, b1r = next_bank()
                banks = ((b0, b0r), (b1, b1r))
                for q in range(2):
                    dv, dr = fetch_tile(bi, dn[ti], l)
                    ti += 1
                    pe_op(lambda e, dv=dv, q=q, banks_l=banks: _mmd(e, dv, q, banks_l, aT, JQ, cs),
                          reads=[dr] + R_a[q * JQ:(q + 1) * JQ], writes=[b0r, b1r])
                for mi in range(2):
                    m = mg * 2 + mi
                    bank, br = banks[mi]
                    if hf == 0:
                        P.op(act, lambda e, m=m, bank=bank: e.activation(out=fT[:, m, cs:TB], in_=bank[:, cs:TB], func=AF.Copy),
                             reads=[br], writes=[R_f[m]])
                    else:
                        P.op(dve, lambda e, m=m, bank=bank: e.tensor_tensor(out=fT[:, m, cs:TB], in0=bank[:, cs:TB], in1=fT[:, m, cs:TB], op=ALU.add),
                             reads=[br, R_f[m]], writes=[R_f[m]])
                        ss_add(fT, m, R_f[m], m == 0, m == KC - 1, cs)
        postnorm_add(l, s_post, want_next, cs)

    def group_rms(l, y_chunks, c0, width):
        n = len(y_chunks)
        rs, rr = ss_finish(width)
        for c in range(n):
            o = l * MIXC + c0 + c
            P.op(dve, lambda e, c=c, rs=rs, o=o: e.scalar_tensor_tensor(
                out=ymix[:, c0 + c, :], in0=ybuf[:, y_chunks[c], :], scalar=mixg[:, o:o + 1], in1=rs, op0=ALU.mult, op1=ALU.mult),
                reads=[R_y[y_chunks[c]], rr, R_const], writes=[R_ym[c0 + c]])

    def mixer(bi, l):
        np_ = cfg.blocks[bi]
        has_s = np_ < TB
        last_prompt = (sum(cfg.blocks[:bi + 1]) == cfg.NEXT)
        P.dma(sp, lambda e: e.dma_start(out=sgub[:].rearrange("p a b -> p (a b)"), in_=sgub_d[:, l * 768:(l + 1) * 768]),
              sem_const, writes=[R_sgub])
        prenorm(l, 2)
        tin = tiles_in(l)
        order = [0, 1, 8, 9, 10, 2, 3, 4, 5, 6, 7, 14, 15, 16, 11, 12, 13]

        def fm_group(wv, ci):
            bank, br = next_bank()

            def mm(e, wv=wv, bank=bank, ci=ci):
                ins = None
                for k in range(KC):
                    ins = e.matmul(bank[:, :], wv[:, k, ci * 128:(ci + 1) * 128], xn[:, k, :], start=(k == 0), stop=(k == KC - 1))
                return ins
            return bank, br, mm

        def cw(k, c):
            o = (l * 3 + k) * 6 + c
            return convw[:, o:o + 1]

        def sgu_pe_bias():
            for h in range(6):
                bank, br = next_bank()
                for tt in range(4):
                    is_s = has_s and tt * 128 >= np_
                    wmat = wsTbd if is_s else wsT
                    pe_op(lambda e, bank=bank, tt=tt, h=h, wmat=wmat: e.matmul(
                        bank[:, tt * 128:(tt + 1) * 128], v_bf[:, tt, h * 128:(h + 1) * 128], wmat[:, l * 6 + h, :], start=True, stop=True),
                        reads=[R_vbf[tt], R_const], writes=[br])
                for tt in range(4):
                    is_s = has_s and tt * 128 >= np_
                    if not is_s:
                        P.op(dve, lambda e, bank=bank, tt=tt, h=h: e.tensor_tensor(
                            out=ybuf[:, h, tt * 128:(tt + 1) * 128], in0=bank[:, tt * 128:(tt + 1) * 128], in1=sgub[:, h, :], op=ALU.add),
                            reads=[br, R_sgub], writes=[R_y[h]])
                    else:
                        for hh in range(128 // TS):
                            c0 = tt * 128 + hh * TS
                            P.op(dve, lambda e, bank=bank, c0=c0, h=h: e.tensor_tensor(
                                out=ybuf[:, h, c0:c0 + TS], in0=bank[:, c0:c0 + TS], in1=sgub[:, h, 0:TS], op=ALU.add),
                                reads=[br, R_sgub], writes=[R_y[h]])

        for ct in order:
            wv, wr = fetch_tile(bi, tin[ct], l)
            if ct <= 13:
                pre = None
                if ct == order[0]:
                    pre = [next_bank(), next_bank()]
                    for k in range(KC):
                        for ci in range(2):
                            pe_op(lambda e, bank_=pre[ci][0], k=k, ci=ci, wv=wv: e.matmul(
                                bank_[:, :], wv[:, k, ci * 128:(ci + 1) * 128], xn[:, k, :], start=(k == 0), stop=(k == KC - 1)),
                                reads=[wr, R_xn[k]], writes=[pre[ci][1]])
                for ci in range(2):
                    if pre is not None:
                        bank, br = pre[ci]
                    else:
                        bank, br, mm = fm_group(wv, ci)
                        pe_op(mm, reads=[wr] + R_xn[:KC], writes=[br])
                    if ct <= 1:
                        g = ct * 2 + ci
                        P.op(act, lambda e, g=g, bank=bank: e.activation(out=pT[:, g, :], in_=bank[:, :], func=AF.Copy),
                             reads=[br], writes=[R_pg[g]])
                    elif 8 <= ct <= 10:
                        c = (ct - 8) * 2 + ci
                        P.op(act, lambda e, c=c, bank=bank: e.activation(out=cau[:, c, :], in_=bank[:, :], func=AF.Copy),
                             reads=[br], writes=[R_cau[c]])
                    elif 2 <= ct <= 4:
                        c = (ct - 2) * 2 + ci
                        conv_chunk(l, c, bank, br, np_, has_s, last_prompt, cw)
                    elif 5 <= ct <= 7:
                        c = (ct - 5) * 2 + ci
                        P.op(dve, lambda e, c=c, bank=bank: e.tensor_tensor(out=ybuf[:, c, :], in0=bank[:, :], in1=cau[:, c, :], op=ALU.mult),
                             reads=[br, R_cau[c]], writes=[R_y[c]])
                        ss_add(ybuf, c, R_y[c], c == 0, c == 5)
                    else:
                        c = (ct - 11) * 2 + ci
                        P.op(act, lambda e, c=c, bank=bank: e.activation(out=cau[:, c, :], in_=bank[:, :], func=AF.Copy),
                             reads=[br], writes=[R_cau[c]])
                        P.op(dve, lambda e, h=c: e.tensor_tensor(out=ybuf[:, h, :], in0=ybuf[:, h, :], in1=cau[:, h, :], op=ALU.mult),
                             reads=[R_cau[c]], writes=[R_y[c]])
                        ss_add(ybuf, c, R_y[c], c == 0, c == 5)
                if ct == 1:
                    pool_dve(bi, l, np_, has_s, last_prompt)
                    if not sched["dense"]:
                        pool_pe(l)
                elif ct == 10 and sched["dense"]:
                    pool_pe(l)
                elif ct == 2:
                    group_rms(l, list(range(4)), 0, POOLW)
            else:
                cv = (ct - 14) * 256
                for tt in range(4):
                    bank, br = next_bank()

                    def mmv(e, wv=wv, bank=bank, tt=tt):
                        ins = None
                        for k in range(KC):
                            ins = e.matmul(bank[:, 0:256], xn[:, k, tt * 128:(tt + 1) * 128], wv[:, k, :], start=(k == 0), stop=(k == KC - 1))
                        return ins
                    pe_op(mmv, reads=[wr] + R_xn[:KC], writes=[br])
                    P.op(act, lambda e, bank=bank, tt=tt, cv=cv: e.activation(out=v_bf[:, tt, cv:cv + 256], in_=bank[:, 0:256], func=AF.Copy),
                         writes=[R_vbf[tt], br])
                    if has_s and tt * 128 >= np_:
                        ts_ = (tt * 128 - np_) // 128
                        P.op(dve, lambda e, bank=bank, ts_=ts_, cv=cv: e.tensor_copy(out=v_f32[:, ts_, cv:cv + 256], in_=bank[:, 0:256]),
                             writes=[R_vf[ts_], br])
                if ct == 14:
                    group_rms(l, list(range(6)), 4, CONVC)
                if ct == 16:
                    if has_s:
                        for ts_ in range((TB - np_) // 128):
                            P.dma(pool, lambda e, ts_=ts_: e.dma_start(out=vs_d[l, ts_ * 128:(ts_ + 1) * 128, :], in_=v_f32[:, ts_, :]),
                                  sem_small, reads=[R_vf[ts_]])
                    sgu_pe_bias()
        group_rms(l, list(range(6)), 10, SGUW)
        tout = tiles_out(l)
        for ct in range(D // 256):
            wv, wr = fetch_tile(bi, tout[ct], l)
            for ci in range(2):
                bank, br = next_bank()

                def mmo(e, wv=wv, bank=bank, ci=ci):
                    ins = None
                    for k in range(MIXC):
                        ins = e.matmul(bank[:, :], wv[:, k, ci * 128:(ci + 1) * 128], ymix[:, k, :], start=(k == 0), stop=(k == MIXC - 1))
                    return ins
                pe_op(mmo, reads=[wr] + R_ym[:MIXC], writes=[br])
                m = ct * 2 + ci
                P.op(act, lambda e, m=m, bank=bank: e.activation(out=fT[:, m, :], in_=bank[:, :], func=AF.Copy),
                     reads=[br], writes=[R_f[m]])
                ss_add(fT, m, R_f[m], m == 0, m == KC - 1)
        postnorm_add(l, 3, True)

    def conv_chunk(l, c, bank, br, np_, has_s, last_prompt, cw):
        zi = ctr["zb"] % 2
        ctr["zb"] += 1
        z = zbv[zi]
        rzl = R_zb[zi]
        P.op(dve, lambda e, z=z, c=c: e.tensor_copy(out=z[:, 0:2], in_=ctail[:, l, c, :]), reads=[R_ctail[l]], writes=rzl)
        P.op(dve, lambda e, z=z, c=c, bank=bank: e.tensor_tensor(out=z[:, 2:2 + np_], in0=bank[:, 0:np_], in1=cau[:, c, 0:np_], op=ALU.mult),
             reads=[br, R_cau[c]], writes=rzl)
        if has_s:
            zs = z[:, 2 + np_:2 + np_ + NSAMP * (TS + 2)].rearrange("p (s t) -> p s t", t=TS + 2)
            P.op(dve, lambda e, zs=zs, c=c: e.tensor_copy(out=zs[:, :, 0:2], in_=chist[:, l * NSAMP:(l + 1) * NSAMP, c, :]),
                 reads=[R_const], writes=rzl)
            P.op(dve, lambda e, zs=zs, c=c, bank=bank: e.tensor_tensor(
                out=zs[:, :, 2:2 + TS], in0=bank[:, np_:TB].rearrange("p (s t) -> p s t", t=TS),
                in1=cau[:, c, np_:TB].rearrange("p (s t) -> p s t", t=TS), op=ALU.mult),
                reads=[br, R_cau[c]], writes=rzl)
        acc = cau[:, c, 0:np_]
        P.op(dve, lambda e, z=z, acc=acc, c=c: e.tensor_scalar(out=acc, in0=z[:, 0:np_], scalar1=cw(0, c), scalar2=None, op0=ALU.mult),
             reads=rzl + [R_const], writes=[R_cau[c]])
        for k in (1, 2):
            P.op(dve, lambda e, z=z, acc=acc, c=c, k=k: e.scalar_tensor_tensor(
                out=acc, in0=z[:, k:k + np_], scalar=cw(k, c), in1=acc, op0=ALU.mult, op1=ALU.add),
                reads=rzl + [R_const], writes=[R_cau[c]])
        if has_s:
            accs = cau[:, c, np_:TB].rearrange("p (s t) -> p s t", t=TS)
            P.op(dve, lambda e, zs=zs, accs=accs, c=c: e.tensor_scalar(out=accs, in0=zs[:, :, 0:TS], scalar1=cw(0, c), scalar2=None, op0=ALU.mult),
                 reads=rzl + [R_const], writes=[R_cau[c]])
            for k in (1, 2):
                P.op(dve, lambda e, zs=zs, accs=accs, c=c, k=k: e.scalar_tensor_tensor(
                    out=accs, in0=zs[:, :, k:k + TS], scalar=cw(k, c), in1=accs, op0=ALU.mult, op1=ALU.add),
                    reads=rzl + [R_const], writes=[R_cau[c]])
            P.op(dve, lambda e, zs=zs, c=c: e.tensor_copy(out=cts[:, l * NSAMP:(l + 1) * NSAMP, c, :], in_=zs[:, :, TS:TS + 2]),
                 reads=rzl, writes=[R_cts])
        P.op(dve, lambda e, z=z, c=c: e.tensor_copy(out=ctail[:, l, c, :], in_=z[:, np_:np_ + 2]), reads=rzl, writes=[R_ctail[l]])

    def pool_segment(l, n, col0, hist_ap, hist_reg, tail_ap, tail_reg, fix_col):
        for g in range(4):
            w = 2 << g
            P.op(dve, lambda e, g=g: e.tensor_copy(out=pext1[:, 0:15], in_=hist_ap[:, g, :]), reads=[hist_reg], writes=R_pext)
            P.op(dve, lambda e, g=g: e.tensor_copy(out=pext1[:, 15:15 + n], in_=pT[:, g, col0:col0 + n]), reads=[R_pg[g]], writes=R_pext)
            P.op(dve, lambda e, g=g: e.tensor_copy(out=tail_ap[:, g, :], in_=pext1[:, n:n + 15]), reads=R_pext, writes=[tail_reg])
            cur = pext1
            cur_r = R_pext
            k = 1
            ti = 0
            while k < w:
                nxt = sg[:, ti, :]
                nr = [R_sg[ti]]
                lo = 2 * k - 1
                P.op(dve, lambda e, cur=cur, nxt=nxt, lo=lo, k=k: e.tensor_tensor(
                    out=nxt[:, lo:15 + n], in0=cur[:, lo:15 + n], in1=cur[:, lo - k:15 + n - k], op=ALU.add),
                    reads=cur_r, writes=nr)
                cur, cur_r = nxt, nr
                ti ^= 1
                k *= 2
            P.op(dve, lambda e, cur=cur, g=g, w=w: e.scalar_tensor_tensor(
                out=dT[:, g, col0:col0 + n], in0=cur[:, 15:15 + n], scalar=1.0 / w, in1=pT[:, g, col0:col0 + n], op0=ALU.mult, op1=ALU.subtract),
                reads=cur_r + [R_pg[g]], writes=[R_dT[g]])
            if fix_col is not None:
                fc = fix_col
                oth = sg[:, ti, :]
                orr = [R_sg[ti]]
                P.op(dve, lambda e, cur=cur, g=g, oth=oth, fc=fc: e.tensor_tensor(
                    out=oth[:, 0:16], in0=cur[:, 15 + fc:15 + fc + 16], in1=invc[:, g, :], op=ALU.mult),
                    reads=cur_r + [R_const], writes=orr)
                P.op(dve, lambda e, g=g, oth=oth, fc=fc: e.tensor_tensor(
                    out=dT[:, g, col0 + fc:col0 + fc + 16], in0=oth[:, 0:16], in1=pT[:, g, col0 + fc:col0 + fc + 16], op=ALU.subtract),
                    reads=orr + [R_pg[g]], writes=[R_dT[g]])

    def pool_dve(bi, l, np_, has_s, last_prompt):
        ext0 = sum(cfg.blocks[:bi])
        fix = None
        if ext0 <= cfg.halo < ext0 + np_:
            fix = cfg.halo - ext0
        pool_segment(l, np_, 0, ptail[:, l], R_ptail[l], ptail[:, l], R_ptail[l], fix)
        if has_s:
            for s_ in range(NSAMP):
                pool_segment(l, TS, np_ + s_ * TS, phist[:, l * NSAMP + s_], R_const, pts[:, l * NSAMP + s_], R_pts, None)

    def pool_pe(l):
        bks = [next_bank() for _ in range(4)]
        for g in range(4):
            bank, br = bks[g]
            pe_op(lambda e, bank=bank, g=g: e.matmul(bank[:, :], poolw[:, l * 4 + g, :], dT[:, g, :], start=True, stop=True),
                  reads=[R_dT[g], R_const], writes=[br])
        for g in range(4):
            bank, br = bks[g]
            P.op(dve, lambda e, bank=bank, g=g: e.tensor_scalar(
                out=ybuf[:, g, :], in0=bank[:, :], scalar1=pscale[:, l * 4 + g:l * 4 + g + 1], scalar2=None, op0=ALU.mult),
                reads=[br, R_const], writes=[R_y[g]])
            ss_add(ybuf, g, R_y[g], g == 0, g == 3)

    for bi in range(NB):
        col0 = bi * TB
        tile_counter["i"] = 0
        P.dma(pool, lambda e, col0=col0: e.dma_start(out=xT[:], in_=xT_d[:, col0:col0 + TB].rearrange("(k p) c -> p k c", p=128)),
              sem_x, writes=R_x)
        import os
        dbg = int(os.environ.get("KDBG", "9"))
        for l in range(L):
            if l == 0:
                ss_x_all()
            sched["dense"] = not ((bi == 0) or (bi == 1 and l >= L // 2))
            cs1 = cs2 = 0
            if bi == 0 and cfg.halo == 256:
                r = L - 1 - l
                cs1 = {0: 240, 1: 128, 2: 112}.get(r, 0)
                cs2 = {0: 256, 1: 240, 2: 128, 3: 112}.get(r, 0)
            ffn(bi, l, 1, 0, 1, True, cs1)
            mixer(bi, l)
            ffn(bi, l, 2, 4, 5, l < L - 1, cs2)
        P.dma(pool, lambda e, col0=col0: e.dma_start(out=yT_d[:, col0:col0 + TB].rearrange("(k p) c -> p k c", p=128), in_=xT[:]),
              sem_y, reads=R_x)
    P.dma(pool, lambda e: e.dma_start(out=ptp_d[:, :], in_=ptail[:].rearrange("p a b c -> p (a b c)")), sem_small, reads=R_ptail)
    P.dma(pool, lambda e: e.dma_start(out=ctp_d[:, :], in_=ctail[:].rearrange("p a b c -> p (a b c)")), sem_small, reads=R_ctail)
    P.dma(pool, lambda e: e.dma_start(out=pts_d[:, :], in_=pts[:].rearrange("p a b c -> p (a b c)")), sem_small, reads=[R_pts])
    P.dma(pool, lambda e: e.dma_start(out=cts_d[:, :], in_=cts[:].rearrange("p a b c -> p (a b c)")), sem_small, reads=[R_cts])
    pool.wait((sem_y[0], sem_y[1]))
    pool.wait((sem_small[0], sem_small[1]))
    for i in range(4):
        if sem_wb[i][1] > 0:
            sp.wait((sem_wb[i][0], sem_wb[i][1]))

    with nc.Block() as block:
        @block.tensor
        def _(e):
            for t in pe.thunks:
                t(e)

        @block.scalar
        def _(e):
            for t in act.thunks:
                t(e)

        @block.vector
        def _(e):
            for t in dve.thunks:
                t(e)

        @block.gpsimd
        def _(e):
            for t in pool.thunks:
                t(e)

        @block.sync
        def _(e):
            for t in sp.thunks:
                t(e)
    return nc


def _mmd(e, dv, q, banks, aT, JQ, cs=0):
    ins = None
    for jj in range(JQ):
        for mi in range(2):
            ins = e.matmul(banks[mi][0][:, cs:TB], dv[:, jj, mi * 128:(mi + 1) * 128], aT[:, q * JQ + jj, cs:TB],
                           start=(q == 0 and jj == 0), stop=(q == 1 and jj == JQ - 1))
    return ins


def _pl(a, nchunk):
    sh = a.shape[:-1]
    b = a.reshape(*sh, nchunk, 128)
    b = np.moveaxis(b, -1, 0)
    return np.ascontiguousarray(b)


def make_core_inputs(cfg, xp_ext, xs, state_pool, state_conv, W, is_seq_start):
    L, KC = cfg.L, cfg.KC
    x = np.concatenate([xp_ext, xs.reshape(-1, cfg.D)], axis=0)
    m = dict(W)
    m["xT"] = np.ascontiguousarray(x.T)
    invc = np.zeros((128, 4, 16), np.float32)
    for g in range(4):
        w = 2 << g
        for t in range(16):
            invc[:, g, t] = 1.0 / (min(t + 1, w) if is_seq_start else w)
    m["invc"] = invc.reshape(128, 64)
    ph = state_pool.reshape(L, cfg.nsamp, 15, 4, 128).transpose(4, 0, 1, 3, 2)
    m["phist"] = np.ascontiguousarray(ph).reshape(128, -1)
    ch = state_conv.reshape(L, cfg.nsamp, 2, 6, 128).transpose(4, 0, 1, 3, 2)
    m["chist"] = np.ascontiguousarray(ch).reshape(128, -1)
    return m


def make_shared_weights(cfg, w_in, w_out, pool_w, pool_scale, conv_w, sgu_w, sgu_b, f1g, f1u, f1d, f2g, f2u, f2d,
                        norm_gains, mix_gain):
    L, KC = cfg.L, cfg.KC
    W = {}
    D, DFF, JQ = cfg.D, cfg.DFF, cfg.JQ

    def tile_cols(a, nk):
        C = a.shape[2]
        return np.ascontiguousarray(a.reshape(L, nk, 128, C // 256, 256).transpose(0, 3, 2, 1, 4)).reshape(L, C // 256, 128, nk * 256)

    def tile_down(a):
        b = a.reshape(L, 2, 2, JQ, 128, D // 256, 256)
        return np.ascontiguousarray(b.transpose(0, 1, 5, 2, 4, 3, 6)).reshape(L, 2, D // 256, 2, 128, JQ * 256)
    W["w1g"], W["w1u"], W["w1d"] = tile_cols(f1g, KC), tile_cols(f1u, KC), tile_down(f1d)
    W["w2g"], W["w2u"], W["w2d"] = tile_cols(f2g, KC), tile_cols(f2u, KC), tile_down(f2d)
    W["w_in"], W["w_out"] = tile_cols(w_in, KC), tile_cols(w_out, MIXC)
    W["gains"] = _pl(norm_gains, KC).reshape(128, -1)
    W["poolw"] = pool_w
    W["pscale"] = _pl(pool_scale, 4).reshape(128, -1)
    W["convw"] = _pl(conv_w, 6).reshape(128, -1)
    W["mixg"] = _pl(mix_gain, MIXC).reshape(128, -1)
    wsT = np.ascontiguousarray(sgu_w.transpose(3, 0, 1, 2))
    W["sguwT"] = wsT.reshape(128, -1)
    ts = cfg.tsamp
    bd = np.zeros_like(wsT)
    for r in range(128 // ts):
        bd[r * ts:(r + 1) * ts, :, :, r * ts:(r + 1) * ts] = wsT[0:ts, :, :, 0:ts]
    W["sguwbd"] = bd.reshape(128, -1)
    W["sgub"] = np.ascontiguousarray(np.broadcast_to(sgu_b.reshape(1, -1), (128, L * 6 * 128))).astype(np.float32)
    t = np.arange(128)
    mask = (t[:, None] <= t[None, :]).astype(np.float32)
    W["mask"] = mask
    mbd = np.zeros((128, 128), np.float32)
    for r in range(128 // ts):
        mbd[r * ts:(r + 1) * ts, r * ts:(r + 1) * ts] = mask[0:ts, 0:ts]
    W["maskbd"] = mbd
    return {k: np.ascontiguousarray(v, dtype=np.float32) for k, v in W.items()}


_NC_CACHE = {}


def get_program(cfg, key):
    if key not in _NC_CACHE:
        _NC_CACHE[key] = build_program(cfg)
    return _NC_CACHE[key]


def kernel(x_prompt, x_sample, state_pool, state_conv, w_in, w_out, pool_w, pool_scale, conv_w, sgu_w, sgu_b,
           ffn1_gate, ffn1_up, ffn1_down, ffn2_gate, ffn2_up, ffn2_down, norm_gains, mix_gain):
    cfg = Cfg()
    f32 = lambda a: np.asarray(a, dtype=np.float32)
    x_prompt, x_sample, state_pool, state_conv = map(f32, (x_prompt, x_sample, state_pool, state_conv))
    W = make_shared_weights(cfg, f32(w_in), f32(w_out), f32(pool_w), f32(pool_scale), f32(conv_w), f32(sgu_w), f32(sgu_b),
                            f32(ffn1_gate), f32(ffn1_up), f32(ffn1_down), f32(ffn2_gate), f32(ffn2_up), f32(ffn2_down),
                            f32(norm_gains), f32(mix_gain))
    B, S, D = x_prompt.shape
    L = cfg.L
    NSEG = 4
    SEG = S // NSEG
    H = cfg.halo
    in_maps = []
    for c in range(8):
        b, s = c // NSEG, c % NSEG
        if s == 0:
            ext = np.concatenate([np.zeros((H, D), np.float32), x_prompt[b, 0:SEG]], axis=0)
        else:
            ext = x_prompt[b, s * SEG - H:(s + 1) * SEG]
        in_maps.append(make_core_inputs(cfg, ext, x_sample[4 * c:4 * c + 4], state_pool[:, 4 * c:4 * c + 4],
                                        state_conv[:, 4 * c:4 * c + 4], W, s == 0))
    nc = get_program(cfg, "full")
    res = run_bass_kernel_spmd(nc, in_maps, core_ids=list(range(8)))
    R = res.results
    y_prompt = np.zeros((B, S, D), np.float32)
    y_sample = np.zeros(x_sample.shape, np.float32)
    npp = np.zeros((L, B, 15, POOLW), np.float32)
    ncp = np.zeros((L, B, 2, CONVC), np.float32)
    nps = np.zeros((L, 32, 15, POOLW), np.float32)
    ncs = np.zeros((L, 32, 2, CONVC), np.float32)
    nvs = np.zeros((L, 32, 64, SGUW), np.float32)
    for c in range(8):
        b, s = c // NSEG, c % NSEG
        yT = R[c]["yT"]
        y = yT.T
        y_prompt[b, s * SEG:(s + 1) * SEG] = y[H:H + SEG]
        y_sample[4 * c:4 * c + 4] = y[H + SEG:].reshape(4, 64, D)
        if s == NSEG - 1:
            pt = R[c]["pool_tail_p"].reshape(128, L, 4, 15)
            npp[:, b] = pt.transpose(1, 3, 2, 0).reshape(L, 15, POOLW)
            ctl = R[c]["conv_tail_p"].reshape(128, L, 6, 2)
            ncp[:, b] = ctl.transpose(1, 3, 2, 0).reshape(L, 2, CONVC)
        ps_ = R[c]["pool_tail_s"].reshape(128, L, 4, 4, 15)
        nps[:, 4 * c:4 * c + 4] = ps_.transpose(1, 2, 4, 3, 0).reshape(L, 4, 15, POOLW)
        cs_ = R[c]["conv_tail_s"].reshape(128, L, 4, 6, 2)
        ncs[:, 4 * c:4 * c + 4] = cs_.transpose(1, 2, 4, 3, 0).reshape(L, 4, 2, CONVC)
        nvs[:, 4 * c:4 * c + 4] = R[c]["v_s"].reshape(L, 4, 64, SGUW)
    return (y_prompt, y_sample, npp, ncp, nps, ncs, nvs)
```
